# Optimizing a Trainium2 kernel written in Bass

```python
import math
import jax, jax.numpy as jnp
from jax import lax
import numpy as np

D_MODEL = 1024
BATCH = 8
SEQ = 8192
DEPTH = 2
DEC_BATCH = 16
DEC_SEQ = 4096
PAST_LEN = 128

HEAD_DIM = 64
GROUP_HEADS = 6
BRANCH_W = GROUP_HEADS * HEAD_DIM
N_MIXERS = 4
MIX_W = N_MIXERS * BRANCH_W
N_PROJ_PIECES = 15
PROJ_W = N_PROJ_PIECES * BRANCH_W
CHUNK = 128
DIL_PATTERNS = ((128, 1), (512, 4), (2048, 16))
LOCAL_BLOCK = 64
DIFF_QK_DIM = HEAD_DIM // 2
CONV_WIDTH = 3
Q_BLOCK = 128
EPS = 1e-6
NEG = -1e30

kernel_name = 'hymba_style_hybrid_encoder'


def rms_norm(x, g):
    xf = x.astype(jnp.float32)
    y = xf * lax.rsqrt(jnp.mean(xf * xf, axis=-1, keepdims=True) + EPS)
    return (y * g.astype(jnp.float32)).astype(x.dtype)


def alibi_slopes(n):
    return 2.0 ** (-8.0 * jnp.arange(1, n + 1, dtype=jnp.float32) / n)


def spatial_gating(u, v, norm_g, w_s, b_s):
    bsz, s, _ = u.shape
    v = rms_norm(v, norm_g)
    vc = v.reshape(bsz, s // CHUNK, CHUNK, GROUP_HEADS, HEAD_DIM)
    mixed = jnp.einsum('gts,bcsge->bctge', w_s.astype(jnp.float32), vc.astype(jnp.float32))
    mixed = mixed + b_s.astype(jnp.float32).T[None, None, :, :, None]
    return (u.astype(jnp.float32) * mixed.reshape(bsz, s, BRANCH_W)).astype(u.dtype)


def dilated_attention(q, k, v, slopes, window, dilation):
    bsz, s, h, e = q.shape
    n_side = window // (2 * dilation)
    L = s // dilation
    nb = -(-L // LOCAL_BLOCK)
    Lp = nb * LOCAL_BLOCK

    def to_sub(x):
        return x.reshape(bsz, L, dilation, h, e).transpose(0, 2, 1, 3, 4)

    qs = jnp.pad(to_sub(q), ((0, 0), (0, 0), (0, Lp - L), (0, 0), (0, 0)))
    pad_kv = ((0, 0), (0, 0), (LOCAL_BLOCK, Lp - L + LOCAL_BLOCK), (0, 0), (0, 0))
    ks = jnp.pad(to_sub(k), pad_kv)
    vs = jnp.pad(to_sub(v), pad_kv)

    def band(x):
        xb = x.reshape(bsz, dilation, nb + 2, LOCAL_BLOCK, h, e)
        return jnp.concatenate([xb[:, :, :-2], xb[:, :, 1:-1], xb[:, :, 2:]], axis=3)

    kb = band(ks).astype(jnp.float32)
    vb = band(vs).astype(jnp.float32)
    qb = qs.reshape(bsz, dilation, nb, LOCAL_BLOCK, h, e).astype(jnp.float32)
    scores = jnp.einsum('brnqhe,brnkhe->brhnqk', qb, kb) / math.sqrt(e)
    q_idx = jnp.arange(Lp).reshape(nb, LOCAL_BLOCK)
    k_idx = jnp.arange(nb)[:, None] * LOCAL_BLOCK - LOCAL_BLOCK + jnp.arange(3 * LOCAL_BLOCK)[None, :]
    rel = k_idx[:, None, :] - q_idx[:, :, None]
    valid = (jnp.abs(rel) <= n_side) & (k_idx[:, None, :] >= 0) & (k_idx[:, None, :] < L)
    bias = -slopes[:, None, None, None] * (jnp.abs(rel) * dilation).astype(jnp.float32)
    scores = jnp.where(valid, scores + bias, NEG)
    m = jnp.max(scores, axis=-1, keepdims=True)
    p = jnp.exp(scores - m)
    den = jnp.sum(p, axis=-1)
    o = jnp.einsum('brhnqk,brnkhe->brnqhe', p, vb) / den.transpose(0, 1, 3, 4, 2)[..., None]
    lse = (m[..., 0] + jnp.log(den)).transpose(0, 1, 3, 4, 2)
    o = o.reshape(bsz, dilation, Lp, h, e)[:, :, :L].transpose(0, 2, 1, 3, 4).reshape(bsz, s, h, e)
    lse = lse.reshape(bsz, dilation, Lp, h)[:, :, :L].transpose(0, 2, 1, 3).reshape(bsz, s, h)
    return o, lse


def dilated_mixture(q, k, v, slopes):
    res = [dilated_attention(q, k, v, slopes, w, d) for (w, d) in DIL_PATTERNS]
    outs = jnp.stack([r[0] for r in res], axis=0)
    lses = jnp.stack([r[1] for r in res], axis=0)
    wts = jax.nn.softmax(lses, axis=0)
    return jnp.sum(outs * wts[..., None], axis=0)


def diff_attention(q, k, v, slopes, lam, lambda_init, subln_g):
    bsz, s, h, _, dq = q.shape
    nqb = s // Q_BLOCK
    qb = q.reshape(bsz, nqb, Q_BLOCK, h, 2, dq).transpose(1, 0, 2, 3, 4, 5)
    kf = k.astype(jnp.float32)
    vf = v.astype(jnp.float32)
    key_pos = jnp.arange(s)

    def block(args):
        qblk, start = args
        sc = jnp.einsum('bqhcd,bkhcd->bchqk', qblk.astype(jnp.float32), kf) / math.sqrt(dq)
        dist = jnp.abs(start + jnp.arange(Q_BLOCK)[:, None] - key_pos[None, :]).astype(jnp.float32)
        sc = sc - slopes[:, None, None] * dist
        p = jax.nn.softmax(sc, axis=-1)
        attn = p[:, 0] - lam * p[:, 1]
        return jnp.einsum('bhqk,bkhe->bqhe', attn, vf)

    o = lax.map(block, (qb, jnp.arange(nqb) * Q_BLOCK))
    o = o.transpose(1, 0, 2, 3, 4).reshape(bsz, s, h, v.shape[-1])
    return rms_norm(o, subln_g) * (1.0 - lambda_init)


def short_conv(x_in, b_g, c_g, conv_w):
    z = c_g * x_in
    zp = jnp.pad(z, ((0, 0), (1, 1), (0, 0)))
    y = conv_w[0] * zp[:, :-2] + conv_w[1] * zp[:, 1:-1] + conv_w[2] * zp[:, 2:]
    return b_g * y


def layer(x, l, norm_g, w_in, sgu_g, w_s, b_s, qn_b, kn_b, qn_c, kn_c,
          lam_q1, lam_k1, lam_q2, lam_k2, subln_g, conv_w, w_out):
    bsz, s, _ = x.shape
    slopes = alibi_slopes(GROUP_HEADS)
    h = rms_norm(x, norm_g)
    proj = h @ w_in
    (a_u, a_v, a_g, b_q, b_k, b_v, b_g, c_q, c_k, c_v, c_g,
     d_in, d_b, d_c, d_g) = jnp.split(proj, N_PROJ_PIECES, axis=-1)

    def heads(t):
        return t.reshape(bsz, s, GROUP_HEADS, HEAD_DIM)

    out_a = spatial_gating(a_u, a_v, sgu_g, w_s, b_s)
    qb = rms_norm(heads(b_q), qn_b)
    kb = rms_norm(heads(b_k), kn_b)
    out_b = dilated_mixture(qb, kb, heads(b_v), slopes).reshape(bsz, s, BRANCH_W).astype(x.dtype)
    lambda_init = 0.8 - 0.6 * math.exp(-0.3 * l)
    lam = (jnp.exp(jnp.sum(lam_q1.astype(jnp.float32) * lam_k1.astype(jnp.float32)))
           - jnp.exp(jnp.sum(lam_q2.astype(jnp.float32) * lam_k2.astype(jnp.float32))) + lambda_init)
    qc = rms_norm(c_q.reshape(bsz, s, GROUP_HEADS, 2, DIFF_QK_DIM), qn_c)
    kc = rms_norm(c_k.reshape(bsz, s, GROUP_HEADS, 2, DIFF_QK_DIM), kn_c)
    out_c = diff_attention(qc, kc, heads(c_v), slopes, lam, lambda_init, subln_g)
    out_c = out_c.reshape(bsz, s, BRANCH_W).astype(x.dtype)
    out_d = short_conv(d_in, d_b, d_c, conv_w)
    mixed = jnp.concatenate([jax.nn.silu(a_g) * out_a, jax.nn.silu(b_g) * out_b,
                             jax.nn.silu(c_g) * out_c, jax.nn.silu(d_g) * out_d], axis=-1).astype(x.dtype)
    return x + mixed @ w_out


def trunk(x, norm_g, w_in, sgu_g, w_s, b_s, qn_b, kn_b, qn_c, kn_c,
          lam_q1, lam_k1, lam_q2, lam_k2, subln_g, conv_w, w_out):
    for l in range(DEPTH):
        x = layer(x, l, norm_g[l], w_in[l], sgu_g[l], w_s[l], b_s[l], qn_b[l], kn_b[l],
                  qn_c[l], kn_c[l], lam_q1[l], lam_k1[l], lam_q2[l], lam_k2[l],
                  subln_g[l], conv_w[l], w_out[l])
    return x


def setup_inputs(seed: int = 0) -> dict:
    key = jax.random.key(seed)
    ks = jax.random.split(key, 20)
    f32 = jnp.float32

    def nrm(k, shape, scale):
        return jax.random.normal(k, shape, f32) * scale

    def gain(k, shape):
        return 1.0 + 0.02 * jax.random.normal(k, shape, f32)

    return {
        'x_prompt': nrm(ks[0], (BATCH, SEQ, D_MODEL), 1.0),
        'x_sample': nrm(ks[1], (DEC_BATCH, DEC_SEQ, D_MODEL), 1.0),
        'norm_g': gain(ks[2], (DEPTH, D_MODEL)),
        'w_in': nrm(ks[3], (DEPTH, D_MODEL, PROJ_W), D_MODEL ** -0.5),
        'sgu_g': gain(ks[4], (DEPTH, BRANCH_W)),
        'w_s': nrm(ks[5], (DEPTH, GROUP_HEADS, CHUNK, CHUNK), 0.5 * CHUNK ** -0.5),
        'b_s': 1.0 + 0.1 * jax.random.normal(ks[6], (DEPTH, GROUP_HEADS, CHUNK), f32),
        'qn_b': gain(ks[7], (DEPTH, HEAD_DIM)),
        'kn_b': gain(ks[8], (DEPTH, HEAD_DIM)),
        'qn_c': gain(ks[9], (DEPTH, DIFF_QK_DIM)),
        'kn_c': gain(ks[10], (DEPTH, DIFF_QK_DIM)),
        'lam_q1': nrm(ks[11], (DEPTH, DIFF_QK_DIM), 0.1),
        'lam_k1': nrm(ks[12], (DEPTH, DIFF_QK_DIM), 0.1),
        'lam_q2': nrm(ks[13], (DEPTH, DIFF_QK_DIM), 0.1),
        'lam_k2': nrm(ks[14], (DEPTH, DIFF_QK_DIM), 0.1),
        'subln_g': gain(ks[15], (DEPTH, HEAD_DIM)),
        'conv_w': nrm(ks[16], (DEPTH, CONV_WIDTH, BRANCH_W), CONV_WIDTH ** -0.5),
        'w_out': nrm(ks[17], (DEPTH, MIX_W, D_MODEL), MIX_W ** -0.5),
    }


def reference(x_prompt, x_sample, norm_g, w_in, sgu_g, w_s, b_s, qn_b, kn_b, qn_c, kn_c,
              lam_q1, lam_k1, lam_q2, lam_k2, subln_g, conv_w, w_out):
    y_prompt = trunk(x_prompt, norm_g, w_in, sgu_g, w_s, b_s, qn_b, kn_b, qn_c, kn_c,
                     lam_q1, lam_k1, lam_q2, lam_k2, subln_g, conv_w, w_out)
    y_sample = trunk(x_sample, norm_g, w_in, sgu_g, w_s, b_s, qn_b, kn_b, qn_c, kn_c,
                     lam_q1, lam_k1, lam_q2, lam_k2, subln_g, conv_w, w_out)
    return (y_prompt, y_sample)
```

```python
import math
from contextlib import ExitStack
import numpy as np
import concourse.bass as bass
import concourse.mybir as mybir
from concourse.bass_utils import run_bass_kernel_spmd

F32 = mybir.dt.float32
BF16 = mybir.dt.bfloat16
I32 = mybir.dt.int32
AF = mybir.ActivationFunctionType
ALU = mybir.AluOpType
AX = mybir.AxisListType

D_MODEL = 1024
DEPTH = 2
BW = 384
PROJ_W = 15 * BW
MIX_W = 4 * BW
EPS = 1e-6
NHEAD = 6
SLOPES = [2.0 ** (-8.0 * (i + 1) / NHEAD) for i in range(NHEAD)]
N_CORES = 8
(P_AU, P_AV, P_AG, P_BQ, P_BK, P_BV, P_BG, P_CQ, P_CK, P_CV, P_CG, P_DI, P_DB, P_DC, P_DG) = range(15)
TC = 1408
TW = 2944
SKIP_T = 60.0
VPAD = 65
WARM_N = 0
QK_REP = 1
WARM_Q = 0
SBK = 2


class Buf:
    __slots__ = ("w", "r")

    def __init__(self):
        self.w = None
        self.r = {}


class Sched:
    ENG = ("pe", "act", "dve", "pool", "sp")

    def __init__(self, n_dma=24):
        self.cnt = {e: 0 for e in ("pe", "act", "dve", "pool")}
        self.dma_val = [0] * n_dma
        self.dma_rr = 0
        self.seen = {e: {} for e in self.ENG}
        self.ops = {e: [] for e in self.ENG}

    def new_phase(self):
        self.ops = {e: [] for e in self.ENG}

    def emit(self, eng, fn, reads=(), writes=(), dma=False):
        waits = {}
        seen = self.seen[eng]

        def add(ev, raw):
            if ev is None:
                return
            k, val = ev
            if k[0] == 'e' and k[1] == eng:
                if not (raw and eng in ("act", "dve", "pool")):
                    return
            if seen.get(k, 0) >= val:
                return
            if waits.get(k, 0) < val:
                waits[k] = val

        for b in reads:
            add(b.w, True)
        for b in writes:
            add(b.w, False)
            for k, v in b.r.items():
                add((k, v), False)
        if dma:
            s = self.dma_rr
            self.dma_rr = (s + 1) % len(self.dma_val)
            prev = self.dma_val[s]
            if prev > 0:
                add((('d', s), prev), False)
            self.dma_val[s] = prev + 16
            ev = (('d', s), prev + 16)
        else:
            self.cnt[eng] += 1
            ev = (('e', eng), self.cnt[eng])
        for k, v in waits.items():
            seen[k] = v
        self.ops[eng].append((fn, list(waits.items()), ev))
        for b in reads:
            if b.r.get(ev[0], 0) < ev[1]:
                b.r[ev[0]] = ev[1]
        for b in writes:
            b.w = ev
            b.r = {}
        return ev

    def barrier(self):
        allev = [(('e', e), c) for e, c in self.cnt.items() if c > 0]
        allev += [(('d', s), v) for s, v in enumerate(self.dma_val) if v > 0]
        for eng in self.ENG:
            waits = {}
            for k, v in allev:
                if k[0] == 'e' and k[1] == eng:
                    continue
                if self.seen[eng].get(k, 0) >= v:
                    continue
                waits[k] = v
                self.seen[eng][k] = v
            self.ops[eng].append((None, list(waits.items()), None))

    def replay(self, nc, sems):
        with nc.Block() as block:
            def run(eng, e):
                for fn, waits, ev in self.ops[eng]:
                    for k, v in waits:
                        e.wait_ge(sems[k], v)
                    if fn is None:
                        continue
                    ins = fn(e)
                    if ev[0][0] == 'd':
                        ins.then_inc(sems[ev[0]], 16)
                    else:
                        ins.then_inc(sems[ev[0]], 1)

            @block.tensor
            def _(e):
                run("pe", e)

            @block.scalar
            def _(e):
                run("act", e)

            @block.vector
            def _(e):
                run("dve", e)

            @block.gpsimd
            def _(e):
                run("pool", e)

            @block.sync
            def _(e):
                run("sp", e)


def ACT(S, out, in_, func, R, W, bias=None, scale=None, accum_out=None):
    kw = {}
    if bias is not None:
        kw["bias"] = bias
    if scale is not None:
        kw["scale"] = scale
    if accum_out is not None:
        kw["accum_out"] = accum_out
    S.emit("act", lambda e: e.activation(out=out, in_=in_, func=func, **kw), R, W)


def TT(S, eng, out, in0, in1, op, R, W):
    S.emit(eng, lambda e: e.tensor_tensor(out=out, in0=in0, in1=in1, op=op), R, W)


def TS(S, eng, out, in0, s1, op0, R, W, s2=None, op1=None):
    if op1 is None:
        S.emit(eng, lambda e: e.tensor_scalar(out=out, in0=in0, scalar1=s1, scalar2=None, op0=op0), R, W)
    else:
        S.emit(eng, lambda e: e.tensor_scalar(out=out, in0=in0, scalar1=s1, scalar2=s2, op0=op0, op1=op1), R, W)


def STT(S, out, in0, scalar, in1, op0, op1, R, W):
    S.emit("dve", lambda e: e.scalar_tensor_tensor(out=out, in0=in0, scalar=scalar, in1=in1, op0=op0, op1=op1), R, W)


def CP(S, eng, out, in_, R, W):
    if eng == "act":
        S.emit(eng, lambda e: e.copy(out=out, in_=in_), R, W)
    else:
        S.emit(eng, lambda e: e.tensor_copy(out=out, in_=in_), R, W)


def MMG(S, mms, R, W):
    def fn(e):
        ins = None
        for (out, lhsT, rhs, st, sp) in mms:
            ins = e.matmul(out, lhsT=lhsT, rhs=rhs, start=st, stop=sp)
        return ins
    S.emit("pe", fn, R, W)


def TRG(S, trs, R, W):
    def fn(e):
        ins = None
        for (out, in_, ident) in trs:
            ins = e.transpose(out=out, in_=in_, identity=ident)
        return ins
    S.emit("pe", fn, R, W)


def DMA(S, out, in_, R, W, slow=False):
    if slow:
        S.emit("sp", lambda e: e.dma_start(out=out, in_=in_, allow_slow_non_contiguous=True), R, W, dma=True)
    else:
        S.emit("sp", lambda e: e.dma_start(out=out, in_=in_), R, W, dma=True)


class PS:
    def __init__(self, banks):
        self.banks = banks
        self.rot = list(range(8))
        self.i = 0

    def set_rot(self, idxs):
        self.rot = list(idxs)
        self.i = 0

    def next(self):
        b = self.banks[self.rot[self.i % len(self.rot)]]
        self.i += 1
        return b


class Ctx:
    pass


_uid = [0]


def mk_sb(nc, es):
    def sb(name, shape, dt):
        _uid[0] += 1
        return es.enter_context(nc.sbuf_tensor(f"{name}_{_uid[0]}", shape, dt))
    return sb


def build_consts(C, S, sb, full=True):
    nc = C.nc
    b = Buf()
    C.constbuf = b
    C.identb = sb("identb", [128, 128], BF16)
    C.identf = sb("identf", [128, 128], F32)
    C.ones_f = sb("ones_f", [128, 64], F32)
    S.emit("pool", lambda e: e.memset(C.ones_f[:], 1.0), [], [b])
    if not full:
        iot = sb("iot", [128, 128], I32)
        S.emit("pool", lambda e: e.iota(iot[:], [[-1, 128]], base=0, channel_multiplier=1), [], [b])
        TS(S, "dve", C.identb[:], iot[:], 0.0, ALU.is_equal, [b], [b])
        TS(S, "dve", C.identf[:], iot[:], 0.0, ALU.is_equal, [b], [b])
        return
    C.absd = sb("absd", [128, TW], F32)
    iot = sb("iot", [128, TW], I32)
    S.emit("pool", lambda e: e.iota(iot[:], [[-1, TW]], base=TC, channel_multiplier=1), [], [b])
    TS(S, "dve", C.identb[:], iot[:, TC:TC + 128], 0.0, ALU.is_equal, [b], [b])
    TS(S, "dve", C.identf[:], iot[:, TC:TC + 128], 0.0, ALU.is_equal, [b], [b])
    C.negd = sb("negd", [128, TW], F32)
    CP(S, "dve", C.negd[:], iot[:], [b], [b])
    TS(S, "dve", C.absd[:], C.negd[:], -1.0, ALU.mult, [b], [b])
    TT(S, "dve", C.absd[:], C.absd[:], C.negd[:], ALU.max, [b], [b])
    C.iot = iot


def build_program(seq_lens, dump=False, depth=DEPTH, phases=4):
    nc = bass.Bass("TRN2", target_bir_lowering=False)
    NT = sum(seq_lens)
    seqs = []
    o = 0
    for L in seq_lens:
        seqs.append((o, L))
        o += L
    C = Ctx()
    C.nc = nc
    ext_in = lambda name, shape: nc.dram_tensor(name, shape, F32, kind="ExternalInput").ap()
    x_in = ext_in("x", [NT, D_MODEL])
    norm_g = ext_in("norm_g", [DEPTH, D_MODEL])
    w_in = ext_in("w_in", [DEPTH, D_MODEL, PROJ_W])
    sgu_g = ext_in("sgu_g", [DEPTH, BW])
    w_s = ext_in("w_s", [DEPTH, NHEAD, 128, 128])
    b_s = ext_in("b_s", [DEPTH, NHEAD, 128])
    qn_b = ext_in("qn_b", [DEPTH, 64])
    kn_b = ext_in("kn_b", [DEPTH, 64])
    qn_c = ext_in("qn_c", [DEPTH, 32])
    kn_c = ext_in("kn_c", [DEPTH, 32])
    lam_q1 = ext_in("lam_q1", [DEPTH, 32])
    lam_k1 = ext_in("lam_k1", [DEPTH, 32])
    lam_q2 = ext_in("lam_q2", [DEPTH, 32])
    lam_k2 = ext_in("lam_k2", [DEPTH, 32])
    subln_g = ext_in("subln_g", [DEPTH, 64])
    conv_w = ext_in("conv_w", [DEPTH, 3, BW])
    w_out = ext_in("w_out", [DEPTH, MIX_W, D_MODEL])
    y_out = nc.dram_tensor("y", [NT, D_MODEL], F32, kind="ExternalOutput").ap()

    skind = "ExternalOutput" if dump else "Internal"
    scr = lambda name, shape, dt: nc.dram_tensor(name, shape, dt, kind=skind).ap()
    D = Ctx()
    D.x1 = scr("x1", [NT, D_MODEL], F32)
    D.qbt = scr("qbt", [BW, NT], BF16)
    D.kbt = scr("kbt", [BW, NT], BF16)
    D.vb = scr("vb", [NT, BW], BF16)
    D.gbt = scr("gbt", [BW, NT], BF16)
    D.qct = scr("qct", [BW, NT], BF16)
    D.kct = scr("kct", [BW, NT], BF16)
    D.vc = scr("vc", [NT, BW], BF16)
    D.gct = scr("gct", [BW, NT], BF16)
    D.zt = scr("zt", [BW, NT], F32)
    D.gdt = scr("gdt", [BW, NT], F32)
    D.mixt = scr("mixt", [MIX_W, NT], BF16)

    S = Sched()
    with ExitStack() as top:
        sems = {}
        for e in ("pe", "act", "dve", "pool"):
            sems[('e', e)] = top.enter_context(nc.semaphore(f"sem_{e}"))
        for s in range(len(S.dma_val)):
            sems[('d', s)] = top.enter_context(nc.semaphore(f"sem_d{s}"))
        banks = []
        for i in range(8):
            t = top.enter_context(nc.psum_tensor(f"bank{i}", [128, 512], F32))
            banks.append((t, Buf()))
        P = PS(banks)
        C.P = P
        W = Ctx()
        W.norm_g, W.w_in, W.sgu_g, W.w_s, W.b_s = norm_g, w_in, sgu_g, w_s, b_s
        W.qn_b, W.kn_b, W.qn_c, W.kn_c = qn_b, kn_b, qn_c, kn_c
        W.lam = (lam_q1, lam_k1, lam_q2, lam_k2)
        W.subln_g, W.conv_w, W.w_out = subln_g, conv_w, w_out

        for l in range(depth):
            xsrc = x_in if l == 0 else D.x1
            ydst = y_out if l == depth - 1 else D.x1
            for ph in (phase1, phase2, phase3, phase4)[:phases]:
                S.new_phase()
                for bk in banks:
                    bk[1].w = None
                    bk[1].r = {}
                with ExitStack() as es:
                    sb = mk_sb(nc, es)
                    ph(C, S, sb, W, D, l, xsrc, ydst, seqs, NT)
                    S.barrier()
                    S.replay(nc, sems)
    return nc


def phase1(C, S, sb, W, D, l, xsrc, ydst, seqs, NT):
    nc, P = C.nc, C.P
    P.set_rot(range(8))
    build_consts(C, S, sb, full=False)
    cb = C.constbuf
    wb = sb("wb", [128, 8, PROJ_W], BF16)
    wbuf = Buf()
    g8 = sb("g8", [128, 8], F32)
    DMA(S, g8[:], W.norm_g[l:l + 1, :].rearrange("o (k p) -> p (o k)", p=128), [], [wbuf], slow=True)
    WST = 640
    wst = [(sb(f"wst{i}", [128, WST], F32), Buf()) for i in range(2)]
    n = 0
    for kc in range(8):
        for c0 in range(0, PROJ_W, WST):
            st, stb = wst[n % 2]
            DMA(S, st[:], W.w_in[l, kc * 128:(kc + 1) * 128, c0:c0 + WST], [], [stb])
            if n % 2 == 0:
                TS(S, "dve", wb[:, kc, c0:c0 + WST], st[:], g8[:, kc:kc + 1], ALU.mult, [stb, wbuf], [wbuf])
            else:
                ACT(S, wb[:, kc, c0:c0 + WST], st[:], AF.Copy, [stb, wbuf], [wbuf], scale=g8[:, kc:kc + 1])
            n += 1
    wsT = sb("wsT", [128, NHEAD, 128], BF16)
    for g in range(NHEAD):
        st, stb = wst[n % 2]
        n += 1
        DMA(S, st[:, 0:128], W.w_s[l, g], [], [stb])
        bk, bb = P.next()
        TRG(S, [(bk[:, 0:128], st[:, 0:128], C.identf[:])], [stb, cb], [bb])
        CP(S, "dve", wsT[:, g, :], bk[:, 0:128], [bb], [wbuf])
    bsT = sb("bsT", [128, NHEAD], F32)
    DMA(S, bsT[:], W.b_s[l].rearrange("g t -> t g"), [], [wbuf], slow=True)
    sgu_bc = sb("sgu_bc", [128, BW], F32)
    DMA(S, sgu_bc[:], W.sgu_g[l:l + 1, :].to_broadcast([128, BW]), [], [wbuf])
    g64 = sb("g64", [128, 4, 64], F32)
    DMA(S, g64[:, 0, :], W.qn_b[l:l + 1, :].to_broadcast([128, 64]), [], [wbuf])
    DMA(S, g64[:, 1, :], W.kn_b[l:l + 1, :].to_broadcast([128, 64]), [], [wbuf])
    DMA(S, g64[:, 2, 0:32], W.qn_c[l:l + 1, :].to_broadcast([128, 32]), [], [wbuf])
    DMA(S, g64[:, 3, 0:32], W.kn_c[l:l + 1, :].to_broadcast([128, 32]), [], [wbuf])
    gq_b = sb("gq_b", [128, BW], F32)
    gk_b = sb("gk_b", [128, BW], F32)
    gq_c = sb("gq_c", [128, BW], F32)
    gk_c = sb("gk_c", [128, BW], F32)
    TS(S, "dve", gq_b[:].rearrange("p (h e) -> p h e", e=64), g64[:, 0, :].unsqueeze(1).to_broadcast([128, 6, 64]),
       0.125, ALU.mult, [wbuf], [wbuf])
    TS(S, "dve", gk_b[:].rearrange("p (h e) -> p h e", e=64), g64[:, 1, :].unsqueeze(1).to_broadcast([128, 6, 64]),
       1.0, ALU.mult, [wbuf], [wbuf])
    TS(S, "dve", gq_c[:].rearrange("p (h e) -> p h e", e=32), g64[:, 2, 0:32].unsqueeze(1).to_broadcast([128, 12, 32]),
       1.0 / math.sqrt(32.0), ALU.mult, [wbuf], [wbuf])
    TS(S, "dve", gk_c[:].rearrange("p (h e) -> p h e", e=32), g64[:, 3, 0:32].unsqueeze(1).to_broadcast([128, 12, 32]),
       1.0, ALU.mult, [wbuf], [wbuf])

    def slots(name, shape, dt, k):
        return [(sb(f"{name}{i}", shape, dt), Buf()) for i in range(k)]
    xs = slots("xs", [128, D_MODEL], F32, 2)
    junk = sb("junk", [128, D_MODEL], BF16)
    junkb = Buf()
    hb = slots("hb", [128, D_MODEL], BF16, 4)
    hT = slots("hT", [128, 8, 512], BF16, 2)
    st4 = slots("st4", [128, 4], F32, 2)
    sqf = slots("sqf", [128, BW], F32, 2)
    s12 = slots("s12", [128, 3, 12], F32, 2)
    vn = slots("vn", [128, BW], BF16, 1)
    sg = slots("sg", [128, BW], F32, 1)
    t1 = slots("t1", [128, BW], F32, 1)
    t2 = slots("t2", [128, BW], F32, 1)
    mixA = slots("mixA", [128, BW], BF16, 1)
    qt = slots("qt", [128, BW], F32, 2)
    qn = slots("qn", [128, BW], BF16, 4)
    stg_T = {nm: slots(f"stg_{nm}", [128, 3, 512], BF16, 1)[0] for nm in ("mixa", "qbt", "kbt", "qct", "kct")}
    stg_v = {nm: slots(f"stg_{nm}", [128, 4, BW], BF16, 1)[0] for nm in ("vb", "vc")}
    fm_b = slots("fm_b", [128, 512], BF16, 3)
    fm_f = slots("fm_f", [128, 512], F32, 2)
    tmpf = slots("tmpf", [128, 512], F32, 2)
    cnt = {"x": 0, "h": 0, "st": 0, "sq": 0, "s12": 0, "vn": 0, "sg": 0, "t1": 0, "t2": 0, "mixA": 0, "qt": 0, "qn": 0,
           "fmb": 0, "fmf": 0, "tmpf": 0}

    def nxt(lst, key):
        v = lst[cnt[key] % len(lst)]
        cnt[key] += 1
        return v

    ngroups = NT // 512

    def prep(g):
        res = []
        for j in range(4):
            t0 = g * 512 + j * 128
            x_t, x_b = nxt(xs, "x")
            DMA(S, x_t[:], xsrc[t0:t0 + 128, :], [], [x_b])
            s_t, s_b = nxt(st4, "st")
            ACT(S, junk[:], x_t[:], AF.Square, [x_b], [junkb, s_b], accum_out=s_t[:, 0:1])
            ACT(S, s_t[:, 1:2], s_t[:, 0:1], AF.Sqrt, [s_b], [s_b], scale=1.0 / D_MODEL, bias=EPS)
            S.emit("dve", lambda e, s_t=s_t: e.reciprocal(out=s_t[:, 2:3], in_=s_t[:, 1:2]), [s_b], [s_b])
            h_t, h_b = nxt(hb, "h")
            ACT(S, h_t[:], x_t[:], AF.Copy, [x_b, s_b], [h_b], scale=s_t[:, 2:3])
            res.append((h_t, h_b))
        return res

    def transposes(g, hs):
        hT_t, hT_b = hT[g % 2]
        for j, (h_t, h_b) in enumerate(hs):
            bk, bb = P.next()
            bkb = bk[:].bitcast(BF16)
            TRG(S, [(bkb[:, kc * 128:(kc + 1) * 128], h_t[:, kc * 128:(kc + 1) * 128], C.identb[:]) for kc in range(8)],
                [h_b, cb], [bb])
            CP(S, "dve" if j % 2 == 0 else "act", hT_t[:, :, j * 128:(j + 1) * 128],
               bkb[:, 0:1024].rearrange("p (k t) -> p k t", t=128), [bb], [hT_b])
        return hT_t, hT_b

    def proj_tm(hT_t, hT_b, j, piece):
        bk, bb = P.next()
        MMG(S, [(bk[:, 0:BW], hT_t[:, kc, j * 128:(j + 1) * 128], wb[:, kc, piece * BW:(piece + 1) * BW], kc == 0, kc == 7)
                for kc in range(8)], [hT_b, wbuf], [bb])
        return bk, bb

    def headnorm(bk, bb, nh, gain, eng2):
        hd = BW // nh
        sq_t, sq_b = nxt(sqf, "sq")
        ACT(S, sq_t[:], bk[:, 0:BW], AF.Square, [bb], [sq_b])
        s_t, s_b = nxt(s12, "s12")
        S.emit("dve", lambda e: e.tensor_reduce(out=s_t[:, 0, 0:nh], in_=sq_t[:].rearrange("p (h e) -> p h e", e=hd),
                                                axis=AX.X, op=ALU.add), [sq_b], [s_b])
        ACT(S, s_t[:, 1, 0:nh], s_t[:, 0, 0:nh], AF.Sqrt, [s_b], [s_b], scale=1.0 / hd, bias=EPS)
        S.emit("dve", lambda e: e.reciprocal(out=s_t[:, 2, 0:nh], in_=s_t[:, 1, 0:nh]), [s_b], [s_b])
        q_t, q_b = nxt(qt, "qt")
        TT(S, "dve", q_t[:].rearrange("p (h e) -> p h e", e=hd), bk[:, 0:BW].rearrange("p (h e) -> p h e", e=hd),
           s_t[:, 2, 0:nh].unsqueeze(2).to_broadcast([128, nh, hd]), ALU.mult, [bb, s_b], [q_b])
        n_t, n_b = nxt(qn, "qn")
        TT(S, eng2, n_t[:], q_t[:], gain[:], ALU.mult, [q_b, wbuf], [n_b])
        return n_t, n_b

    def tr3(src_t, src_b, stg, j, eng):
        st_t, st_b = stg
        bk, bb = P.next()
        bkb = bk[:].bitcast(BF16)
        TRG(S, [(bkb[:, c * 128:(c + 1) * 128], src_t[:, c * 128:(c + 1) * 128], C.identb[:]) for c in range(3)],
            [src_b, cb], [bb])
        CP(S, eng, st_t[:, :, j * 128:(j + 1) * 128], bkb[:, 0:384].rearrange("p (c t) -> p c t", t=128), [bb], [st_b])

    hs = prep(0)
    for g in range(ngroups):
        tok0 = g * 512
        hT_t, hT_b = transposes(g, hs)
        if g + 1 < ngroups:
            hs = prep(g + 1)
        for j in range(4):
            bk_v, bb_v = proj_tm(hT_t, hT_b, j, P_AV)
            s_t, s_b = nxt(st4, "st")
            sq_t, sq_b = nxt(sqf, "sq")
            ACT(S, sq_t[:], bk_v[:, 0:BW], AF.Square, [bb_v], [sq_b, s_b], accum_out=s_t[:, 0:1])
            ACT(S, s_t[:, 1:2], s_t[:, 0:1], AF.Sqrt, [s_b], [s_b], scale=1.0 / BW, bias=EPS)
            S.emit("dve", lambda e, s_t=s_t: e.reciprocal(out=s_t[:, 2:3], in_=s_t[:, 1:2]), [s_b], [s_b])
            vn_t, vn_b = nxt(vn, "vn")
            STT(S, vn_t[:], bk_v[:, 0:BW], s_t[:, 2:3], sgu_bc[:], ALU.mult, ALU.mult, [bb_v, s_b, wbuf], [vn_b])
            bk, bb = proj_tm(hT_t, hT_b, j, P_BQ)
            qb_n = headnorm(bk, bb, 6, gq_b, "pool")
            bk, bb = proj_tm(hT_t, hT_b, j, P_BK)
            kb_n = headnorm(bk, bb, 6, gk_b, "pool")
            bk_u, bb_u = proj_tm(hT_t, hT_b, j, P_AU)
            bk_g, bb_g = proj_tm(hT_t, hT_b, j, P_AG)
            sg_t, sg_b = nxt(sg, "sg")
            ACT(S, sg_t[:], bk_g[:, 0:BW], AF.Silu, [bb_g], [sg_b])
            bk_m, bb_m = P.next()
            MMG(S, [(bk_m[:, gg * 64:(gg + 1) * 64], wsT[:, gg, :], vn_t[:, gg * 64:(gg + 1) * 64], True, True)
                    for gg in range(NHEAD)], [vn_b, wbuf], [bb_m])
            t1_t, t1_b = nxt(t1, "t1")
            TT(S, "dve", t1_t[:].rearrange("p (h e) -> p h e", e=64), bk_m[:, 0:BW].rearrange("p (h e) -> p h e", e=64),
               bsT[:].unsqueeze(2).to_broadcast([128, 6, 64]), ALU.add, [bb_m, wbuf], [t1_b])
            t2_t, t2_b = nxt(t2, "t2")
            TT(S, "dve", t2_t[:], bk_u[:, 0:BW], t1_t[:], ALU.mult, [bb_u, t1_b], [t2_b])
            ma_t, ma_b = nxt(mixA, "mixA")
            TT(S, "pool", ma_t[:], t2_t[:], sg_t[:], ALU.mult, [t2_b, sg_b], [ma_b])
            bk, bb = proj_tm(hT_t, hT_b, j, P_CQ)
            qc_n = headnorm(bk, bb, 12, gq_c, "pool")
            tr3(qb_n[0], qb_n[1], stg_T["qbt"], j, "dve")
            tr3(kb_n[0], kb_n[1], stg_T["kbt"], j, "act")
            bk, bb = proj_tm(hT_t, hT_b, j, P_CK)
            kc_n = headnorm(bk, bb, 12, gk_c, "pool")
            bk, bb = proj_tm(hT_t, hT_b, j, P_BV)
            CP(S, "act", stg_v["vb"][0][:, j, :], bk[:, 0:BW], [bb], [stg_v["vb"][1]])
            tr3(ma_t, ma_b, stg_T["mixa"], j, "dve")
            bk, bb = proj_tm(hT_t, hT_b, j, P_CV)
            CP(S, "act", stg_v["vc"][0][:, j, :], bk[:, 0:BW], [bb], [stg_v["vc"][1]])
            tr3(qc_n[0], qc_n[1], stg_T["qct"], j, "dve")
            tr3(kc_n[0], kc_n[1], stg_T["kct"], j, "act")
        for nm, dst, r0 in (("mixa", D.mixt, 0), ("qbt", D.qbt, 0), ("kbt", D.kbt, 0), ("qct", D.qct, 0), ("kct", D.kct, 0)):
            st_t, st_b = stg_T[nm]
            DMA(S, dst[r0:r0 + BW, tok0:tok0 + 512].rearrange("(c p) t -> p c t", p=128), st_t[:], [st_b], [])
        for nm, dst in (("vb", D.vb), ("vc", D.vc)):
            st_t, st_b = stg_v[nm]
            DMA(S, dst[tok0:tok0 + 512, :].rearrange("(j p) c -> p j c", p=128), st_t[:], [st_b], [])

        def proj_fm(piece, ch):
            bk, bb = P.next()
            c0 = piece * BW + ch * 128
            MMG(S, [(bk[:, 0:512], wb[:, kc, c0:c0 + 128], hT_t[:, kc, :], kc == 0, kc == 7) for kc in range(8)],
                [hT_b, wbuf], [bb])
            return bk, bb

        for piece, dst in ((P_BG, D.gbt), (P_CG, D.gct)):
            for ch in range(3):
                bk, bb = proj_fm(piece, ch)
                f_t, f_b = nxt(fm_b, "fmb")
                ACT(S, f_t[:], bk[:, 0:512], AF.Silu, [bb], [f_b])
                DMA(S, dst[ch * 128:(ch + 1) * 128, tok0:tok0 + 512], f_t[:], [f_b], [])
        for ch in range(3):
            bk_i, bb_i = proj_fm(P_DI, ch)
            bk_c, bb_c = proj_fm(P_DC, ch)
            tm_t, tm_b = nxt(tmpf, "tmpf")
            CP(S, "act", tm_t[:], bk_i[:, 0:512], [bb_i], [tm_b])
            f_t, f_b = nxt(fm_f, "fmf")
            TT(S, "dve", f_t[:], bk_c[:, 0:512], tm_t[:], ALU.mult, [bb_c, tm_b], [f_b])
            DMA(S, D.zt[ch * 128:(ch + 1) * 128, tok0:tok0 + 512], f_t[:], [f_b], [])
            bk_g, bb_g = proj_fm(P_DG, ch)
            bk_b, bb_b = proj_fm(P_DB, ch)
            tm_t, tm_b = nxt(tmpf, "tmpf")
            ACT(S, tm_t[:], bk_g[:, 0:512], AF.Silu, [bb_g], [tm_b])
            f_t, f_b = nxt(fm_f, "fmf")
            TT(S, "dve", f_t[:], bk_b[:, 0:512], tm_t[:], ALU.mult, [bb_b, tm_b], [f_b])
            DMA(S, D.gdt[ch * 128:(ch + 1) * 128, tok0:tok0 + 512], f_t[:], [f_b], [])


def load_head(C, S, slot, qsrc, ksrc, vsrc, h, s0, L):
    (q_t, k_t, v_t, hb) = slot
    for r in range(2):
        DMA(S, q_t[64 * r:64 * r + 64, 0:L], qsrc[h * 64:(h + 1) * 64, s0:s0 + L], [], [hb])
        DMA(S, k_t[64 * r:64 * r + 64, 0:L], ksrc[h * 64:(h + 1) * 64, s0:s0 + L], [], [hb])
    nb = L // 128
    for b0 in range(0, nb, 16):
        b1 = min(nb, b0 + 16)
        DMA(S, v_t[:, b0:b1, 0:64],
            vsrc[s0 + b0 * 128:s0 + b1 * 128, h * 64:(h + 1) * 64].rearrange("(b p) e -> p b e", p=128), [], [hb])


def alloc_attn(C, S, sb, maxL):
    A = Ctx()
    A.slots = []
    for i in range(2):
        q_t = sb(f"aq{i}", [128, maxL], BF16)
        k_t = sb(f"ak{i}", [128, maxL], BF16)
        v_t = sb(f"av{i}", [128, maxL // 128, 65], BF16)
        hb = Buf()
        S.emit("pool", lambda e, v_t=v_t: e.memset(v_t[:, :, 64:65], 1.0), [], [hb])
        A.slots.append((q_t, k_t, v_t, hb))
    A.pT = [(sb(f"pT{i}", [128, 512], BF16), Buf()) for i in range(8)]
    A.ex = [(sb(f"ex{i}", [128, 512], F32), Buf()) for i in range(4)]
    A.npT = 0
    A.nex = 0
    return A


def pe_warmup(S, P, lhsT, rhs, n):
    bk, bb = P.next()
    MMG(S, [(bk[:, 0:512], lhsT, rhs, True, True) for _ in range(n)], [], [bb])


class Pipe:
    def __init__(self, look):
        self.look = look
        self.t = 0
        self.n = 0
        self.pend = []

    def at(self, delay, fn):
        self.pend.append((self.t + delay, self.n, fn))
        self.n += 1

    def tick(self):
        self.t += 1
        while True:
            self.pend.sort(key=lambda p: (p[0], p[1]))
            if not self.pend or self.pend[0][0] > self.t:
                break
            self.pend.pop(0)[2]()

    def block(self, qk_fn, pv_fn):
        qk_fn()
        self.at(self.look + 1, pv_fn)
        self.tick()

    def flush(self):
        while self.pend:
            self.tick()


def phase2(C, S, sb, W, D, l, xsrc, ydst, seqs, NT):
    nc, P = C.nc, C.P
    build_consts(C, S, sb)
    cb = C.constbuf
    maxL = max(L for _, L in seqs)
    A = alloc_attn(C, S, sb, maxL)
    mtab = sb("mtab", [128, TW], F32)
    mt2 = sb("mt2", [128, TW], F32)
    mi = sb("mi", [128, TW], I32)

    def le(out, lim):
        TS(S, "dve", out, C.absd[:], -1.0, ALU.mult, [cb], [cb], s2=lim + 1.0, op1=ALU.add)
        TS(S, "dve", out, out, 0.0, ALU.max, [cb], [cb], s2=1.0, op1=ALU.min)
    le(mtab[:], 64.0)
    mt3 = C.negd
    for dil, lim in ((4, 256.0), (16, 1024.0)):
        TS(S, "dve", mt2[:], C.absd[:], 1.0 / dil, ALU.mult, [cb], [cb])
        CP(S, "dve", mi[:], mt2[:], [cb], [cb])
        CP(S, "dve", mt3[:], mi[:], [cb], [cb])
        TT(S, "dve", mt2[:], mt2[:], mt3[:], ALU.is_equal, [cb], [cb])
        le(mt3[:], lim)
        TT(S, "dve", mt2[:], mt2[:], mt3[:], ALU.mult, [cb], [cb])
        TT(S, "dve", mtab[:], mtab[:], mt2[:], ALU.add, [cb], [cb])
    rtab = [(sb(f"rtab{i}", [128, TW], F32), Buf()) for i in range(2)]
    gate = [(sb(f"gate{i}", [64, 512], BF16), Buf()) for i in range(3)]
    rc = [(sb(f"rc{i}", [128, 512], F32), Buf()) for i in range(2)]
    tq = [(sb(f"tq{i}", [64, 512], F32), Buf()) for i in range(2)]
    ob = [(sb(f"ob{i}", [64, 512], BF16), Buf()) for i in range(2)]
    P.set_rot([0, 1, 2, 3, 6, 7])
    accs = [P.banks[4], P.banks[5]]
    pipe = Pipe(2)
    units = [(s0, L, h) for (s0, L) in seqs for h in range(NHEAD)]
    load_head(C, S, A.slots[0], D.qbt, D.kbt, D.vb, units[0][2], units[0][0], units[0][1])
    nq = 0
    nblkc = [0]
    for ui, (s0, L, h) in enumerate(units):
        slot = A.slots[ui % 2]
        q_t, k_t, v_t, hb = slot
        r_t, r_b = rtab[ui % 2]
        ACT(S, r_t[:], C.absd[:], AF.Exp, [cb], [r_b], scale=-SLOPES[h])
        TT(S, "pool", r_t[:], r_t[:], mtab[:], ALU.mult, [r_b, cb], [r_b])
        nblk = L // 128
        for qi in range(L // 512):
            q0 = qi * 512
            acc, accb = accs[nq % 2]
            g_t, g_b = gate[nq % 3]
            DMA(S, g_t[:], D.gbt[h * 64:(h + 1) * 64, s0 + q0:s0 + q0 + 512], [], [g_b])
            kbs = list(range(max(0, q0 // 128 - 8), min(nblk, q0 // 128 + 4 + 8)))
            rc_t, rc_b = rc[nq % 2]
            t_t, t_b = tq[nq % 2]
            o_t, o_b = ob[nq % 2]
            dst = D.mixt[BW + h * 64:BW + (h + 1) * 64, s0 + q0:s0 + q0 + 512]

            def epiA(acc=acc, accb=accb, rc_t=rc_t, rc_b=rc_b):
                ACT(S, rc_t[64:65, :], acc[64:65, 0:512], AF.Ln, [accb], [rc_b])
                ACT(S, rc_t[64:65, :], rc_t[64:65, :], AF.Exp, [rc_b], [rc_b], scale=-1.0)

            def epiB(acc=acc, accb=accb, rc_t=rc_t, rc_b=rc_b, t_t=t_t, t_b=t_b, o_t=o_t, o_b=o_b,
                     g_t=g_t, g_b=g_b, dst=dst):
                bc, bcb = P.next()
                MMG(S, [(bc[0:64, 0:512], C.ones_f[64:65, 0:64], rc_t[64:65, :], True, True)], [rc_b, cb], [bcb])
                TT(S, "dve", t_t[:], acc[0:64, 0:512], g_t[:], ALU.mult, [accb, g_b], [t_b])
                TT(S, "dve", o_t[:], bc[0:64, 0:512], t_t[:], ALU.mult, [bcb, t_b], [o_b])
                DMA(S, dst, o_t[:], [o_b], [])

            sbl = [kbs[i:i + 2] for i in range(0, len(kbs), 2)]
            for si, grp in enumerate(sbl):
                pts = []
                for _ in grp:
                    pts.append(A.pT[A.npT % len(A.pT)])
                    A.npT += 1
                first, last = (si == 0), (si == len(sbl) - 1)

                def qk(grp=grp, pts=pts, q0=q0, k_t=k_t, q_t=q_t, hb=hb, r_t=r_t, r_b=r_b):
                    bks = [P.next() for _ in grp]
                    MMG(S, [(bks[j][0][:, 0:512], k_t[64 * j:64 * j + 64, kb * 128:(kb + 1) * 128],
                             q_t[64 * j:64 * j + 64, q0:q0 + 512], True, True) for j, kb in enumerate(grp)],
                        [hb], [bb for _, bb in bks])
                    for j, kb in enumerate(grp):
                        o = kb * 128 - q0
                        bk, bb = bks[j]
                        p_t, p_b = pts[j]
                        ex_t, ex_b = A.ex[A.nex % len(A.ex)]
                        A.nex += 1
                        ACT(S, ex_t[:], bk[:, 0:512], AF.Exp, [bb], [ex_b])
                        eng = "pool" if nblkc[0] % 3 == 2 else "dve"
                        nblkc[0] += 1
                        TT(S, eng, p_t[:], ex_t[:], r_t[:, TC - o:TC - o + 512], ALU.mult, [ex_b, r_b], [p_b])

                def pv(grp=grp, pts=pts, first=first, last=last, acc=acc, accb=accb, v_t=v_t, hb=hb,
                       epiA=epiA, epiB=epiB):
                    MMG(S, [(acc[0:65, 0:512], v_t[:, kb, :], pts[j][0][:], first and j == 0, last and j == len(grp) - 1)
                            for j, kb in enumerate(grp)], [hb] + [p_b for _, p_b in pts], [accb])
                    if last:
                        pipe.at(2, epiA)
                        pipe.at(4, epiB)
                pipe.block(qk, pv)
            nq += 1
            if qi == 0 and ui + 1 < len(units):
                n0, nL, nh = units[ui + 1]
                load_head(C, S, A.slots[(ui + 1) % 2], D.qbt, D.kbt, D.vb, nh, n0, nL)
    pipe.flush()


def load_head_c(C, S, slot, D, h, s0, L, sl, JLf, cb):
    (q_t, k_t, v_t, hb, kxb, pat, patf) = slot
    for c in range(2):
        r0 = h * 64 + c * 32
        DMA(S, q_t[64 * c + 3:64 * c + 35, 0:L], D.qct[r0:r0 + 32, s0:s0 + L], [], [hb])
        DMA(S, k_t[64 * c + 3:64 * c + 35, 0:L], D.kct[r0:r0 + 32, s0:s0 + L], [], [hb])
    nb = L // 128
    for b0 in range(0, nb, 16):
        b1 = min(nb, b0 + 16)
        DMA(S, v_t[:, b0:b1, 0:64],
            D.vc[s0 + b0 * 128:s0 + b1 * 128, h * 64:(h + 1) * 64].rearrange("(b p) e -> p b e", p=128), [], [hb])
    pb = Buf()
    TS(S, "dve", patf[0:1, 0, :], JLf[0:1, :], sl, ALU.mult, [cb], [pb])
    CP(S, "dve", pat[0:1, 0, :], patf[0:1, 0, :], [pb], [pb])
    CP(S, "dve", patf[0:1, 1, :], pat[0:1, 0, :], [pb], [pb])
    TT(S, "dve", patf[0:1, 2, :], patf[0:1, 0, :], patf[0:1, 1, :], ALU.subtract, [pb], [pb])
    CP(S, "dve", pat[0:1, 1, :], patf[0:1, 2, :], [pb], [pb])
    CP(S, "dve", patf[0:1, 1, :], pat[0:1, 1, :], [pb], [pb])
    TT(S, "dve", patf[0:1, 0, :], patf[0:1, 2, :], patf[0:1, 1, :], ALU.subtract, [pb], [pb])
    CP(S, "dve", pat[0:1, 2, :], patf[0:1, 0, :], [pb], [pb])
    nt = L // 512
    for c in range(2):
        for r in range(3):
            DMA(S, q_t[64 * c + r:64 * c + r + 1, 0:L].rearrange("p (t j) -> p t j", j=512),
                pat[0:1, r, :].unsqueeze(1).to_broadcast([1, nt, 512]), [pb], [hb])
        S.emit("dve", lambda e, c=c: e.memset(k_t[64 * c:64 * c + 3, 0:512], 0.0), [], [kxb[0]])
        if L > 512:
            S.emit("dve", lambda e, c=c: e.memset(k_t[64 * c:64 * c + 3, 512:L], 1.0), [], list(kxb[1:L // 512]))


def phase3(C, S, sb, W, D, l, xsrc, ydst, seqs, NT):
    nc, P = C.nc, C.P
    build_consts(C, S, sb)
    cb = C.constbuf
    maxL = max(L for _, L in seqs)
    NB = maxL // 128
    lambda_init = 0.8 - 0.6 * math.exp(-0.3 * l)
    A = Ctx()
    A.slots = []
    for i in range(2):
        q_t = sb(f"aq{i}", [128, maxL], BF16)
        k_t = sb(f"ak{i}", [128, maxL], BF16)
        v_t = sb(f"av{i}", [128, maxL // 128, VPAD], BF16)
        pat = sb(f"pat{i}", [1, 3, 512], BF16)
        patf = sb(f"patf{i}", [1, 3, 512], F32)
        hb = Buf()
        kxb = [Buf() for _ in range(maxL // 512)]
        S.emit("pool", lambda e, v_t=v_t: e.memset(v_t[:, :, 64:VPAD], 1.0), [], [hb])
        A.slots.append((q_t, k_t, v_t, hb, kxb, pat, patf))
    A.pT = [(sb(f"pT{i}", [128, 512], BF16), Buf()) for i in range(10)]
    A.ex = [(sb(f"ex{i}", [128, 512], F32), Buf()) for i in range(4)]
    A.npT = 0
    A.nex = 0
    tli = sb("tli", [128, NB + 1], I32)
    tri = sb("tri", [128, NB + 1], I32)
    jli = sb("jli", [128, 512], I32)
    TLf = sb("TLf", [128, NB + 1], F32)
    TRf = sb("TRf", [128, NB + 1], F32)
    JLf = sb("JLf", [128, 512], F32)
    S.emit("pool", lambda e: e.iota(tli[:], [[128, NB + 1]], base=0, channel_multiplier=-1), [], [cb])
    S.emit("pool", lambda e: e.iota(tri[:], [[128, NB + 1]], base=0, channel_multiplier=1), [], [cb])
    S.emit("pool", lambda e: e.iota(jli[:], [[1, 512]], base=0, channel_multiplier=0), [], [cb])
    for a, b in ((TLf, tli), (TRf, tri), (JLf, jli)):
        CP(S, "dve", a[:], b[:], [cb], [cb])
    lamv = sb("lamv", [64, 4, 32], F32)
    lams = sb("lams", [64, 8], F32)
    lamj = sb("lamj", [64, 32], F32)
    for i, src in enumerate(W.lam):
        DMA(S, lamv[:, i, :], src[l:l + 1, :].to_broadcast([64, 32]), [], [cb])
    for i in range(2):
        TT(S, "dve", lamj[:], lamv[:, 2 * i, :], lamv[:, 2 * i + 1, :], ALU.mult, [cb], [cb])
        S.emit("dve", lambda e, i=i: e.tensor_reduce(out=lams[:, i:i + 1], in_=lamj[:], axis=AX.X, op=ALU.add), [cb], [cb])
    ACT(S, lams[:, 2:4], lams[:, 0:2], AF.Exp, [cb], [cb])
    TT(S, "dve", lams[:, 4:5], lams[:, 3:4], lams[:, 2:3], ALU.subtract, [cb], [cb])
    TS(S, "dve", lams[:, 5:6], lams[:, 4:5], -lambda_init, ALU.add, [cb], [cb])
    subg = sb("subg", [64, 2], F32)
    DMA(S, subg[:, 0:1], W.subln_g[l:l + 1, :].rearrange("o e -> e o"), [], [cb], slow=True)
    TS(S, "dve", subg[:, 1:2], subg[:, 0:1], 1.0 - lambda_init, ALU.mult, [cb], [cb])
    neglam = lams[:, 5:6]

    rtab = [(sb(f"rtab{i}", [128, 896], F32), Buf()) for i in range(2)]
    bl = [sb(f"bl{i}", [128, NB + 1], F32) for i in range(2)]
    br = [sb(f"br{i}", [128, NB + 1], F32) for i in range(2)]
    gate = [(sb(f"gate{i}", [64, 512], BF16), Buf()) for i in range(3)]
    ocs = [(sb(f"ocs{i}", [65, 2, 512], F32), Buf()) for i in range(2)]
    rc = [(sb(f"rc{i}", [128, 2, 512], F32), Buf()) for i in range(2)]
    a01 = [(sb(f"a01{i}", [64, 2, 512], F32), Buf()) for i in range(2)]
    at = [(sb(f"at{i}", [64, 512], F32), Buf()) for i in range(2)]
    sqb = [(sb(f"sqb{i}", [64, 512], F32), Buf()) for i in range(2)]
    rs = [(sb(f"rs{i}", [64, 512], F32), Buf()) for i in range(2)]
    ob = [(sb(f"ob{i}", [64, 512], BF16), Buf()) for i in range(2)]
    P.set_rot([0, 1, 2, 3, 6, 7])
    accs = [(P.banks[4], P.banks[5]), (P.banks[4], P.banks[5])]
    pipe = Pipe(2)
    units = [(s0, L, h) for (s0, L) in seqs for h in range(NHEAD)]
    load_head_c(C, S, A.slots[0], D, units[0][2], units[0][0], units[0][1], SLOPES[units[0][2]], JLf, cb)
    nq = 0
    for ui, (s0, L, h) in enumerate(units):
        q_t, k_t, v_t, hb, kxb, pat, patf = A.slots[ui % 2]
        sl = SLOPES[h]
        r_t, r_b = rtab[ui % 2]
        ACT(S, r_t[:], C.absd[:, TC - 384:TC - 384 + 896], AF.Exp, [cb], [r_b], scale=-sl)
        bl_t, br_t = bl[ui % 2], br[ui % 2]
        TS(S, "dve", bl_t[:], TLf[:], -sl, ALU.mult, [cb], [r_b])
        TS(S, "dve", br_t[:], TRf[:], -sl, ALU.mult, [cb], [r_b])
        nblk = L // 128
        dmax = SKIP_T / sl
        if WARM_N:
            pe_warmup(S, P, C.identb[:], A.pT[0][0][:], WARM_N)
        for qi in range(L // 512):
            q0 = qi * 512
            def next_signs(qn=qi + 1, k_t=k_t, kxb=kxb):
                for c in range(2):
                    S.emit("dve", lambda e, c=c: e.memset(k_t[64 * c:64 * c + 3, (qn - 1) * 512:qn * 512], -1.0),
                           [], [kxb[qn - 1]])
                    S.emit("dve", lambda e, c=c: e.memset(k_t[64 * c:64 * c + 3, qn * 512:(qn + 1) * 512], 0.0),
                           [], [kxb[qn]])
            g_t, g_b = gate[nq % 3]
            DMA(S, g_t[:], D.gct[h * 64:(h + 1) * 64, s0 + q0:s0 + q0 + 512], [], [g_b])
            oc_t, oc_b = ocs[nq % 2]
            rc_t, rc_b = rc[nq % 2]
            a_t, a_b = a01[nq % 2]
            at_t, at_b = at[nq % 2]
            sq_t, sq_b = sqb[nq % 2]
            rs_t, rs_b = rs[nq % 2]
            o_t, o_b = ob[nq % 2]
            acc2 = accs[nq % 2]
            dstm = D.mixt[2 * BW + h * 64:2 * BW + (h + 1) * 64, s0 + q0:s0 + q0 + 512]
            lefts = [kb for kb in range(nblk) if kb * 128 - q0 <= -128 and (q0 - kb * 128 - 127) <= dmax]
            diags = [kb for kb in range(nblk) if 0 <= kb * 128 - q0 < 512]
            rights = [kb for kb in range(nblk) if kb * 128 - q0 >= 512 and (kb * 128 - q0 - 511) <= dmax]
            sbs = [("D", kb) for kb in diags] + [("R", kb) for kb in rights] + [("L", kb) for kb in lefts]
            sign_at = min(len(sbs) - 1, 9)

            def epi0(oc_t=oc_t, oc_b=oc_b, acc2=acc2):
                for c in range(2):
                    CP(S, "dve", oc_t[:, c, :], acc2[c][0][0:65, 0:512], [acc2[c][1]], [oc_b])

            def epiA(oc_t=oc_t, oc_b=oc_b, rc_t=rc_t, rc_b=rc_b):
                ACT(S, rc_t[64:65, :, :], oc_t[64:65, :, :], AF.Ln, [oc_b], [rc_b])
                ACT(S, rc_t[64:65, :, :], rc_t[64:65, :, :], AF.Exp, [rc_b], [rc_b], scale=-1.0)

            def epiB(oc_t=oc_t, oc_b=oc_b, rc_t=rc_t, rc_b=rc_b, a_t=a_t, a_b=a_b, at_t=at_t, at_b=at_b,
                     sq_t=sq_t, sq_b=sq_b):
                for c in range(2):
                    bk, bb = P.next()
                    MMG(S, [(bk[0:64, 0:512], C.ones_f[64:65, 0:64], rc_t[64:65, c, :], True, True)], [rc_b, cb], [bb])
                    TT(S, "dve", a_t[:, c, :], oc_t[0:64, c, :], bk[0:64, 0:512], ALU.mult, [oc_b, bb], [a_b])
                STT(S, at_t[:], a_t[:, 1, :], neglam, a_t[:, 0, :], ALU.mult, ALU.add, [a_b, cb], [at_b])
                TT(S, "dve", sq_t[:], at_t[:], at_t[:], ALU.mult, [at_b], [sq_b])

            def epiC(sq_t=sq_t, sq_b=sq_b, rs_t=rs_t, rs_b=rs_b):
                bk, bb = P.next()
                MMG(S, [(bk[0:64, 0:512], C.ones_f[0:64, 0:64], sq_t[:], True, True)], [sq_b, cb], [bb])
                ACT(S, rs_t[:], bk[0:64, 0:512], AF.Ln, [bb], [rs_b], scale=1.0 / 64, bias=EPS)
                ACT(S, rs_t[:], rs_t[:], AF.Exp, [rs_b], [rs_b], scale=-0.5)

            def epiD(at_t=at_t, at_b=at_b, rs_t=rs_t, rs_b=rs_b, o_t=o_t, o_b=o_b, g_t=g_t, g_b=g_b, dstm=dstm):
                TT(S, "dve", at_t[:], at_t[:], rs_t[:], ALU.mult, [at_b, rs_b], [at_b])
                STT(S, o_t[:], at_t[:], subg[:, 1:2], g_t[:], ALU.mult, ALU.mult, [at_b, g_b, cb], [o_b])
                DMA(S, dstm, o_t[:], [o_b], [])

            for si, (kind, kb) in enumerate(sbs):
                pts = []
                for _ in range(2):
                    pts.append(A.pT[A.npT % len(A.pT)])
                    A.npT += 1
                first, last = (si == 0), (si == len(sbs) - 1)

                def qk(kind=kind, kb=kb, pts=pts, q0=q0, k_t=k_t, q_t=q_t, hb=hb, r_t=r_t, r_b=r_b,
                       bl_t=bl_t, br_t=br_t, kxb=kxb):
                    bks = [P.next() for _ in range(2)]
                    MMG(S, [(bks[c][0][:, 0:512], k_t[64 * c:64 * c + 35, kb * 128:(kb + 1) * 128],
                             q_t[64 * c:64 * c + 35, q0:q0 + 512], True, True) for _ in range(QK_REP) for c in range(2)],
                        [hb, kxb[kb // 4]], [bks[0][1], bks[1][1]])
                    o = kb * 128 - q0
                    for c in range(2):
                        bk, bb = bks[c]
                        p_t, p_b = pts[c]
                        if kind == "L":
                            n = (-o) // 128
                            ACT(S, p_t[:], bk[:, 0:512], AF.Exp, [bb, r_b], [p_b], bias=bl_t[:, n:n + 1])
                        elif kind == "R":
                            n = o // 128
                            ACT(S, p_t[:], bk[:, 0:512], AF.Exp, [bb, r_b], [p_b], bias=br_t[:, n:n + 1])
                        else:
                            ex_t, ex_b = A.ex[A.nex % len(A.ex)]
                            A.nex += 1
                            ACT(S, ex_t[:], bk[:, 0:512], AF.Exp, [bb], [ex_b])
                            TT(S, "pool" if c == 0 else "dve", p_t[:], ex_t[:], r_t[:, 384 - o:384 - o + 512], ALU.mult,
                               [ex_b, r_b], [p_b])

                def pv(kb=kb, pts=pts, first=first, last=last, acc2=acc2, v_t=v_t, hb=hb,
                       epi0=epi0, epiA=epiA, epiB=epiB, epiC=epiC, epiD=epiD):
                    MMG(S, [(acc2[c][0][0:VPAD, 0:512], v_t[:, kb, :], pts[c][0][:], first, last) for c in range(2)],
                        [hb, pts[0][1], pts[1][1]], [acc2[0][1], acc2[1][1]])
                    if last:
                        epi0()
                        pipe.at(2, epiA)
                        pipe.at(5, epiB)
                        pipe.at(8, epiC)
                        pipe.at(11, epiD)
                pipe.block(qk, pv)
                if si == sign_at and qi + 1 < L // 512:
                    next_signs()
                if WARM_Q and si == 2:
                    pe_warmup(S, P, C.identb[:], A.pT[0][0][:], WARM_Q)
            nq += 1
            if qi == 0 and ui + 1 < len(units):
                n0, nL, nh = units[ui + 1]
                load_head_c(C, S, A.slots[(ui + 1) % 2], D, nh, n0, nL, SLOPES[nh], JLf, cb)
    pipe.flush()


def phase4(C, S, sb, W, D, l, xsrc, ydst, seqs, NT):
    nc, P = C.nc, C.P
    P.set_rot(range(8))
    wo = sb("wo", [128, 12, D_MODEL], BF16)
    wbuf = Buf()
    wst = [(sb(f"wost{i}", [128, D_MODEL], F32), Buf()) for i in range(2)]
    for kc in range(12):
        st, stb = wst[kc % 2]
        DMA(S, st[:], W.w_out[l, kc * 128:(kc + 1) * 128, :], [], [stb])
        CP(S, ("dve", "act")[kc % 2], wo[:, kc, :], st[:], [stb], [wbuf])
    cw = sb("cw", [128, 3, 3], F32)
    for ch in range(3):
        DMA(S, cw[:, ch, :], W.conv_w[l, :, ch * 128:(ch + 1) * 128].rearrange("j p -> p j"), [], [wbuf], slow=True)
    mx = [(sb(f"mx{i}", [128, 12, 512], BF16), Buf()) for i in range(2)]
    zt = [(sb(f"zt{i}", [128, 3, 514], F32), Buf()) for i in range(2)]
    gd = [(sb(f"gd{i}", [128, 3, 512], F32), Buf()) for i in range(2)]
    cv = [(sb(f"cv{i}", [128, 3, 512], F32), Buf()) for i in range(2)]
    xr = [(sb(f"xr{i}", [128, D_MODEL], F32), Buf()) for i in range(3)]
    yo = [(sb(f"yo{i}", [128, D_MODEL], F32), Buf()) for i in range(3)]
    starts = {s0 for s0, _ in seqs}
    ends = {s0 + L for s0, L in seqs}
    ngroups = NT // 512
    nt = 0

    def loads(g):
        tok0 = g * 512
        m_t, m_b = mx[g % 2]
        DMA(S, m_t[:, 0:9, :], D.mixt[0:3 * BW, tok0:tok0 + 512].rearrange("(c p) t -> p c t", p=128), [], [m_b])
        z_t, z_b = zt[g % 2]
        lo = 0 if tok0 in starts else 1
        hi = 0 if (tok0 + 512) in ends else 1
        if lo == 0:
            S.emit("pool", lambda e: e.memset(z_t[:, :, 0:1], 0.0), [], [z_b])
        if hi == 0:
            S.emit("pool", lambda e: e.memset(z_t[:, :, 513:514], 0.0), [], [z_b])
        DMA(S, z_t[:, :, 1 - lo:513 + hi],
            D.zt[:, tok0 - lo:tok0 + 512 + hi].rearrange("(c p) t -> p c t", p=128), [], [z_b])
        g_t, g_b = gd[g % 2]
        DMA(S, g_t[:], D.gdt[:, tok0:tok0 + 512].rearrange("(c p) t -> p c t", p=128), [], [g_b])

    loads(0)
    for g in range(ngroups):
        tok0 = g * 512
        if g + 1 < ngroups:
            loads(g + 1)
        m_t, m_b = mx[g % 2]
        z_t, z_b = zt[g % 2]
        g_t, g_b = gd[g % 2]
        c_t, c_b = cv[g % 2]
        for ch in range(3):
            TS(S, "dve", c_t[:, ch, :], z_t[:, ch, 0:512], cw[:, ch, 0:1], ALU.mult, [z_b, wbuf], [c_b])
            STT(S, c_t[:, ch, :], z_t[:, ch, 1:513], cw[:, ch, 1:2], c_t[:, ch, :], ALU.mult, ALU.add, [z_b, c_b, wbuf], [c_b])
            STT(S, c_t[:, ch, :], z_t[:, ch, 2:514], cw[:, ch, 2:3], c_t[:, ch, :], ALU.mult, ALU.add, [z_b, c_b, wbuf], [c_b])
            TT(S, "pool", m_t[:, 9 + ch, :], c_t[:, ch, :], g_t[:, ch, :], ALU.mult, [c_b, g_b], [m_b])
        for j in range(4):
            t0 = tok0 + j * 128
            x_t, x_b = xr[nt % 3]
            DMA(S, x_t[:], xsrc[t0:t0 + 128, :], [], [x_b])
            y_t, y_b = yo[nt % 3]
            for half in range(2):
                bk, bb = P.next()
                MMG(S, [(bk[:, 0:512], m_t[:, kc, j * 128:(j + 1) * 128], wo[:, kc, half * 512:(half + 1) * 512],
                         kc == 0, kc == 11) for kc in range(12)], [m_b, wbuf], [bb])
                TT(S, "dve", y_t[:, half * 512:(half + 1) * 512], bk[:, 0:512], x_t[:, half * 512:(half + 1) * 512],
                   ALU.add, [bb, x_b], [y_b])
            DMA(S, ydst[t0:t0 + 128, :], y_t[:], [y_b], [])
            nt += 1


_NC_CACHE = {}
SEQS = (8192, 4096, 4096)
WNAMES = ("norm_g", "w_in", "sgu_g", "w_s", "b_s", "qn_b", "kn_b", "qn_c", "kn_c",
          "lam_q1", "lam_k1", "lam_q2", "lam_k2", "subln_g", "conv_w", "w_out")


def kernel(x_prompt, x_sample, **w):
    x_prompt = np.asarray(x_prompt, dtype=np.float32)
    x_sample = np.asarray(x_sample, dtype=np.float32)
    if "nc" not in _NC_CACHE:
        _NC_CACHE["nc"] = build_program(SEQS)
    nc = _NC_CACHE["nc"]
    wmap = {k: np.ascontiguousarray(np.asarray(w[k], dtype=np.float32)) for k in WNAMES}
    in_maps = []
    for c in range(N_CORES):
        xc = np.concatenate([x_prompt[c], x_sample[2 * c], x_sample[2 * c + 1]], axis=0)
        m = {"x": np.ascontiguousarray(xc)}
        m.update(wmap)
        in_maps.append(m)
    res = run_bass_kernel_spmd(nc, in_maps, core_ids=list(range(N_CORES)))
    yp = np.empty_like(x_prompt)
    ys = np.empty_like(x_sample)
    for c in range(N_CORES):
        y = res.results[c]["y"]
        yp[c] = y[0:8192]
        ys[2 * c] = y[8192:12288]
        ys[2 * c + 1] = y[12288:16384]
    return (yp, ys)
```

```python
import math
from contextlib import ExitStack
import numpy as np
import concourse.bass as bass
import concourse.mybir as mybir
from concourse.bass_utils import run_bass_kernel_spmd

F32 = mybir.dt.float32
BF16 = mybir.dt.bfloat16
I32 = mybir.dt.int32
AF = mybir.ActivationFunctionType
ALU = mybir.AluOpType
AX = mybir.AxisListType

D_MODEL = 1024
DEPTH = 2
BW = 384
PROJ_W = 15 * BW
MIX_W = 4 * BW
EPS = 1e-6
NHEAD = 6
SLOPES = [2.0 ** (-8.0 * (i + 1) / NHEAD) for i in range(NHEAD)]
N_CORES = 8
(P_AU, P_AV, P_AG, P_BQ, P_BK, P_BV, P_BG, P_CQ, P_CK, P_CV, P_CG, P_DI, P_DB, P_DC, P_DG) = range(15)
TC = 1408
TW = 2944
SKIP_T = 60.0
VPAD = 65
WARM_N = 0
QK_REP = 1
WARM_Q = 0
SBK = 2


class Buf:
    __slots__ = ("w", "r")

    def __init__(self):
        self.w = None
        self.r = {}


class Sched:
    ENG = ("pe", "act", "dve", "pool", "sp")

    def __init__(self, n_dma=24):
        self.cnt = {e: 0 for e in ("pe", "act", "dve", "pool")}
        self.dma_val = [0] * n_dma
        self.dma_rr = 0
        self.seen = {e: {} for e in self.ENG}
        self.ops = {e: [] for e in self.ENG}

    def new_phase(self):
        self.ops = {e: [] for e in self.ENG}

    def emit(self, eng, fn, reads=(), writes=(), dma=False):
        waits = {}
        seen = self.seen[eng]

        def add(ev, raw):
            if ev is None:
                return
            k, val = ev
            if k[0] == 'e' and k[1] == eng:
                if not (raw and eng in ("act", "dve", "pool")):
                    return
            if seen.get(k, 0) >= val:
                return
            if waits.get(k, 0) < val:
                waits[k] = val

        for b in reads:
            add(b.w, True)
        for b in writes:
            add(b.w, False)
            for k, v in b.r.items():
                add((k, v), False)
        if dma:
            s = self.dma_rr
            self.dma_rr = (s + 1) % len(self.dma_val)
            prev = self.dma_val[s]
            if prev > 0:
                add((('d', s), prev), False)
            self.dma_val[s] = prev + 16
            ev = (('d', s), prev + 16)
        else:
            self.cnt[eng] += 1
            ev = (('e', eng), self.cnt[eng])
        for k, v in waits.items():
            seen[k] = v
        self.ops[eng].append((fn, list(waits.items()), ev))
        for b in reads:
            if b.r.get(ev[0], 0) < ev[1]:
                b.r[ev[0]] = ev[1]
        for b in writes:
            b.w = ev
            b.r = {}
        return ev

    def barrier(self):
        allev = [(('e', e), c) for e, c in self.cnt.items() if c > 0]
        allev += [(('d', s), v) for s, v in enumerate(self.dma_val) if v > 0]
        for eng in self.ENG:
            waits = {}
            for k, v in allev:
                if k[0] == 'e' and k[1] == eng:
                    continue
                if self.seen[eng].get(k, 0) >= v:
                    continue
                waits[k] = v
                self.seen[eng][k] = v
            self.ops[eng].append((None, list(waits.items()), None))

    def replay(self, nc, sems):
        with nc.Block() as block:
            def run(eng, e):
                for fn, waits, ev in self.ops[eng]:
                    for k, v in waits:
                        e.wait_ge(sems[k], v)
                    if fn is None:
                        continue
                    ins = fn(e)
                    if ev[0][0] == 'd':
                        ins.then_inc(sems[ev[0]], 16)
                    else:
                        ins.then_inc(sems[ev[0]], 1)

            @block.tensor
            def _(e):
                run("pe", e)

            @block.scalar
            def _(e):
                run("act", e)

            @block.vector
            def _(e):
                run("dve", e)

            @block.gpsimd
            def _(e):
                run("pool", e)

            @block.sync
            def _(e):
                run("sp", e)


def ACT(S, out, in_, func, R, W, bias=None, scale=None, accum_out=None):
    kw = {}
    if bias is not None:
        kw["bias"] = bias
    if scale is not None:
        kw["scale"] = scale
    if accum_out is not None:
        kw["accum_out"] = accum_out
    S.emit("act", lambda e: e.activation(out=out, in_=in_, func=func, **kw), R, W)


def TT(S, eng, out, in0, in1, op, R, W):
    S.emit(eng, lambda e: e.tensor_tensor(out=out, in0=in0, in1=in1, op=op), R, W)


def TS(S, eng, out, in0, s1, op0, R, W, s2=None, op1=None):
    if op1 is None:
        S.emit(eng, lambda e: e.tensor_scalar(out=out, in0=in0, scalar1=s1, scalar2=None, op0=op0), R, W)
    else:
        S.emit(eng, lambda e: e.tensor_scalar(out=out, in0=in0, scalar1=s1, scalar2=s2, op0=op0, op1=op1), R, W)


def STT(S, out, in0, scalar, in1, op0, op1, R, W):
    S.emit("dve", lambda e: e.scalar_tensor_tensor(out=out, in0=in0, scalar=scalar, in1=in1, op0=op0, op1=op1), R, W)


def CP(S, eng, out, in_, R, W):
    if eng == "act":
        S.emit(eng, lambda e: e.copy(out=out, in_=in_), R, W)
    else:
        S.emit(eng, lambda e: e.tensor_copy(out=out, in_=in_), R, W)


def MMG(S, mms, R, W):
    def fn(e):
        ins = None
        for (out, lhsT, rhs, st, sp) in mms:
            ins = e.matmul(out, lhsT=lhsT, rhs=rhs, start=st, stop=sp)
        return ins
    S.emit("pe", fn, R, W)


def TRG(S, trs, R, W):
    def fn(e):
        ins = None
        for (out, in_, ident) in trs:
            ins = e.transpose(out=out, in_=in_, identity=ident)
        return ins
    S.emit("pe", fn, R, W)


def DMA(S, out, in_, R, W, slow=False):
    if slow:
        S.emit("sp", lambda e: e.dma_start(out=out, in_=in_, allow_slow_non_contiguous=True), R, W, dma=True)
    else:
        S.emit("sp", lambda e: e.dma_start(out=out, in_=in_), R, W, dma=True)


class PS:
    def __init__(self, banks):
        self.banks = banks
        self.rot = list(range(8))
        self.i = 0

    def set_rot(self, idxs):
        self.rot = list(idxs)
        self.i = 0

    def next(self):
        b = self.banks[self.rot[self.i % len(self.rot)]]
        self.i += 1
        return b


class Ctx:
    pass


_uid = [0]


def mk_sb(nc, es):
    def sb(name, shape, dt):
        _uid[0] += 1
        return es.enter_context(nc.sbuf_tensor(f"{name}_{_uid[0]}", shape, dt))
    return sb


def build_consts(C, S, sb, full=True):
    nc = C.nc
    b = Buf()
    C.constbuf = b
    C.identb = sb("identb", [128, 128], BF16)
    C.identf = sb("identf", [128, 128], F32)
    C.ones_f = sb("ones_f", [128, 64], F32)
    S.emit("pool", lambda e: e.memset(C.ones_f[:], 1.0), [], [b])
    if not full:
        iot = sb("iot", [128, 128], I32)
        S.emit("pool", lambda e: e.iota(iot[:], [[-1, 128]], base=0, channel_multiplier=1), [], [b])
        TS(S, "dve", C.identb[:], iot[:], 0.0, ALU.is_equal, [b], [b])
        TS(S, "dve", C.identf[:], iot[:], 0.0, ALU.is_equal, [b], [b])
        return
    C.absd = sb("absd", [128, TW], F32)
    iot = sb("iot", [128, TW], I32)
    S.emit("pool", lambda e: e.iota(iot[:], [[-1, TW]], base=TC, channel_multiplier=1), [], [b])
    TS(S, "dve", C.identb[:], iot[:, TC:TC + 128], 0.0, ALU.is_equal, [b], [b])
    TS(S, "dve", C.identf[:], iot[:, TC:TC + 128], 0.0, ALU.is_equal, [b], [b])
    C.negd = sb("negd", [128, TW], F32)
    CP(S, "dve", C.negd[:], iot[:], [b], [b])
    TS(S, "dve", C.absd[:], C.negd[:], -1.0, ALU.mult, [b], [b])
    TT(S, "dve", C.absd[:], C.absd[:], C.negd[:], ALU.max, [b], [b])
    C.iot = iot


def build_program(seq_lens, dump=False, depth=DEPTH, phases=4):
    nc = bass.Bass("TRN2", target_bir_lowering=False)
    NT = sum(seq_lens)
    seqs = []
    o = 0
    for L in seq_lens:
        seqs.append((o, L))
        o += L
    C = Ctx()
    C.nc = nc
    ext_in = lambda name, shape: nc.dram_tensor(name, shape, F32, kind="ExternalInput").ap()
    x_in = ext_in("x", [NT, D_MODEL])
    norm_g = ext_in("norm_g", [DEPTH, D_MODEL])
    w_in = ext_in("w_in", [DEPTH, D_MODEL, PROJ_W])
    sgu_g = ext_in("sgu_g", [DEPTH, BW])
    w_s = ext_in("w_s", [DEPTH, NHEAD, 128, 128])
    b_s = ext_in("b_s", [DEPTH, NHEAD, 128])
    qn_b = ext_in("qn_b", [DEPTH, 64])
    kn_b = ext_in("kn_b", [DEPTH, 64])
    qn_c = ext_in("qn_c", [DEPTH, 32])
    kn_c = ext_in("kn_c", [DEPTH, 32])
    lam_q1 = ext_in("lam_q1", [DEPTH, 32])
    lam_k1 = ext_in("lam_k1", [DEPTH, 32])
    lam_q2 = ext_in("lam_q2", [DEPTH, 32])
    lam_k2 = ext_in("lam_k2", [DEPTH, 32])
    subln_g = ext_in("subln_g", [DEPTH, 64])
    conv_w = ext_in("conv_w", [DEPTH, 3, BW])
    w_out = ext_in("w_out", [DEPTH, MIX_W, D_MODEL])
    y_out = nc.dram_tensor("y", [NT, D_MODEL], F32, kind="ExternalOutput").ap()

    skind = "ExternalOutput" if dump else "Internal"
    scr = lambda name, shape, dt: nc.dram_tensor(name, shape, dt, kind=skind).ap()
    D = Ctx()
    D.x1 = scr("x1", [NT, D_MODEL], F32)
    D.qbt = scr("qbt", [BW, NT], BF16)
    D.kbt = scr("kbt", [BW, NT], BF16)
    D.vb = scr("vb", [NT, BW], BF16)
    D.gbt = scr("gbt", [BW, NT], BF16)
    D.qct = scr("qct", [BW, NT], BF16)
    D.kct = scr("kct", [BW, NT], BF16)
    D.vc = scr("vc", [NT, BW], BF16)
    D.gct = scr("gct", [BW, NT], BF16)
    D.zt = scr("zt", [BW, NT], F32)
    D.gdt = scr("gdt", [BW, NT], F32)
    D.mixt = scr("mixt", [MIX_W, NT], BF16)

    S = Sched()
    with ExitStack() as top:
        sems = {}
        for e in ("pe", "act", "dve", "pool"):
            sems[('e', e)] = top.enter_context(nc.semaphore(f"sem_{e}"))
        for s in range(len(S.dma_val)):
            sems[('d', s)] = top.enter_context(nc.semaphore(f"sem_d{s}"))
        banks = []
        for i in range(8):
            t = top.enter_context(nc.psum_tensor(f"bank{i}", [128, 512], F32))
            banks.append((t, Buf()))
        P = PS(banks)
        C.P = P
        W = Ctx()
        W.norm_g, W.w_in, W.sgu_g, W.w_s, W.b_s = norm_g, w_in, sgu_g, w_s, b_s
        W.qn_b, W.kn_b, W.qn_c, W.kn_c = qn_b, kn_b, qn_c, kn_c
        W.lam = (lam_q1, lam_k1, lam_q2, lam_k2)
        W.subln_g, W.conv_w, W.w_out = subln_g, conv_w, w_out

        for l in range(depth):
            xsrc = x_in if l == 0 else D.x1
            ydst = y_out if l == depth - 1 else D.x1
            for ph in (phase1, phase2, phase3, phase4)[:phases]:
                S.new_phase()
                for bk in banks:
                    bk[1].w = None
                    bk[1].r = {}
                with ExitStack() as es:
                    sb = mk_sb(nc, es)
                    ph(C, S, sb, W, D, l, xsrc, ydst, seqs, NT)
                    S.barrier()
                    S.replay(nc, sems)
    return nc


def phase1(C, S, sb, W, D, l, xsrc, ydst, seqs, NT):
    nc, P = C.nc, C.P
    P.set_rot(range(8))
    build_consts(C, S, sb, full=False)
    cb = C.constbuf
    wb = sb("wb", [128, 8, PROJ_W], BF16)
    wbuf = Buf()
    g8 = sb("g8", [128, 8], F32)
    DMA(S, g8[:], W.norm_g[l:l + 1, :].rearrange("o (k p) -> p (o k)", p=128), [], [wbuf], slow=True)
    WST = 640
    wst = [(sb(f"wst{i}", [128, WST], F32), Buf()) for i in range(2)]
    n = 0
    for kc in range(8):
        for c0 in range(0, PROJ_W, WST):
            st, stb = wst[n % 2]
            DMA(S, st[:], W.w_in[l, kc * 128:(kc + 1) * 128, c0:c0 + WST], [], [stb])
            if n % 2 == 0:
                TS(S, "dve", wb[:, kc, c0:c0 + WST], st[:], g8[:, kc:kc + 1], ALU.mult, [stb, wbuf], [wbuf])
            else:
                ACT(S, wb[:, kc, c0:c0 + WST], st[:], AF.Copy, [stb, wbuf], [wbuf], scale=g8[:, kc:kc + 1])
            n += 1
    wsT = sb("wsT", [128, NHEAD, 128], BF16)
    for g in range(NHEAD):
        st, stb = wst[n % 2]
        n += 1
        DMA(S, st[:, 0:128], W.w_s[l, g], [], [stb])
        bk, bb = P.next()
        TRG(S, [(bk[:, 0:128], st[:, 0:128], C.identf[:])], [stb, cb], [bb])
        CP(S, "dve", wsT[:, g, :], bk[:, 0:128], [bb], [wbuf])
    bsT = sb("bsT", [128, NHEAD], F32)
    DMA(S, bsT[:], W.b_s[l].rearrange("g t -> t g"), [], [wbuf], slow=True)
    sgu_bc = sb("sgu_bc", [128, BW], F32)
    DMA(S, sgu_bc[:], W.sgu_g[l:l + 1, :].to_broadcast([128, BW]), [], [wbuf])
    g64 = sb("g64", [128, 4, 64], F32)
    DMA(S, g64[:, 0, :], W.qn_b[l:l + 1, :].to_broadcast([128, 64]), [], [wbuf])
    DMA(S, g64[:, 1, :], W.kn_b[l:l + 1, :].to_broadcast([128, 64]), [], [wbuf])
    DMA(S, g64[:, 2, 0:32], W.qn_c[l:l + 1, :].to_broadcast([128, 32]), [], [wbuf])
    DMA(S, g64[:, 3, 0:32], W.kn_c[l:l + 1, :].to_broadcast([128, 32]), [], [wbuf])
    gq_b = sb("gq_b", [128, BW], F32)
    gk_b = sb("gk_b", [128, BW], F32)
    gq_c = sb("gq_c", [128, BW], F32)
    gk_c = sb("gk_c", [128, BW], F32)
    TS(S, "dve", gq_b[:].rearrange("p (h e) -> p h e", e=64), g64[:, 0, :].unsqueeze(1).to_broadcast([128, 6, 64]),
       0.125, ALU.mult, [wbuf], [wbuf])
    TS(S, "dve", gk_b[:].rearrange("p (h e) -> p h e", e=64), g64[:, 1, :].unsqueeze(1).to_broadcast([128, 6, 64]),
       1.0, ALU.mult, [wbuf], [wbuf])
    TS(S, "dve", gq_c[:].rearrange("p (h e) -> p h e", e=32), g64[:, 2, 0:32].unsqueeze(1).to_broadcast([128, 12, 32]),
       1.0 / math.sqrt(32.0), ALU.mult, [wbuf], [wbuf])
    TS(S, "dve", gk_c[:].rearrange("p (h e) -> p h e", e=32), g64[:, 3, 0:32].unsqueeze(1).to_broadcast([128, 12, 32]),
       1.0, ALU.mult, [wbuf], [wbuf])

    def slots(name, shape, dt, k):
        return [(sb(f"{name}{i}", shape, dt), Buf()) for i in range(k)]
    xs = slots("xs", [128, D_MODEL], F32, 2)
    junk = sb("junk", [128, D_MODEL], BF16)
    junkb = Buf()
    hb = slots("hb", [128, D_MODEL], BF16, 4)
    hT = slots("hT", [128, 8, 512], BF16, 2)
    st4 = slots("st4", [128, 4], F32, 2)
    sqf = slots("sqf", [128, BW], F32, 2)
    s12 = slots("s12", [128, 3, 12], F32, 2)
    vn = slots("vn", [128, BW], BF16, 1)
    sg = slots("sg", [128, BW], F32, 1)
    t1 = slots("t1", [128, BW], F32, 1)
    t2 = slots("t2", [128, BW], F32, 1)
    mixA = slots("mixA", [128, BW], BF16, 1)
    qt = slots("qt", [128, BW], F32, 2)
    qn = slots("qn", [128, BW], BF16, 4)
    stg_T = {nm: slots(f"stg_{nm}", [128, 3, 512], BF16, 1)[0] for nm in ("mixa", "qbt", "kbt", "qct", "kct")}
    stg_v = {nm: slots(f"stg_{nm}", [128, 4, BW], BF16, 1)[0] for nm in ("vb", "vc")}
    fm_b = slots("fm_b", [128, 512], BF16, 3)
    fm_f = slots("fm_f", [128, 512], F32, 2)
    tmpf = slots("tmpf", [128, 512], F32, 2)
    cnt = {"x": 0, "h": 0, "st": 0, "sq": 0, "s12": 0, "vn": 0, "sg": 0, "t1": 0, "t2": 0, "mixA": 0, "qt": 0, "qn": 0,
           "fmb": 0, "fmf": 0, "tmpf": 0}

    def nxt(lst, key):
        v = lst[cnt[key] % len(lst)]
        cnt[key] += 1
        return v

    ngroups = NT // 512

    def prep(g):
        res = []
        for j in range(4):
            t0 = g * 512 + j * 128
            x_t, x_b = nxt(xs, "x")
            DMA(S, x_t[:], xsrc[t0:t0 + 128, :], [], [x_b])
            s_t, s_b = nxt(st4, "st")
            ACT(S, junk[:], x_t[:], AF.Square, [x_b], [junkb, s_b], accum_out=s_t[:, 0:1])
            ACT(S, s_t[:, 1:2], s_t[:, 0:1], AF.Sqrt, [s_b], [s_b], scale=1.0 / D_MODEL, bias=EPS)
            S.emit("dve", lambda e, s_t=s_t: e.reciprocal(out=s_t[:, 2:3], in_=s_t[:, 1:2]), [s_b], [s_b])
            h_t, h_b = nxt(hb, "h")
            ACT(S, h_t[:], x_t[:], AF.Copy, [x_b, s_b], [h_b], scale=s_t[:, 2:3])
            res.append((h_t, h_b))
        return res

    def transposes(g, hs):
        hT_t, hT_b = hT[g % 2]
        for j, (h_t, h_b) in enumerate(hs):
            bk, bb = P.next()
            bkb = bk[:].bitcast(BF16)
            TRG(S, [(bkb[:, kc * 128:(kc + 1) * 128], h_t[:, kc * 128:(kc + 1) * 128], C.identb[:]) for kc in range(8)],
                [h_b, cb], [bb])
            CP(S, "dve" if j % 2 == 0 else "act", hT_t[:, :, j * 128:(j + 1) * 128],
               bkb[:, 0:1024].rearrange("p (k t) -> p k t", t=128), [bb], [hT_b])
        return hT_t, hT_b

    def proj_tm(hT_t, hT_b, j, piece):
        bk, bb = P.next()
        MMG(S, [(bk[:, 0:BW], hT_t[:, kc, j * 128:(j + 1) * 128], wb[:, kc, piece * BW:(piece + 1) * BW], kc == 0, kc == 7)
                for kc in range(8)], [hT_b, wbuf], [bb])
        return bk, bb

    def headnorm(bk, bb, nh, gain, eng2):
        hd = BW // nh
        sq_t, sq_b = nxt(sqf, "sq")
        ACT(S, sq_t[:], bk[:, 0:BW], AF.Square, [bb], [sq_b])
        s_t, s_b = nxt(s12, "s12")
        S.emit("dve", lambda e: e.tensor_reduce(out=s_t[:, 0, 0:nh], in_=sq_t[:].rearrange("p (h e) -> p h e", e=hd),
                                                axis=AX.X, op=ALU.add), [sq_b], [s_b])
        ACT(S, s_t[:, 1, 0:nh], s_t[:, 0, 0:nh], AF.Sqrt, [s_b], [s_b], scale=1.0 / hd, bias=EPS)
        S.emit("dve", lambda e: e.reciprocal(out=s_t[:, 2, 0:nh], in_=s_t[:, 1, 0:nh]), [s_b], [s_b])
        q_t, q_b = nxt(qt, "qt")
        TT(S, "dve", q_t[:].rearrange("p (h e) -> p h e", e=hd), bk[:, 0:BW].rearrange("p (h e) -> p h e", e=hd),
           s_t[:, 2, 0:nh].unsqueeze(2).to_broadcast([128, nh, hd]), ALU.mult, [bb, s_b], [q_b])
        n_t, n_b = nxt(qn, "qn")
        TT(S, eng2, n_t[:], q_t[:], gain[:], ALU.mult, [q_b, wbuf], [n_b])
        return n_t, n_b

    def tr3(src_t, src_b, stg, j, eng):
        st_t, st_b = stg
        bk, bb = P.next()
        bkb = bk[:].bitcast(BF16)
        TRG(S, [(bkb[:, c * 128:(c + 1) * 128], src_t[:, c * 128:(c + 1) * 128], C.identb[:]) for c in range(3)],
            [src_b, cb], [bb])
        CP(S, eng, st_t[:, :, j * 128:(j + 1) * 128], bkb[:, 0:384].rearrange("p (c t) -> p c t", t=128), [bb], [st_b])

    hs = prep(0)
    for g in range(ngroups):
        tok0 = g * 512
        hT_t, hT_b = transposes(g, hs)
        if g + 1 < ngroups:
            hs = prep(g + 1)
        for j in range(4):
            bk_v, bb_v = proj_tm(hT_t, hT_b, j, P_AV)
            s_t, s_b = nxt(st4, "st")
            sq_t, sq_b = nxt(sqf, "sq")
            ACT(S, sq_t[:], bk_v[:, 0:BW], AF.Square, [bb_v], [sq_b, s_b], accum_out=s_t[:, 0:1])
            ACT(S, s_t[:, 1:2], s_t[:, 0:1], AF.Sqrt, [s_b], [s_b], scale=1.0 / BW, bias=EPS)
            S.emit("dve", lambda e, s_t=s_t: e.reciprocal(out=s_t[:, 2:3], in_=s_t[:, 1:2]), [s_b], [s_b])
            vn_t, vn_b = nxt(vn, "vn")
            STT(S, vn_t[:], bk_v[:, 0:BW], s_t[:, 2:3], sgu_bc[:], ALU.mult, ALU.mult, [bb_v, s_b, wbuf], [vn_b])
            bk, bb = proj_tm(hT_t, hT_b, j, P_BQ)
            qb_n = headnorm(bk, bb, 6, gq_b, "pool")
            bk, bb = proj_tm(hT_t, hT_b, j, P_BK)
            kb_n = headnorm(bk, bb, 6, gk_b, "pool")
            bk_u, bb_u = proj_tm(hT_t, hT_b, j, P_AU)
            bk_g, bb_g = proj_tm(hT_t, hT_b, j, P_AG)
            sg_t, sg_b = nxt(sg, "sg")
            ACT(S, sg_t[:], bk_g[:, 0:BW], AF.Silu, [bb_g], [sg_b])
            bk_m, bb_m = P.next()
            MMG(S, [(bk_m[:, gg * 64:(gg + 1) * 64], wsT[:, gg, :], vn_t[:, gg * 64:(gg + 1) * 64], True, True)
                    for gg in range(NHEAD)], [vn_b, wbuf], [bb_m])
            t1_t, t1_b = nxt(t1, "t1")
            TT(S, "dve", t1_t[:].rearrange("p (h e) -> p h e", e=64), bk_m[:, 0:BW].rearrange("p (h e) -> p h e", e=64),
               bsT[:].unsqueeze(2).to_broadcast([128, 6, 64]), ALU.add, [bb_m, wbuf], [t1_b])
            t2_t, t2_b = nxt(t2, "t2")
            TT(S, "dve", t2_t[:], bk_u[:, 0:BW], t1_t[:], ALU.mult, [bb_u, t1_b], [t2_b])
            ma_t, ma_b = nxt(mixA, "mixA")
            TT(S, "pool", ma_t[:], t2_t[:], sg_t[:], ALU.mult, [t2_b, sg_b], [ma_b])
            bk, bb = proj_tm(hT_t, hT_b, j, P_CQ)
            qc_n = headnorm(bk, bb, 12, gq_c, "pool")
            tr3(qb_n[0], qb_n[1], stg_T["qbt"], j, "dve")
            tr3(kb_n[0], kb_n[1], stg_T["kbt"], j, "act")
            bk, bb = proj_tm(hT_t, hT_b, j, P_CK)
            kc_n = headnorm(bk, bb, 12, gk_c, "pool")
            bk, bb = proj_tm(hT_t, hT_b, j, P_BV)
            CP(S, "act", stg_v["vb"][0][:, j, :], bk[:, 0:BW], [bb], [stg_v["vb"][1]])
            tr3(ma_t, ma_b, stg_T["mixa"], j, "dve")
            bk, bb = proj_tm(hT_t, hT_b, j, P_CV)
            CP(S, "act", stg_v["vc"][0][:, j, :], bk[:, 0:BW], [bb], [stg_v["vc"][1]])
            tr3(qc_n[0], qc_n[1], stg_T["qct"], j, "dve")
            tr3(kc_n[0], kc_n[1], stg_T["kct"], j, "act")
        for nm, dst, r0 in (("mixa", D.mixt, 0), ("qbt", D.qbt, 0), ("kbt", D.kbt, 0), ("qct", D.qct, 0), ("kct", D.kct, 0)):
            st_t, st_b = stg_T[nm]
            DMA(S, dst[r0:r0 + BW, tok0:tok0 + 512].rearrange("(c p) t -> p c t", p=128), st_t[:], [st_b], [])
        for nm, dst in (("vb", D.vb), ("vc", D.vc)):
            st_t, st_b = stg_v[nm]
            DMA(S, dst[tok0:tok0 + 512, :].rearrange("(j p) c -> p j c", p=128), st_t[:], [st_b], [])

        def proj_fm(piece, ch):
            bk, bb = P.next()
            c0 = piece * BW + ch * 128
            MMG(S, [(bk[:, 0:512], wb[:, kc, c0:c0 + 128], hT_t[:, kc, :], kc == 0, kc == 7) for kc in range(8)],
                [hT_b, wbuf], [bb])
            return bk, bb

        for piece, dst in ((P_BG, D.gbt), (P_CG, D.gct)):
            for ch in range(3):
                bk, bb = proj_fm(piece, ch)
                f_t, f_b = nxt(fm_b, "fmb")
                ACT(S, f_t[:], bk[:, 0:512], AF.Silu, [bb], [f_b])
                DMA(S, dst[ch * 128:(ch + 1) * 128, tok0:tok0 + 512], f_t[:], [f_b], [])
        for ch in range(3):
            bk_i, bb_i = proj_fm(P_DI, ch)
            bk_c, bb_c = proj_fm(P_DC, ch)
            tm_t, tm_b = nxt(tmpf, "tmpf")
            CP(S, "act", tm_t[:], bk_i[:, 0:512], [bb_i], [tm_b])
            f_t, f_b = nxt(fm_f, "fmf")
            TT(S, "dve", f_t[:], bk_c[:, 0:512], tm_t[:], ALU.mult, [bb_c, tm_b], [f_b])
            DMA(S, D.zt[ch * 128:(ch + 1) * 128, tok0:tok0 + 512], f_t[:], [f_b], [])
            bk_g, bb_g = proj_fm(P_DG, ch)
            bk_b, bb_b = proj_fm(P_DB, ch)
            tm_t, tm_b = nxt(tmpf, "tmpf")
            ACT(S, tm_t[:], bk_g[:, 0:512], AF.Silu, [bb_g], [tm_b])
            f_t, f_b = nxt(fm_f, "fmf")
            TT(S, "dve", f_t[:], bk_b[:, 0:512], tm_t[:], ALU.mult, [bb_b, tm_b], [f_b])
            DMA(S, D.gdt[ch * 128:(ch + 1) * 128, tok0:tok0 + 512], f_t[:], [f_b], [])


def load_head(C, S, slot, qsrc, ksrc, vsrc, h, s0, L):
    (q_t, k_t, v_t, hb) = slot
    for r in range(2):
        DMA(S, q_t[64 * r:64 * r + 64, 0:L], qsrc[h * 64:(h + 1) * 64, s0:s0 + L], [], [hb])
        DMA(S, k_t[64 * r:64 * r + 64, 0:L], ksrc[h * 64:(h + 1) * 64, s0:s0 + L], [], [hb])
    nb = L // 128
    for b0 in range(0, nb, 16):
        b1 = min(nb, b0 + 16)
        DMA(S, v_t[:, b0:b1, 0:64],
            vsrc[s0 + b0 * 128:s0 + b1 * 128, h * 64:(h + 1) * 64].rearrange("(b p) e -> p b e", p=128), [], [hb])


def alloc_attn(C, S, sb, maxL):
    A = Ctx()
    A.slots = []
    for i in range(2):
        q_t = sb(f"aq{i}", [128, maxL], BF16)
        k_t = sb(f"ak{i}", [128, maxL], BF16)
        v_t = sb(f"av{i}", [128, maxL // 128, 65], BF16)
        hb = Buf()
        S.emit("pool", lambda e, v_t=v_t: e.memset(v_t[:, :, 64:65], 1.0), [], [hb])
        A.slots.append((q_t, k_t, v_t, hb))
    A.pT = [(sb(f"pT{i}", [128, 512], BF16), Buf()) for i in range(8)]
    A.ex = [(sb(f"ex{i}", [128, 512], F32), Buf()) for i in range(4)]
    A.npT = 0
    A.nex = 0
    return A


def pe_warmup(S, P, lhsT, rhs, n):
    bk, bb = P.next()
    MMG(S, [(bk[:, 0:512], lhsT, rhs, True, True) for _ in range(n)], [], [bb])


class Pipe:
    def __init__(self, look):
        self.look = look
        self.t = 0
        self.n = 0
        self.pend = []

    def at(self, delay, fn):
        self.pend.append((self.t + delay, self.n, fn))
        self.n += 1

    def tick(self):
        self.t += 1
        while True:
            self.pend.sort(key=lambda p: (p[0], p[1]))
            if not self.pend or self.pend[0][0] > self.t:
                break
            self.pend.pop(0)[2]()

    def block(self, qk_fn, pv_fn):
        qk_fn()
        self.at(self.look + 1, pv_fn)
        self.tick()

    def flush(self):
        while self.pend:
            self.tick()


def phase2(C, S, sb, W, D, l, xsrc, ydst, seqs, NT):
    nc, P = C.nc, C.P
    build_consts(C, S, sb)
    cb = C.constbuf
    maxL = max(L for _, L in seqs)
    A = alloc_attn(C, S, sb, maxL)
    mtab = sb("mtab", [128, TW], F32)
    mt2 = sb("mt2", [128, TW], F32)
    mi = sb("mi", [128, TW], I32)

    def le(out, lim):
        TS(S, "dve", out, C.absd[:], -1.0, ALU.mult, [cb], [cb], s2=lim + 1.0, op1=ALU.add)
        TS(S, "dve", out, out, 0.0, ALU.max, [cb], [cb], s2=1.0, op1=ALU.min)
    le(mtab[:], 64.0)
    mt3 = C.negd
    for dil, lim in ((4, 256.0), (16, 1024.0)):
        TS(S, "dve", mt2[:], C.absd[:], 1.0 / dil, ALU.mult, [cb], [cb])
        CP(S, "dve", mi[:], mt2[:], [cb], [cb])
        CP(S, "dve", mt3[:], mi[:], [cb], [cb])
        TT(S, "dve", mt2[:], mt2[:], mt3[:], ALU.is_equal, [cb], [cb])
        le(mt3[:], lim)
        TT(S, "dve", mt2[:], mt2[:], mt3[:], ALU.mult, [cb], [cb])
        TT(S, "dve", mtab[:], mtab[:], mt2[:], ALU.add, [cb], [cb])
    rtab = [(sb(f"rtab{i}", [128, TW], F32), Buf()) for i in range(2)]
    gate = [(sb(f"gate{i}", [64, 512], BF16), Buf()) for i in range(3)]
    rc = [(sb(f"rc{i}", [128, 512], F32), Buf()) for i in range(2)]
    tq = [(sb(f"tq{i}", [64, 512], F32), Buf()) for i in range(2)]
    ob = [(sb(f"ob{i}", [64, 512], BF16), Buf()) for i in range(2)]
    P.set_rot([0, 1, 2, 3, 6, 7])
    accs = [P.banks[4], P.banks[5]]
    pipe = Pipe(2)
    units = [(s0, L, h) for (s0, L) in seqs for h in range(NHEAD)]
    load_head(C, S, A.slots[0], D.qbt, D.kbt, D.vb, units[0][2], units[0][0], units[0][1])
    nq = 0
    nblkc = [0]
    for ui, (s0, L, h) in enumerate(units):
        slot = A.slots[ui % 2]
        q_t, k_t, v_t, hb = slot
        r_t, r_b = rtab[ui % 2]
        ACT(S, r_t[:], C.absd[:], AF.Exp, [cb], [r_b], scale=-SLOPES[h])
        TT(S, "pool", r_t[:], r_t[:], mtab[:], ALU.mult, [r_b, cb], [r_b])
        nblk = L // 128
        for qi in range(L // 512):
            q0 = qi * 512
            acc, accb = accs[nq % 2]
            g_t, g_b = gate[nq % 3]
            DMA(S, g_t[:], D.gbt[h * 64:(h + 1) * 64, s0 + q0:s0 + q0 + 512], [], [g_b])
            kbs = list(range(max(0, q0 // 128 - 8), min(nblk, q0 // 128 + 4 + 8)))
            dmax_b = SKIP_T / SLOPES[h]
            kbs = [kb for kb in kbs if max(q0 - kb * 128 - 127, kb * 128 - q0 - 511, 0) <= dmax_b]
            rc_t, rc_b = rc[nq % 2]
            t_t, t_b = tq[nq % 2]
            o_t, o_b = ob[nq % 2]
            dst = D.mixt[BW + h * 64:BW + (h + 1) * 64, s0 + q0:s0 + q0 + 512]

            def epiA(acc=acc, accb=accb, rc_t=rc_t, rc_b=rc_b):
                ACT(S, rc_t[64:65, :], acc[64:65, 0:512], AF.Ln, [accb], [rc_b])
                ACT(S, rc_t[64:65, :], rc_t[64:65, :], AF.Exp, [rc_b], [rc_b], scale=-1.0)

            def epiB(acc=acc, accb=accb, rc_t=rc_t, rc_b=rc_b, t_t=t_t, t_b=t_b, o_t=o_t, o_b=o_b,
                     g_t=g_t, g_b=g_b, dst=dst):
                bc, bcb = P.next()
                MMG(S, [(bc[0:64, 0:512], C.ones_f[64:65, 0:64], rc_t[64:65, :], True, True)], [rc_b, cb], [bcb])
                TT(S, "dve", t_t[:], acc[0:64, 0:512], g_t[:], ALU.mult, [accb, g_b], [t_b])
                TT(S, "dve", o_t[:], bc[0:64, 0:512], t_t[:], ALU.mult, [bcb, t_b], [o_b])
                DMA(S, dst, o_t[:], [o_b], [])

            sbl = [kbs[i:i + 2] for i in range(0, len(kbs), 2)]
            for si, grp in enumerate(sbl):
                pts = []
                for _ in grp:
                    pts.append(A.pT[A.npT % len(A.pT)])
                    A.npT += 1
                first, last = (si == 0), (si == len(sbl) - 1)

                def qk(grp=grp, pts=pts, q0=q0, k_t=k_t, q_t=q_t, hb=hb, r_t=r_t, r_b=r_b):
                    bks = [P.next() for _ in grp]
                    MMG(S, [(bks[j][0][:, 0:512], k_t[64 * j:64 * j + 64, kb * 128:(kb + 1) * 128],
                             q_t[64 * j:64 * j + 64, q0:q0 + 512], True, True) for j, kb in enumerate(grp)],
                        [hb], [bb for _, bb in bks])
                    for j, kb in enumerate(grp):
                        o = kb * 128 - q0
                        bk, bb = bks[j]
                        p_t, p_b = pts[j]
                        ex_t, ex_b = A.ex[A.nex % len(A.ex)]
                        A.nex += 1
                        ACT(S, ex_t[:], bk[:, 0:512], AF.Exp, [bb], [ex_b])
                        eng = "pool" if nblkc[0] % 3 == 2 else "dve"
                        nblkc[0] += 1
                        TT(S, eng, p_t[:], ex_t[:], r_t[:, TC - o:TC - o + 512], ALU.mult, [ex_b, r_b], [p_b])

                def pv(grp=grp, pts=pts, first=first, last=last, acc=acc, accb=accb, v_t=v_t, hb=hb,
                       epiA=epiA, epiB=epiB):
                    MMG(S, [(acc[0:65, 0:512], v_t[:, kb, :], pts[j][0][:], first and j == 0, last and j == len(grp) - 1)
                            for j, kb in enumerate(grp)], [hb] + [p_b for _, p_b in pts], [accb])
                    if last:
                        pipe.at(2, epiA)
                        pipe.at(4, epiB)
                pipe.block(qk, pv)
            nq += 1
            if qi == 0 and ui + 1 < len(units):
                n0, nL, nh = units[ui + 1]
                load_head(C, S, A.slots[(ui + 1) % 2], D.qbt, D.kbt, D.vb, nh, n0, nL)
    pipe.flush()


def load_head_c(C, S, slot, D, h, s0, L, sl, JLf, cb):
    (q_t, k_t, v_t, hb, kxb, pat, patf) = slot
    for c in range(2):
        r0 = h * 64 + c * 32
        DMA(S, q_t[64 * c + 3:64 * c + 35, 0:L], D.qct[r0:r0 + 32, s0:s0 + L], [], [hb])
        DMA(S, k_t[64 * c + 3:64 * c + 35, 0:L], D.kct[r0:r0 + 32, s0:s0 + L], [], [hb])
    nb = L // 128
    for b0 in range(0, nb, 16):
        b1 = min(nb, b0 + 16)
        DMA(S, v_t[:, b0:b1, 0:64],
            D.vc[s0 + b0 * 128:s0 + b1 * 128, h * 64:(h + 1) * 64].rearrange("(b p) e -> p b e", p=128), [], [hb])
    pb = Buf()
    TS(S, "dve", patf[0:1, 0, :], JLf[0:1, :], sl, ALU.mult, [cb], [pb])
    CP(S, "dve", pat[0:1, 0, :], patf[0:1, 0, :], [pb], [pb])
    CP(S, "dve", patf[0:1, 1, :], pat[0:1, 0, :], [pb], [pb])
    TT(S, "dve", patf[0:1, 2, :], patf[0:1, 0, :], patf[0:1, 1, :], ALU.subtract, [pb], [pb])
    CP(S, "dve", pat[0:1, 1, :], patf[0:1, 2, :], [pb], [pb])
    CP(S, "dve", patf[0:1, 1, :], pat[0:1, 1, :], [pb], [pb])
    TT(S, "dve", patf[0:1, 0, :], patf[0:1, 2, :], patf[0:1, 1, :], ALU.subtract, [pb], [pb])
    CP(S, "dve", pat[0:1, 2, :], patf[0:1, 0, :], [pb], [pb])
    nt = L // 512
    for c in range(2):
        for r in range(3):
            DMA(S, q_t[64 * c + r:64 * c + r + 1, 0:L].rearrange("p (t j) -> p t j", j=512),
                pat[0:1, r, :].unsqueeze(1).to_broadcast([1, nt, 512]), [pb], [hb])
        S.emit("dve", lambda e, c=c: e.memset(k_t[64 * c:64 * c + 3, 0:512], 0.0), [], [kxb[0]])
        if L > 512:
            S.emit("dve", lambda e, c=c: e.memset(k_t[64 * c:64 * c + 3, 512:L], 1.0), [], list(kxb[1:L // 512]))


def phase3(C, S, sb, W, D, l, xsrc, ydst, seqs, NT):
    nc, P = C.nc, C.P
    build_consts(C, S, sb)
    cb = C.constbuf
    maxL = max(L for _, L in seqs)
    NB = maxL // 128
    lambda_init = 0.8 - 0.6 * math.exp(-0.3 * l)
    A = Ctx()
    A.slots = []
    for i in range(2):
        q_t = sb(f"aq{i}", [128, maxL], BF16)
        k_t = sb(f"ak{i}", [128, maxL], BF16)
        v_t = sb(f"av{i}", [128, maxL // 128, VPAD], BF16)
        pat = sb(f"pat{i}", [1, 3, 512], BF16)
        patf = sb(f"patf{i}", [1, 3, 512], F32)
        hb = Buf()
        kxb = [Buf() for _ in range(maxL // 512)]
        S.emit("pool", lambda e, v_t=v_t: e.memset(v_t[:, :, 64:VPAD], 1.0), [], [hb])
        A.slots.append((q_t, k_t, v_t, hb, kxb, pat, patf))
    A.pT = [(sb(f"pT{i}", [128, 512], BF16), Buf()) for i in range(10)]
    A.ex = [(sb(f"ex{i}", [128, 512], F32), Buf()) for i in range(4)]
    A.npT = 0
    A.nex = 0
    tli = sb("tli", [128, NB + 1], I32)
    tri = sb("tri", [128, NB + 1], I32)
    jli = sb("jli", [128, 512], I32)
    TLf = sb("TLf", [128, NB + 1], F32)
    TRf = sb("TRf", [128, NB + 1], F32)
    JLf = sb("JLf", [128, 512], F32)
    S.emit("pool", lambda e: e.iota(tli[:], [[128, NB + 1]], base=0, channel_multiplier=-1), [], [cb])
    S.emit("pool", lambda e: e.iota(tri[:], [[128, NB + 1]], base=0, channel_multiplier=1), [], [cb])
    S.emit("pool", lambda e: e.iota(jli[:], [[1, 512]], base=0, channel_multiplier=0), [], [cb])
    for a, b in ((TLf, tli), (TRf, tri), (JLf, jli)):
        CP(S, "dve", a[:], b[:], [cb], [cb])
    lamv = sb("lamv", [64, 4, 32], F32)
    lams = sb("lams", [64, 8], F32)
    lamj = sb("lamj", [64, 32], F32)
    for i, src in enumerate(W.lam):
        DMA(S, lamv[:, i, :], src[l:l + 1, :].to_broadcast([64, 32]), [], [cb])
    for i in range(2):
        TT(S, "dve", lamj[:], lamv[:, 2 * i, :], lamv[:, 2 * i + 1, :], ALU.mult, [cb], [cb])
        S.emit("dve", lambda e, i=i: e.tensor_reduce(out=lams[:, i:i + 1], in_=lamj[:], axis=AX.X, op=ALU.add), [cb], [cb])
    ACT(S, lams[:, 2:4], lams[:, 0:2], AF.Exp, [cb], [cb])
    TT(S, "dve", lams[:, 4:5], lams[:, 3:4], lams[:, 2:3], ALU.subtract, [cb], [cb])
    TS(S, "dve", lams[:, 5:6], lams[:, 4:5], -lambda_init, ALU.add, [cb], [cb])
    subg = sb("subg", [64, 2], F32)
    DMA(S, subg[:, 0:1], W.subln_g[l:l + 1, :].rearrange("o e -> e o"), [], [cb], slow=True)
    TS(S, "dve", subg[:, 1:2], subg[:, 0:1], 1.0 - lambda_init, ALU.mult, [cb], [cb])
    neglam = lams[:, 5:6]

    rtab = [(sb(f"rtab{i}", [128, 896], F32), Buf()) for i in range(2)]
    bl = [sb(f"bl{i}", [128, NB + 1], F32) for i in range(2)]
    br = [sb(f"br{i}", [128, NB + 1], F32) for i in range(2)]
    gate = [(sb(f"gate{i}", [64, 512], BF16), Buf()) for i in range(3)]
    ocs = [(sb(f"ocs{i}", [65, 2, 512], F32), Buf()) for i in range(2)]
    rc = [(sb(f"rc{i}", [128, 2, 512], F32), Buf()) for i in range(2)]
    a01 = [(sb(f"a01{i}", [64, 2, 512], F32), Buf()) for i in range(2)]
    at = [(sb(f"at{i}", [64, 512], F32), Buf()) for i in range(2)]
    sqb = [(sb(f"sqb{i}", [64, 512], F32), Buf()) for i in range(2)]
    rs = [(sb(f"rs{i}", [64, 512], F32), Buf()) for i in range(2)]
    ob = [(sb(f"ob{i}", [64, 512], BF16), Buf()) for i in range(2)]
    P.set_rot([0, 1, 2, 3, 6, 7])
    accs = [(P.banks[4], P.banks[5]), (P.banks[4], P.banks[5])]
    pipe = Pipe(2)
    units = [(s0, L, h) for (s0, L) in seqs for h in range(NHEAD)]
    load_head_c(C, S, A.slots[0], D, units[0][2], units[0][0], units[0][1], SLOPES[units[0][2]], JLf, cb)
    nq = 0
    for ui, (s0, L, h) in enumerate(units):
        q_t, k_t, v_t, hb, kxb, pat, patf = A.slots[ui % 2]
        sl = SLOPES[h]
        r_t, r_b = rtab[ui % 2]
        ACT(S, r_t[:], C.absd[:, TC - 384:TC - 384 + 896], AF.Exp, [cb], [r_b], scale=-sl)
        bl_t, br_t = bl[ui % 2], br[ui % 2]
        TS(S, "dve", bl_t[:], TLf[:], -sl, ALU.mult, [cb], [r_b])
        TS(S, "dve", br_t[:], TRf[:], -sl, ALU.mult, [cb], [r_b])
        nblk = L // 128
        dmax = SKIP_T / sl
        if WARM_N:
            pe_warmup(S, P, C.identb[:], A.pT[0][0][:], WARM_N)
        for qi in range(L // 512):
            q0 = qi * 512
            def next_signs(qn=qi + 1, k_t=k_t, kxb=kxb):
                for c in range(2):
                    S.emit("dve", lambda e, c=c: e.memset(k_t[64 * c:64 * c + 3, (qn - 1) * 512:qn * 512], -1.0),
                           [], [kxb[qn - 1]])
                    S.emit("dve", lambda e, c=c: e.memset(k_t[64 * c:64 * c + 3, qn * 512:(qn + 1) * 512], 0.0),
                           [], [kxb[qn]])
            g_t, g_b = gate[nq % 3]
            DMA(S, g_t[:], D.gct[h * 64:(h + 1) * 64, s0 + q0:s0 + q0 + 512], [], [g_b])
            oc_t, oc_b = ocs[nq % 2]
            rc_t, rc_b = rc[nq % 2]
            a_t, a_b = a01[nq % 2]
            at_t, at_b = at[nq % 2]
            sq_t, sq_b = sqb[nq % 2]
            rs_t, rs_b = rs[nq % 2]
            o_t, o_b = ob[nq % 2]
            acc2 = accs[nq % 2]
            dstm = D.mixt[2 * BW + h * 64:2 * BW + (h + 1) * 64, s0 + q0:s0 + q0 + 512]
            lefts = [kb for kb in range(nblk) if kb * 128 - q0 <= -128 and (q0 - kb * 128 - 127) <= dmax]
            diags = [kb for kb in range(nblk) if 0 <= kb * 128 - q0 < 512]
            rights = [kb for kb in range(nblk) if kb * 128 - q0 >= 512 and (kb * 128 - q0 - 511) <= dmax]
            sbs = [("D", kb) for kb in diags] + [("R", kb) for kb in rights] + [("L", kb) for kb in lefts]
            sign_at = min(len(sbs) - 1, 9)

            def epi0(oc_t=oc_t, oc_b=oc_b, acc2=acc2):
                for c in range(2):
                    CP(S, "dve", oc_t[:, c, :], acc2[c][0][0:65, 0:512], [acc2[c][1]], [oc_b])

            def epiA(oc_t=oc_t, oc_b=oc_b, rc_t=rc_t, rc_b=rc_b):
                ACT(S, rc_t[64:65, :, :], oc_t[64:65, :, :], AF.Ln, [oc_b], [rc_b])
                ACT(S, rc_t[64:65, :, :], rc_t[64:65, :, :], AF.Exp, [rc_b], [rc_b], scale=-1.0)

            def epiB(oc_t=oc_t, oc_b=oc_b, rc_t=rc_t, rc_b=rc_b, a_t=a_t, a_b=a_b, at_t=at_t, at_b=at_b,
                     sq_t=sq_t, sq_b=sq_b):
                for c in range(2):
                    bk, bb = P.next()
                    MMG(S, [(bk[0:64, 0:512], C.ones_f[64:65, 0:64], rc_t[64:65, c, :], True, True)], [rc_b, cb], [bb])
                    TT(S, "dve", a_t[:, c, :], oc_t[0:64, c, :], bk[0:64, 0:512], ALU.mult, [oc_b, bb], [a_b])
                STT(S, at_t[:], a_t[:, 1, :], neglam, a_t[:, 0, :], ALU.mult, ALU.add, [a_b, cb], [at_b])
                TT(S, "dve", sq_t[:], at_t[:], at_t[:], ALU.mult, [at_b], [sq_b])

            def epiC(sq_t=sq_t, sq_b=sq_b, rs_t=rs_t, rs_b=rs_b):
                bk, bb = P.next()
                MMG(S, [(bk[0:64, 0:512], C.ones_f[0:64, 0:64], sq_t[:], True, True)], [sq_b, cb], [bb])
                ACT(S, rs_t[:], bk[0:64, 0:512], AF.Ln, [bb], [rs_b], scale=1.0 / 64, bias=EPS)
                ACT(S, rs_t[:], rs_t[:], AF.Exp, [rs_b], [rs_b], scale=-0.5)

            def epiD(at_t=at_t, at_b=at_b, rs_t=rs_t, rs_b=rs_b, o_t=o_t, o_b=o_b, g_t=g_t, g_b=g_b, dstm=dstm):
                TT(S, "dve", at_t[:], at_t[:], rs_t[:], ALU.mult, [at_b, rs_b], [at_b])
                STT(S, o_t[:], at_t[:], subg[:, 1:2], g_t[:], ALU.mult, ALU.mult, [at_b, g_b, cb], [o_b])
                DMA(S, dstm, o_t[:], [o_b], [])

            for si, (kind, kb) in enumerate(sbs):
                pts = []
                for _ in range(2):
                    pts.append(A.pT[A.npT % len(A.pT)])
                    A.npT += 1
                first, last = (si == 0), (si == len(sbs) - 1)

                def qk(kind=kind, kb=kb, pts=pts, q0=q0, k_t=k_t, q_t=q_t, hb=hb, r_t=r_t, r_b=r_b,
                       bl_t=bl_t, br_t=br_t, kxb=kxb):
                    bks = [P.next() for _ in range(2)]
                    MMG(S, [(bks[c][0][:, 0:512], k_t[64 * c:64 * c + 35, kb * 128:(kb + 1) * 128],
                             q_t[64 * c:64 * c + 35, q0:q0 + 512], True, True) for _ in range(QK_REP) for c in range(2)],
                        [hb, kxb[kb // 4]], [bks[0][1], bks[1][1]])
                    o = kb * 128 - q0
                    for c in range(2):
                        bk, bb = bks[c]
                        p_t, p_b = pts[c]
                        if kind == "L":
                            n = (-o) // 128
                            ACT(S, p_t[:], bk[:, 0:512], AF.Exp, [bb, r_b], [p_b], bias=bl_t[:, n:n + 1])
                        elif kind == "R":
                            n = o // 128
                            ACT(S, p_t[:], bk[:, 0:512], AF.Exp, [bb, r_b], [p_b], bias=br_t[:, n:n + 1])
                        else:
                            ex_t, ex_b = A.ex[A.nex % len(A.ex)]
                            A.nex += 1
                            ACT(S, ex_t[:], bk[:, 0:512], AF.Exp, [bb], [ex_b])
                            TT(S, "pool" if c == 0 else "dve", p_t[:], ex_t[:], r_t[:, 384 - o:384 - o + 512], ALU.mult,
                               [ex_b, r_b], [p_b])

                def pv(kb=kb, pts=pts, first=first, last=last, acc2=acc2, v_t=v_t, hb=hb,
                       epi0=epi0, epiA=epiA, epiB=epiB, epiC=epiC, epiD=epiD):
                    MMG(S, [(acc2[c][0][0:VPAD, 0:512], v_t[:, kb, :], pts[c][0][:], first, last) for c in range(2)],
                        [hb, pts[0][1], pts[1][1]], [acc2[0][1], acc2[1][1]])
                    if last:
                        epi0()
                        pipe.at(2, epiA)
                        pipe.at(5, epiB)
                        pipe.at(8, epiC)
                        pipe.at(11, epiD)
                pipe.block(qk, pv)
                if si == sign_at and qi + 1 < L // 512:
                    next_signs()
                if WARM_Q and si == 2:
                    pe_warmup(S, P, C.identb[:], A.pT[0][0][:], WARM_Q)
            nq += 1
            if qi == 0 and ui + 1 < len(units):
                n0, nL, nh = units[ui + 1]
                load_head_c(C, S, A.slots[(ui + 1) % 2], D, nh, n0, nL, SLOPES[nh], JLf, cb)
    pipe.flush()


def phase4(C, S, sb, W, D, l, xsrc, ydst, seqs, NT):
    nc, P = C.nc, C.P
    P.set_rot(range(8))
    wo = sb("wo", [128, 12, D_MODEL], BF16)
    wbuf = Buf()
    wst = [(sb(f"wost{i}", [128, D_MODEL], F32), Buf()) for i in range(2)]
    for kc in range(12):
        st, stb = wst[kc % 2]
        DMA(S, st[:], W.w_out[l, kc * 128:(kc + 1) * 128, :], [], [stb])
        CP(S, ("dve", "act")[kc % 2], wo[:, kc, :], st[:], [stb], [wbuf])
    cw = sb("cw", [128, 3, 3], F32)
    for ch in range(3):
        DMA(S, cw[:, ch, :], W.conv_w[l, :, ch * 128:(ch + 1) * 128].rearrange("j p -> p j"), [], [wbuf], slow=True)
    mx = [(sb(f"mx{i}", [128, 12, 512], BF16), Buf()) for i in range(2)]
    zt = [(sb(f"zt{i}", [128, 3, 514], F32), Buf()) for i in range(2)]
    gd = [(sb(f"gd{i}", [128, 3, 512], F32), Buf()) for i in range(2)]
    cv = [(sb(f"cv{i}", [128, 3, 512], F32), Buf()) for i in range(2)]
    xr = [(sb(f"xr{i}", [128, D_MODEL], F32), Buf()) for i in range(3)]
    yo = [(sb(f"yo{i}", [128, D_MODEL], F32), Buf()) for i in range(3)]
    starts = {s0 for s0, _ in seqs}
    ends = {s0 + L for s0, L in seqs}
    ngroups = NT // 512
    nt = 0

    def loads(g):
        tok0 = g * 512
        m_t, m_b = mx[g % 2]
        DMA(S, m_t[:, 0:9, :], D.mixt[0:3 * BW, tok0:tok0 + 512].rearrange("(c p) t -> p c t", p=128), [], [m_b])
        z_t, z_b = zt[g % 2]
        lo = 0 if tok0 in starts else 1
        hi = 0 if (tok0 + 512) in ends else 1
        if lo == 0:
            S.emit("pool", lambda e: e.memset(z_t[:, :, 0:1], 0.0), [], [z_b])
        if hi == 0:
            S.emit("pool", lambda e: e.memset(z_t[:, :, 513:514], 0.0), [], [z_b])
        DMA(S, z_t[:, :, 1 - lo:513 + hi],
            D.zt[:, tok0 - lo:tok0 + 512 + hi].rearrange("(c p) t -> p c t", p=128), [], [z_b])
        g_t, g_b = gd[g % 2]
        DMA(S, g_t[:], D.gdt[:, tok0:tok0 + 512].rearrange("(c p) t -> p c t", p=128), [], [g_b])

    loads(0)
    for g in range(ngroups):
        tok0 = g * 512
        if g + 1 < ngroups:
            loads(g + 1)
        m_t, m_b = mx[g % 2]
        z_t, z_b = zt[g % 2]
        g_t, g_b = gd[g % 2]
        c_t, c_b = cv[g % 2]
        for ch in range(3):
            TS(S, "dve", c_t[:, ch, :], z_t[:, ch, 0:512], cw[:, ch, 0:1], ALU.mult, [z_b, wbuf], [c_b])
            STT(S, c_t[:, ch, :], z_t[:, ch, 1:513], cw[:, ch, 1:2], c_t[:, ch, :], ALU.mult, ALU.add, [z_b, c_b, wbuf], [c_b])
            STT(S, c_t[:, ch, :], z_t[:, ch, 2:514], cw[:, ch, 2:3], c_t[:, ch, :], ALU.mult, ALU.add, [z_b, c_b, wbuf], [c_b])
            TT(S, "pool", m_t[:, 9 + ch, :], c_t[:, ch, :], g_t[:, ch, :], ALU.mult, [c_b, g_b], [m_b])
        for j in range(4):
            t0 = tok0 + j * 128
            x_t, x_b = xr[nt % 3]
            DMA(S, x_t[:], xsrc[t0:t0 + 128, :], [], [x_b])
            y_t, y_b = yo[nt % 3]
            for half in range(2):
                bk, bb = P.next()
                MMG(S, [(bk[:, 0:512], m_t[:, kc, j * 128:(j + 1) * 128], wo[:, kc, half * 512:(half + 1) * 512],
                         kc == 0, kc == 11) for kc in range(12)], [m_b, wbuf], [bb])
                TT(S, "dve", y_t[:, half * 512:(half + 1) * 512], bk[:, 0:512], x_t[:, half * 512:(half + 1) * 512],
                   ALU.add, [bb, x_b], [y_b])
            DMA(S, ydst[t0:t0 + 128, :], y_t[:], [y_b], [])
            nt += 1


_NC_CACHE = {}
SEQS = (8192, 4096, 4096)
WNAMES = ("norm_g", "w_in", "sgu_g", "w_s", "b_s", "qn_b", "kn_b", "qn_c", "kn_c",
          "lam_q1", "lam_k1", "lam_q2", "lam_k2", "subln_g", "conv_w", "w_out")


def kernel(x_prompt, x_sample, **w):
    x_prompt = np.asarray(x_prompt, dtype=np.float32)
    x_sample = np.asarray(x_sample, dtype=np.float32)
    if "nc" not in _NC_CACHE:
        _NC_CACHE["nc"] = build_program(SEQS)
    nc = _NC_CACHE["nc"]
    wmap = {k: np.ascontiguousarray(np.asarray(w[k], dtype=np.float32)) for k in WNAMES}
    in_maps = []
    for c in range(N_CORES):
        xc = np.concatenate([x_prompt[c], x_sample[2 * c], x_sample[2 * c + 1]], axis=0)
        m = {"x": np.ascontiguousarray(xc)}
        m.update(wmap)
        in_maps.append(m)
    res = run_bass_kernel_spmd(nc, in_maps, core_ids=list(range(N_CORES)))
    yp = np.empty_like(x_prompt)
    ys = np.empty_like(x_sample)
    for c in range(N_CORES):
        y = res.results[c]["y"]
        yp[c] = y[0:8192]
        ys[2 * c] = y[8192:12288]
        ys[2 * c + 1] = y[12288:16384]
    return (yp, ys)
```

```python
import math
from contextlib import ExitStack
import numpy as np
import concourse.bass as bass
import concourse.mybir as mybir
from concourse.bass_utils import run_bass_kernel_spmd

F32 = mybir.dt.float32
BF16 = mybir.dt.bfloat16
I32 = mybir.dt.int32
AF = mybir.ActivationFunctionType
ALU = mybir.AluOpType
AX = mybir.AxisListType

D_MODEL = 1024
DEPTH = 2
BW = 384
PROJ_W = 15 * BW
MIX_W = 4 * BW
EPS = 1e-6
NHEAD = 6
SLOPES = [2.0 ** (-8.0 * (i + 1) / NHEAD) for i in range(NHEAD)]
N_CORES = 8
(P_AU, P_AV, P_AG, P_BQ, P_BK, P_BV, P_BG, P_CQ, P_CK, P_CV, P_CG, P_DI, P_DB, P_DC, P_DG) = range(15)
TC = 1408
TW = 2944
SKIP_T = 60.0
VPAD = 128
KPAD = 64
WARM_N = 0
QK_REP = 1
WARM_Q = 0
SBK = 2


class Buf:
    __slots__ = ("w", "r")

    def __init__(self):
        self.w = None
        self.r = {}


class Sched:
    ENG = ("pe", "act", "dve", "pool", "sp")

    def __init__(self, n_dma=24):
        self.cnt = {e: 0 for e in ("pe", "act", "dve", "pool")}
        self.dma_val = [0] * n_dma
        self.dma_rr = 0
        self.seen = {e: {} for e in self.ENG}
        self.ops = {e: [] for e in self.ENG}

    def new_phase(self):
        self.ops = {e: [] for e in self.ENG}

    def emit(self, eng, fn, reads=(), writes=(), dma=False):
        waits = {}
        seen = self.seen[eng]

        def add(ev, raw):
            if ev is None:
                return
            k, val = ev
            if k[0] == 'e' and k[1] == eng:
                if not (raw and eng in ("act", "dve", "pool")):
                    return
            if seen.get(k, 0) >= val:
                return
            if waits.get(k, 0) < val:
                waits[k] = val

        for b in reads:
            add(b.w, True)
        for b in writes:
            add(b.w, False)
            for k, v in b.r.items():
                add((k, v), False)
        if dma:
            s = self.dma_rr
            self.dma_rr = (s + 1) % len(self.dma_val)
            prev = self.dma_val[s]
            if prev > 0:
                add((('d', s), prev), False)
            self.dma_val[s] = prev + 16
            ev = (('d', s), prev + 16)
        else:
            self.cnt[eng] += 1
            ev = (('e', eng), self.cnt[eng])
        for k, v in waits.items():
            seen[k] = v
        self.ops[eng].append((fn, list(waits.items()), ev))
        for b in reads:
            if b.r.get(ev[0], 0) < ev[1]:
                b.r[ev[0]] = ev[1]
        for b in writes:
            b.w = ev
            b.r = {}
        return ev

    def barrier(self):
        allev = [(('e', e), c) for e, c in self.cnt.items() if c > 0]
        allev += [(('d', s), v) for s, v in enumerate(self.dma_val) if v > 0]
        for eng in self.ENG:
            waits = {}
            for k, v in allev:
                if k[0] == 'e' and k[1] == eng:
                    continue
                if self.seen[eng].get(k, 0) >= v:
                    continue
                waits[k] = v
                self.seen[eng][k] = v
            self.ops[eng].append((None, list(waits.items()), None))

    def replay(self, nc, sems):
        with nc.Block() as block:
            def run(eng, e):
                for fn, waits, ev in self.ops[eng]:
                    for k, v in waits:
                        e.wait_ge(sems[k], v)
                    if fn is None:
                        continue
                    ins = fn(e)
                    if ev[0][0] == 'd':
                        ins.then_inc(sems[ev[0]], 16)
                    else:
                        ins.then_inc(sems[ev[0]], 1)

            @block.tensor
            def _(e):
                run("pe", e)

            @block.scalar
            def _(e):
                run("act", e)

            @block.vector
            def _(e):
                run("dve", e)

            @block.gpsimd
            def _(e):
                run("pool", e)

            @block.sync
            def _(e):
                run("sp", e)


def ACT(S, out, in_, func, R, W, bias=None, scale=None, accum_out=None):
    kw = {}
    if bias is not None:
        kw["bias"] = bias
    if scale is not None:
        kw["scale"] = scale
    if accum_out is not None:
        kw["accum_out"] = accum_out
    S.emit("act", lambda e: e.activation(out=out, in_=in_, func=func, **kw), R, W)


def TT(S, eng, out, in0, in1, op, R, W):
    S.emit(eng, lambda e: e.tensor_tensor(out=out, in0=in0, in1=in1, op=op), R, W)


def TS(S, eng, out, in0, s1, op0, R, W, s2=None, op1=None):
    if op1 is None:
        S.emit(eng, lambda e: e.tensor_scalar(out=out, in0=in0, scalar1=s1, scalar2=None, op0=op0), R, W)
    else:
        S.emit(eng, lambda e: e.tensor_scalar(out=out, in0=in0, scalar1=s1, scalar2=s2, op0=op0, op1=op1), R, W)


def STT(S, out, in0, scalar, in1, op0, op1, R, W):
    S.emit("dve", lambda e: e.scalar_tensor_tensor(out=out, in0=in0, scalar=scalar, in1=in1, op0=op0, op1=op1), R, W)


def CP(S, eng, out, in_, R, W):
    if eng == "act":
        S.emit(eng, lambda e: e.copy(out=out, in_=in_), R, W)
    else:
        S.emit(eng, lambda e: e.tensor_copy(out=out, in_=in_), R, W)


def MMG(S, mms, R, W):
    def fn(e):
        ins = None
        for (out, lhsT, rhs, st, sp) in mms:
            ins = e.matmul(out, lhsT=lhsT, rhs=rhs, start=st, stop=sp)
        return ins
    S.emit("pe", fn, R, W)


def TRG(S, trs, R, W):
    def fn(e):
        ins = None
        for (out, in_, ident) in trs:
            ins = e.transpose(out=out, in_=in_, identity=ident)
        return ins
    S.emit("pe", fn, R, W)


def DMA(S, out, in_, R, W, slow=False):
    if slow:
        S.emit("sp", lambda e: e.dma_start(out=out, in_=in_, allow_slow_non_contiguous=True), R, W, dma=True)
    else:
        S.emit("sp", lambda e: e.dma_start(out=out, in_=in_), R, W, dma=True)


class PS:
    def __init__(self, banks):
        self.banks = banks
        self.rot = list(range(8))
        self.i = 0

    def set_rot(self, idxs):
        self.rot = list(idxs)
        self.i = 0

    def next(self):
        b = self.banks[self.rot[self.i % len(self.rot)]]
        self.i += 1
        return b


class Ctx:
    pass


_uid = [0]


def mk_sb(nc, es):
    def sb(name, shape, dt):
        _uid[0] += 1
        return es.enter_context(nc.sbuf_tensor(f"{name}_{_uid[0]}", shape, dt))
    return sb


def build_consts(C, S, sb, full=True, lo=0, width=TW):
    nc = C.nc
    b = Buf()
    C.constbuf = b
    C.identb = sb("identb", [128, 128], BF16)
    C.identf = sb("identf", [128, 128], F32)
    C.ones_f = sb("ones_f", [128, 64], F32)
    S.emit("pool", lambda e: e.memset(C.ones_f[:], 1.0), [], [b])
    if not full:
        iot = sb("iot", [128, 128], I32)
        S.emit("pool", lambda e: e.iota(iot[:], [[-1, 128]], base=0, channel_multiplier=1), [], [b])
        TS(S, "dve", C.identb[:], iot[:], 0.0, ALU.is_equal, [b], [b])
        TS(S, "dve", C.identf[:], iot[:], 0.0, ALU.is_equal, [b], [b])
        return
    C.absd = sb("absd", [128, width], F32)
    iot = sb("iot", [128, width], I32)
    S.emit("pool", lambda e: e.iota(iot[:], [[-1, width]], base=TC - lo, channel_multiplier=1), [], [b])
    TS(S, "dve", C.identb[:], iot[:, TC - lo:TC - lo + 128], 0.0, ALU.is_equal, [b], [b])
    TS(S, "dve", C.identf[:], iot[:, TC - lo:TC - lo + 128], 0.0, ALU.is_equal, [b], [b])
    C.negd = sb("negd", [128, width], F32)
    CP(S, "dve", C.negd[:], iot[:], [b], [b])
    TS(S, "dve", C.absd[:], C.negd[:], -1.0, ALU.mult, [b], [b])
    TT(S, "dve", C.absd[:], C.absd[:], C.negd[:], ALU.max, [b], [b])
    C.iot = iot


def build_program(seq_lens, dump=False, depth=DEPTH, phases=4):
    nc = bass.Bass("TRN2", target_bir_lowering=False)
    NT = sum(seq_lens)
    seqs = []
    o = 0
    for L in seq_lens:
        seqs.append((o, L))
        o += L
    C = Ctx()
    C.nc = nc
    ext_in = lambda name, shape: nc.dram_tensor(name, shape, F32, kind="ExternalInput").ap()
    x_in = ext_in("x", [NT, D_MODEL])
    norm_g = ext_in("norm_g", [DEPTH, D_MODEL])
    w_in = ext_in("w_in", [DEPTH, D_MODEL, PROJ_W])
    sgu_g = ext_in("sgu_g", [DEPTH, BW])
    w_s = ext_in("w_s", [DEPTH, NHEAD, 128, 128])
    b_s = ext_in("b_s", [DEPTH, NHEAD, 128])
    qn_b = ext_in("qn_b", [DEPTH, 64])
    kn_b = ext_in("kn_b", [DEPTH, 64])
    qn_c = ext_in("qn_c", [DEPTH, 32])
    kn_c = ext_in("kn_c", [DEPTH, 32])
    lam_q1 = ext_in("lam_q1", [DEPTH, 32])
    lam_k1 = ext_in("lam_k1", [DEPTH, 32])
    lam_q2 = ext_in("lam_q2", [DEPTH, 32])
    lam_k2 = ext_in("lam_k2", [DEPTH, 32])
    subln_g = ext_in("subln_g", [DEPTH, 64])
    conv_w = ext_in("conv_w", [DEPTH, 3, BW])
    w_out = ext_in("w_out", [DEPTH, MIX_W, D_MODEL])
    y_out = nc.dram_tensor("y", [NT, D_MODEL], F32, kind="ExternalOutput").ap()

    skind = "ExternalOutput" if dump else "Internal"
    scr = lambda name, shape, dt: nc.dram_tensor(name, shape, dt, kind=skind).ap()
    D = Ctx()
    D.x1 = scr("x1", [NT, D_MODEL], F32)
    D.qbt = scr("qbt", [BW, NT], BF16)
    D.kbt = scr("kbt", [BW, NT], BF16)
    D.vb = scr("vb", [NT, BW], BF16)
    D.gbt = scr("gbt", [BW, NT], BF16)
    D.qct = scr("qct", [BW, NT], BF16)
    D.kct = scr("kct", [BW, NT], BF16)
    D.vc = scr("vc", [NT, BW], BF16)
    D.gct = scr("gct", [BW, NT], BF16)
    D.zt = scr("zt", [BW, NT], F32)
    D.gdt = scr("gdt", [BW, NT], F32)
    D.mixt = scr("mixt", [MIX_W, NT], BF16)

    S = Sched()
    with ExitStack() as top:
        sems = {}
        for e in ("pe", "act", "dve", "pool"):
            sems[('e', e)] = top.enter_context(nc.semaphore(f"sem_{e}"))
        for s in range(len(S.dma_val)):
            sems[('d', s)] = top.enter_context(nc.semaphore(f"sem_d{s}"))
        banks = []
        for i in range(8):
            t = top.enter_context(nc.psum_tensor(f"bank{i}", [128, 512], F32))
            banks.append((t, Buf()))
        P = PS(banks)
        C.P = P
        W = Ctx()
        W.norm_g, W.w_in, W.sgu_g, W.w_s, W.b_s = norm_g, w_in, sgu_g, w_s, b_s
        W.qn_b, W.kn_b, W.qn_c, W.kn_c = qn_b, kn_b, qn_c, kn_c
        W.lam = (lam_q1, lam_k1, lam_q2, lam_k2)
        W.subln_g, W.conv_w, W.w_out = subln_g, conv_w, w_out

        for l in range(depth):
            xsrc = x_in if l == 0 else D.x1
            ydst = y_out if l == depth - 1 else D.x1
            for ph in (phase1, phase2, phase3, phase4)[:phases]:
                S.new_phase()
                for bk in banks:
                    bk[1].w = None
                    bk[1].r = {}
                with ExitStack() as es:
                    sb = mk_sb(nc, es)
                    ph(C, S, sb, W, D, l, xsrc, ydst, seqs, NT)
                    S.barrier()
                    S.replay(nc, sems)
    return nc


def phase1(C, S, sb, W, D, l, xsrc, ydst, seqs, NT):
    nc, P = C.nc, C.P
    P.set_rot(range(8))
    build_consts(C, S, sb, full=False)
    cb = C.constbuf
    wb = sb("wb", [128, 8, PROJ_W], BF16)
    wbuf = Buf()
    g8 = sb("g8", [128, 8], F32)
    DMA(S, g8[:], W.norm_g[l:l + 1, :].rearrange("o (k p) -> p (o k)", p=128), [], [wbuf], slow=True)
    WST = 640
    wst = [(sb(f"wst{i}", [128, WST], F32), Buf()) for i in range(2)]
    n = 0
    for kc in range(8):
        for c0 in range(0, PROJ_W, WST):
            st, stb = wst[n % 2]
            DMA(S, st[:], W.w_in[l, kc * 128:(kc + 1) * 128, c0:c0 + WST], [], [stb])
            if n % 2 == 0:
                TS(S, "dve", wb[:, kc, c0:c0 + WST], st[:], g8[:, kc:kc + 1], ALU.mult, [stb, wbuf], [wbuf])
            else:
                ACT(S, wb[:, kc, c0:c0 + WST], st[:], AF.Copy, [stb, wbuf], [wbuf], scale=g8[:, kc:kc + 1])
            n += 1
    wsT = sb("wsT", [128, NHEAD, 128], BF16)
    for g in range(NHEAD):
        st, stb = wst[n % 2]
        n += 1
        DMA(S, st[:, 0:128], W.w_s[l, g], [], [stb])
        bk, bb = P.next()
        TRG(S, [(bk[:, 0:128], st[:, 0:128], C.identf[:])], [stb, cb], [bb])
        CP(S, "dve", wsT[:, g, :], bk[:, 0:128], [bb], [wbuf])
    bsT = sb("bsT", [128, NHEAD], F32)
    DMA(S, bsT[:], W.b_s[l].rearrange("g t -> t g"), [], [wbuf], slow=True)
    sgu_bc = sb("sgu_bc", [128, BW], F32)
    DMA(S, sgu_bc[:], W.sgu_g[l:l + 1, :].to_broadcast([128, BW]), [], [wbuf])
    g64 = sb("g64", [128, 4, 64], F32)
    DMA(S, g64[:, 0, :], W.qn_b[l:l + 1, :].to_broadcast([128, 64]), [], [wbuf])
    DMA(S, g64[:, 1, :], W.kn_b[l:l + 1, :].to_broadcast([128, 64]), [], [wbuf])
    DMA(S, g64[:, 2, 0:32], W.qn_c[l:l + 1, :].to_broadcast([128, 32]), [], [wbuf])
    DMA(S, g64[:, 3, 0:32], W.kn_c[l:l + 1, :].to_broadcast([128, 32]), [], [wbuf])
    gq_b = sb("gq_b", [128, BW], F32)
    gk_b = sb("gk_b", [128, BW], F32)
    gq_c = sb("gq_c", [128, BW], F32)
    gk_c = sb("gk_c", [128, BW], F32)
    TS(S, "dve", gq_b[:].rearrange("p (h e) -> p h e", e=64), g64[:, 0, :].unsqueeze(1).to_broadcast([128, 6, 64]),
       0.125, ALU.mult, [wbuf], [wbuf])
    TS(S, "dve", gk_b[:].rearrange("p (h e) -> p h e", e=64), g64[:, 1, :].unsqueeze(1).to_broadcast([128, 6, 64]),
       1.0, ALU.mult, [wbuf], [wbuf])
    TS(S, "dve", gq_c[:].rearrange("p (h e) -> p h e", e=32), g64[:, 2, 0:32].unsqueeze(1).to_broadcast([128, 12, 32]),
       1.0 / math.sqrt(32.0), ALU.mult, [wbuf], [wbuf])
    TS(S, "dve", gk_c[:].rearrange("p (h e) -> p h e", e=32), g64[:, 3, 0:32].unsqueeze(1).to_broadcast([128, 12, 32]),
       1.0, ALU.mult, [wbuf], [wbuf])

    def slots(name, shape, dt, k):
        return [(sb(f"{name}{i}", shape, dt), Buf()) for i in range(k)]
    xs = slots("xs", [128, D_MODEL], F32, 2)
    junk = sb("junk", [128, D_MODEL], BF16)
    junkb = Buf()
    hb = slots("hb", [128, D_MODEL], BF16, 4)
    hT = slots("hT", [128, 8, 512], BF16, 2)
    st4 = slots("st4", [128, 4], F32, 2)
    sqf = slots("sqf", [128, BW], F32, 2)
    s12 = slots("s12", [128, 3, 12], F32, 2)
    vn = slots("vn", [128, BW], BF16, 1)
    sg = slots("sg", [128, BW], F32, 1)
    t1 = slots("t1", [128, BW], F32, 1)
    t2 = slots("t2", [128, BW], F32, 1)
    mixA = slots("mixA", [128, BW], BF16, 1)
    qt = slots("qt", [128, BW], F32, 2)
    qn = slots("qn", [128, BW], BF16, 4)
    stg_T = {nm: slots(f"stg_{nm}", [128, 3, 512], BF16, 1)[0] for nm in ("mixa", "qbt", "kbt", "qct", "kct")}
    stg_v = {nm: slots(f"stg_{nm}", [128, 4, BW], BF16, 1)[0] for nm in ("vb", "vc")}
    fm_b = slots("fm_b", [128, 512], BF16, 3)
    fm_f = slots("fm_f", [128, 512], F32, 2)
    tmpf = slots("tmpf", [128, 512], F32, 2)
    cnt = {"x": 0, "h": 0, "st": 0, "sq": 0, "s12": 0, "vn": 0, "sg": 0, "t1": 0, "t2": 0, "mixA": 0, "qt": 0, "qn": 0,
           "fmb": 0, "fmf": 0, "tmpf": 0}

    def nxt(lst, key):
        v = lst[cnt[key] % len(lst)]
        cnt[key] += 1
        return v

    ngroups = NT // 512

    def prep(g):
        res = []
        for j in range(4):
            t0 = g * 512 + j * 128
            x_t, x_b = nxt(xs, "x")
            DMA(S, x_t[:], xsrc[t0:t0 + 128, :], [], [x_b])
            s_t, s_b = nxt(st4, "st")
            ACT(S, junk[:], x_t[:], AF.Square, [x_b], [junkb, s_b], accum_out=s_t[:, 0:1])
            ACT(S, s_t[:, 1:2], s_t[:, 0:1], AF.Sqrt, [s_b], [s_b], scale=1.0 / D_MODEL, bias=EPS)
            S.emit("dve", lambda e, s_t=s_t: e.reciprocal(out=s_t[:, 2:3], in_=s_t[:, 1:2]), [s_b], [s_b])
            h_t, h_b = nxt(hb, "h")
            ACT(S, h_t[:], x_t[:], AF.Copy, [x_b, s_b], [h_b], scale=s_t[:, 2:3])
            res.append((h_t, h_b))
        return res

    def transposes(g, hs):
        hT_t, hT_b = hT[g % 2]
        for j, (h_t, h_b) in enumerate(hs):
            bk, bb = P.next()
            bkb = bk[:].bitcast(BF16)
            TRG(S, [(bkb[:, kc * 128:(kc + 1) * 128], h_t[:, kc * 128:(kc + 1) * 128], C.identb[:]) for kc in range(8)],
                [h_b, cb], [bb])
            CP(S, "dve" if j % 2 == 0 else "act", hT_t[:, :, j * 128:(j + 1) * 128],
               bkb[:, 0:1024].rearrange("p (k t) -> p k t", t=128), [bb], [hT_b])
        return hT_t, hT_b

    def proj_tm(hT_t, hT_b, j, piece):
        bk, bb = P.next()
        MMG(S, [(bk[:, 0:BW], hT_t[:, kc, j * 128:(j + 1) * 128], wb[:, kc, piece * BW:(piece + 1) * BW], kc == 0, kc == 7)
                for kc in range(8)], [hT_b, wbuf], [bb])
        return bk, bb

    def headnorm(bk, bb, nh, gain, eng2):
        hd = BW // nh
        sq_t, sq_b = nxt(sqf, "sq")
        ACT(S, sq_t[:], bk[:, 0:BW], AF.Square, [bb], [sq_b])
        s_t, s_b = nxt(s12, "s12")
        S.emit("dve", lambda e: e.tensor_reduce(out=s_t[:, 0, 0:nh], in_=sq_t[:].rearrange("p (h e) -> p h e", e=hd),
                                                axis=AX.X, op=ALU.add), [sq_b], [s_b])
        ACT(S, s_t[:, 1, 0:nh], s_t[:, 0, 0:nh], AF.Sqrt, [s_b], [s_b], scale=1.0 / hd, bias=EPS)
        S.emit("dve", lambda e: e.reciprocal(out=s_t[:, 2, 0:nh], in_=s_t[:, 1, 0:nh]), [s_b], [s_b])
        q_t, q_b = nxt(qt, "qt")
        TT(S, "dve", q_t[:].rearrange("p (h e) -> p h e", e=hd), bk[:, 0:BW].rearrange("p (h e) -> p h e", e=hd),
           s_t[:, 2, 0:nh].unsqueeze(2).to_broadcast([128, nh, hd]), ALU.mult, [bb, s_b], [q_b])
        n_t, n_b = nxt(qn, "qn")
        TT(S, eng2, n_t[:], q_t[:], gain[:], ALU.mult, [q_b, wbuf], [n_b])
        return n_t, n_b

    def tr3(src_t, src_b, stg, j, eng):
        st_t, st_b = stg
        bk, bb = P.next()
        bkb = bk[:].bitcast(BF16)
        TRG(S, [(bkb[:, c * 128:(c + 1) * 128], src_t[:, c * 128:(c + 1) * 128], C.identb[:]) for c in range(3)],
            [src_b, cb], [bb])
        CP(S, eng, st_t[:, :, j * 128:(j + 1) * 128], bkb[:, 0:384].rearrange("p (c t) -> p c t", t=128), [bb], [st_b])

    hs = prep(0)
    for g in range(ngroups):
        tok0 = g * 512
        hT_t, hT_b = transposes(g, hs)
        if g + 1 < ngroups:
            hs = prep(g + 1)
        for j in range(4):
            bk_v, bb_v = proj_tm(hT_t, hT_b, j, P_AV)
            s_t, s_b = nxt(st4, "st")
            sq_t, sq_b = nxt(sqf, "sq")
            ACT(S, sq_t[:], bk_v[:, 0:BW], AF.Square, [bb_v], [sq_b, s_b], accum_out=s_t[:, 0:1])
            ACT(S, s_t[:, 1:2], s_t[:, 0:1], AF.Sqrt, [s_b], [s_b], scale=1.0 / BW, bias=EPS)
            S.emit("dve", lambda e, s_t=s_t: e.reciprocal(out=s_t[:, 2:3], in_=s_t[:, 1:2]), [s_b], [s_b])
            vn_t, vn_b = nxt(vn, "vn")
            STT(S, vn_t[:], bk_v[:, 0:BW], s_t[:, 2:3], sgu_bc[:], ALU.mult, ALU.mult, [bb_v, s_b, wbuf], [vn_b])
            bk, bb = proj_tm(hT_t, hT_b, j, P_BQ)
            qb_n = headnorm(bk, bb, 6, gq_b, "pool")
            bk, bb = proj_tm(hT_t, hT_b, j, P_BK)
            kb_n = headnorm(bk, bb, 6, gk_b, "pool")
            bk_u, bb_u = proj_tm(hT_t, hT_b, j, P_AU)
            bk_g, bb_g = proj_tm(hT_t, hT_b, j, P_AG)
            sg_t, sg_b = nxt(sg, "sg")
            ACT(S, sg_t[:], bk_g[:, 0:BW], AF.Silu, [bb_g], [sg_b])
            bk_m, bb_m = P.next()
            MMG(S, [(bk_m[:, gg * 64:(gg + 1) * 64], wsT[:, gg, :], vn_t[:, gg * 64:(gg + 1) * 64], True, True)
                    for gg in range(NHEAD)], [vn_b, wbuf], [bb_m])
            t1_t, t1_b = nxt(t1, "t1")
            TT(S, "dve", t1_t[:].rearrange("p (h e) -> p h e", e=64), bk_m[:, 0:BW].rearrange("p (h e) -> p h e", e=64),
               bsT[:].unsqueeze(2).to_broadcast([128, 6, 64]), ALU.add, [bb_m, wbuf], [t1_b])
            t2_t, t2_b = nxt(t2, "t2")
            TT(S, "dve", t2_t[:], bk_u[:, 0:BW], t1_t[:], ALU.mult, [bb_u, t1_b], [t2_b])
            ma_t, ma_b = nxt(mixA, "mixA")
            TT(S, "pool", ma_t[:], t2_t[:], sg_t[:], ALU.mult, [t2_b, sg_b], [ma_b])
            bk, bb = proj_tm(hT_t, hT_b, j, P_CQ)
            qc_n = headnorm(bk, bb, 12, gq_c, "pool")
            tr3(qb_n[0], qb_n[1], stg_T["qbt"], j, "dve")
            tr3(kb_n[0], kb_n[1], stg_T["kbt"], j, "act")
            bk, bb = proj_tm(hT_t, hT_b, j, P_CK)
            kc_n = headnorm(bk, bb, 12, gk_c, "pool")
            bk, bb = proj_tm(hT_t, hT_b, j, P_BV)
            CP(S, "act", stg_v["vb"][0][:, j, :], bk[:, 0:BW], [bb], [stg_v["vb"][1]])
            tr3(ma_t, ma_b, stg_T["mixa"], j, "dve")
            bk, bb = proj_tm(hT_t, hT_b, j, P_CV)
            CP(S, "act", stg_v["vc"][0][:, j, :], bk[:, 0:BW], [bb], [stg_v["vc"][1]])
            tr3(qc_n[0], qc_n[1], stg_T["qct"], j, "dve")
            tr3(kc_n[0], kc_n[1], stg_T["kct"], j, "act")
        for nm, dst, r0 in (("mixa", D.mixt, 0), ("qbt", D.qbt, 0), ("kbt", D.kbt, 0), ("qct", D.qct, 0), ("kct", D.kct, 0)):
            st_t, st_b = stg_T[nm]
            DMA(S, dst[r0:r0 + BW, tok0:tok0 + 512].rearrange("(c p) t -> p c t", p=128), st_t[:], [st_b], [])
        for nm, dst in (("vb", D.vb), ("vc", D.vc)):
            st_t, st_b = stg_v[nm]
            DMA(S, dst[tok0:tok0 + 512, :].rearrange("(j p) c -> p j c", p=128), st_t[:], [st_b], [])

        def proj_fm(piece, ch):
            bk, bb = P.next()
            c0 = piece * BW + ch * 128
            MMG(S, [(bk[:, 0:512], wb[:, kc, c0:c0 + 128], hT_t[:, kc, :], kc == 0, kc == 7) for kc in range(8)],
                [hT_b, wbuf], [bb])
            return bk, bb

        for piece, dst in ((P_BG, D.gbt), (P_CG, D.gct)):
            for ch in range(3):
                bk, bb = proj_fm(piece, ch)
                f_t, f_b = nxt(fm_b, "fmb")
                ACT(S, f_t[:], bk[:, 0:512], AF.Silu, [bb], [f_b])
                DMA(S, dst[ch * 128:(ch + 1) * 128, tok0:tok0 + 512], f_t[:], [f_b], [])
        for ch in range(3):
            bk_i, bb_i = proj_fm(P_DI, ch)
            bk_c, bb_c = proj_fm(P_DC, ch)
            tm_t, tm_b = nxt(tmpf, "tmpf")
            CP(S, "act", tm_t[:], bk_i[:, 0:512], [bb_i], [tm_b])
            f_t, f_b = nxt(fm_f, "fmf")
            TT(S, "dve", f_t[:], bk_c[:, 0:512], tm_t[:], ALU.mult, [bb_c, tm_b], [f_b])
            DMA(S, D.zt[ch * 128:(ch + 1) * 128, tok0:tok0 + 512], f_t[:], [f_b], [])
            bk_g, bb_g = proj_fm(P_DG, ch)
            bk_b, bb_b = proj_fm(P_DB, ch)
            tm_t, tm_b = nxt(tmpf, "tmpf")
            ACT(S, tm_t[:], bk_g[:, 0:512], AF.Silu, [bb_g], [tm_b])
            f_t, f_b = nxt(fm_f, "fmf")
            TT(S, "dve", f_t[:], bk_b[:, 0:512], tm_t[:], ALU.mult, [bb_b, tm_b], [f_b])
            DMA(S, D.gdt[ch * 128:(ch + 1) * 128, tok0:tok0 + 512], f_t[:], [f_b], [])


def load_head(C, S, slot, qsrc, ksrc, vsrc, h, s0, L):
    (q_t, k_t, v_t, hb) = slot
    for r in range(2):
        DMA(S, q_t[64 * r:64 * r + 64, 0:L], qsrc[h * 64:(h + 1) * 64, s0:s0 + L], [], [hb])
        DMA(S, k_t[64 * r:64 * r + 64, 0:L], ksrc[h * 64:(h + 1) * 64, s0:s0 + L], [], [hb])
    nb = L // 128
    for b0 in range(0, nb, 16):
        b1 = min(nb, b0 + 16)
        DMA(S, v_t[:, b0:b1, 0:64],
            vsrc[s0 + b0 * 128:s0 + b1 * 128, h * 64:(h + 1) * 64].rearrange("(b p) e -> p b e", p=128), [], [hb])


def alloc_attn(C, S, sb, maxL):
    A = Ctx()
    A.slots = []
    for i in range(2):
        q_t = sb(f"aq{i}", [128, maxL], BF16)
        k_t = sb(f"ak{i}", [128, maxL], BF16)
        v_t = sb(f"av{i}", [128, maxL // 128, VPAD], BF16)
        hb = Buf()
        S.emit("pool", lambda e, v_t=v_t: e.memset(v_t[:, :, 64:VPAD], 1.0), [], [hb])
        A.slots.append((q_t, k_t, v_t, hb))
    A.pT = [(sb(f"pT{i}", [128, 512], BF16), Buf()) for i in range(8)]
    A.ex = [(sb(f"ex{i}", [128, 512], F32), Buf()) for i in range(4)]
    A.npT = 0
    A.nex = 0
    return A


def pe_warmup(S, P, lhsT, rhs, n):
    bk, bb = P.next()
    MMG(S, [(bk[:, 0:512], lhsT, rhs, True, True) for _ in range(n)], [], [bb])


class Pipe:
    def __init__(self, look):
        self.look = look
        self.t = 0
        self.n = 0
        self.pend = []

    def at(self, delay, fn):
        self.pend.append((self.t + delay, self.n, fn))
        self.n += 1

    def tick(self):
        self.t += 1
        while True:
            self.pend.sort(key=lambda p: (p[0], p[1]))
            if not self.pend or self.pend[0][0] > self.t:
                break
            self.pend.pop(0)[2]()

    def block(self, qk_fn, pv_fn):
        qk_fn()
        self.at(self.look + 1, pv_fn)
        self.tick()

    def flush(self):
        while self.pend:
            self.tick()


def phase2(C, S, sb, W, D, l, xsrc, ydst, seqs, NT):
    nc, P = C.nc, C.P
    build_consts(C, S, sb)
    cb = C.constbuf
    maxL = max(L for _, L in seqs)
    A = alloc_attn(C, S, sb, maxL)
    mtab = sb("mtab", [128, TW], F32)
    mt2 = sb("mt2", [128, TW], F32)
    mi = C.iot

    def le(out, lim):
        TS(S, "dve", out, C.absd[:], -1.0, ALU.mult, [cb], [cb], s2=lim + 1.0, op1=ALU.add)
        TS(S, "dve", out, out, 0.0, ALU.max, [cb], [cb], s2=1.0, op1=ALU.min)
    le(mtab[:], 64.0)
    mt3 = C.negd
    for dil, lim in ((4, 256.0), (16, 1024.0)):
        TS(S, "dve", mt2[:], C.absd[:], 1.0 / dil, ALU.mult, [cb], [cb])
        CP(S, "dve", mi[:], mt2[:], [cb], [cb])
        CP(S, "dve", mt3[:], mi[:], [cb], [cb])
        TT(S, "dve", mt2[:], mt2[:], mt3[:], ALU.is_equal, [cb], [cb])
        le(mt3[:], lim)
        TT(S, "dve", mt2[:], mt2[:], mt3[:], ALU.mult, [cb], [cb])
        TT(S, "dve", mtab[:], mtab[:], mt2[:], ALU.add, [cb], [cb])
    rtab = [(sb(f"rtab{i}", [128, TW], F32), Buf()) for i in range(2)]
    gate = [(sb(f"gate{i}", [64, 512], BF16), Buf()) for i in range(3)]
    rc = [(sb(f"rc{i}", [128, 512], F32), Buf()) for i in range(2)]
    tq = [(sb(f"tq{i}", [64, 512], F32), Buf()) for i in range(2)]
    ob = [(sb(f"ob{i}", [64, 512], BF16), Buf()) for i in range(2)]
    P.set_rot([0, 1, 2, 3, 6, 7])
    accs = [P.banks[4], P.banks[5]]
    pipe = Pipe(2)
    units = [(s0, L, h) for (s0, L) in seqs for h in range(NHEAD)]
    load_head(C, S, A.slots[0], D.qbt, D.kbt, D.vb, units[0][2], units[0][0], units[0][1])
    nq = 0
    nblkc = [0]
    for ui, (s0, L, h) in enumerate(units):
        slot = A.slots[ui % 2]
        q_t, k_t, v_t, hb = slot
        r_t, r_b = rtab[ui % 2]
        ACT(S, r_t[:], C.absd[:], AF.Exp, [cb], [r_b], scale=-SLOPES[h])
        TT(S, "pool", r_t[:], r_t[:], mtab[:], ALU.mult, [r_b, cb], [r_b])
        nblk = L // 128
        for qi in range(L // 512):
            q0 = qi * 512
            acc, accb = accs[nq % 2]
            g_t, g_b = gate[nq % 3]
            DMA(S, g_t[:], D.gbt[h * 64:(h + 1) * 64, s0 + q0:s0 + q0 + 512], [], [g_b])
            kbs = list(range(max(0, q0 // 128 - 8), min(nblk, q0 // 128 + 4 + 8)))
            dmax_b = SKIP_T / SLOPES[h]
            kbs = [kb for kb in kbs if max(q0 - kb * 128 - 127, kb * 128 - q0 - 511, 0) <= dmax_b]
            rc_t, rc_b = rc[nq % 2]
            t_t, t_b = tq[nq % 2]
            o_t, o_b = ob[nq % 2]
            dst = D.mixt[BW + h * 64:BW + (h + 1) * 64, s0 + q0:s0 + q0 + 512]

            def epiA(acc=acc, accb=accb, rc_t=rc_t, rc_b=rc_b):
                ACT(S, rc_t[64:65, :], acc[64:65, 0:512], AF.Ln, [accb], [rc_b])
                ACT(S, rc_t[64:65, :], rc_t[64:65, :], AF.Exp, [rc_b], [rc_b], scale=-1.0)

            def epiB(acc=acc, accb=accb, rc_t=rc_t, rc_b=rc_b, t_t=t_t, t_b=t_b, o_t=o_t, o_b=o_b,
                     g_t=g_t, g_b=g_b, dst=dst):
                bc, bcb = P.next()
                MMG(S, [(bc[0:64, 0:512], C.ones_f[64:65, 0:64], rc_t[64:65, :], True, True)], [rc_b, cb], [bcb])
                TT(S, "dve", t_t[:], acc[0:64, 0:512], g_t[:], ALU.mult, [accb, g_b], [t_b])
                TT(S, "dve", o_t[:], bc[0:64, 0:512], t_t[:], ALU.mult, [bcb, t_b], [o_b])
                DMA(S, dst, o_t[:], [o_b], [])

            sbl = [kbs[i:i + 2] for i in range(0, len(kbs), 2)]
            for si, grp in enumerate(sbl):
                pts = []
                for _ in grp:
                    pts.append(A.pT[A.npT % len(A.pT)])
                    A.npT += 1
                first, last = (si == 0), (si == len(sbl) - 1)

                def qk(grp=grp, pts=pts, q0=q0, k_t=k_t, q_t=q_t, hb=hb, r_t=r_t, r_b=r_b):
                    bks = [P.next() for _ in grp]
                    MMG(S, [(bks[j][0][:, 0:512], k_t[64 * j:64 * j + 64, kb * 128:(kb + 1) * 128],
                             q_t[64 * j:64 * j + 64, q0:q0 + 512], True, True) for j, kb in enumerate(grp)],
                        [hb], [bb for _, bb in bks])
                    for j, kb in enumerate(grp):
                        o = kb * 128 - q0
                        bk, bb = bks[j]
                        p_t, p_b = pts[j]
                        ex_t, ex_b = A.ex[A.nex % len(A.ex)]
                        A.nex += 1
                        ACT(S, ex_t[:], bk[:, 0:512], AF.Exp, [bb], [ex_b])
                        eng = "pool" if nblkc[0] % 3 == 2 else "dve"
                        nblkc[0] += 1
                        TT(S, eng, p_t[:], ex_t[:], r_t[:, TC - o:TC - o + 512], ALU.mult, [ex_b, r_b], [p_b])

                def pv(grp=grp, pts=pts, first=first, last=last, acc=acc, accb=accb, v_t=v_t, hb=hb,
                       epiA=epiA, epiB=epiB):
                    MMG(S, [(acc[0:VPAD, 0:512], v_t[:, kb, :], pts[j][0][:], first and j == 0, last and j == len(grp) - 1)
                            for j, kb in enumerate(grp)], [hb] + [p_b for _, p_b in pts], [accb])
                    if last:
                        pipe.at(2, epiA)
                        pipe.at(4, epiB)
                pipe.block(qk, pv)
            nq += 1
            if qi == 0 and ui + 1 < len(units):
                n0, nL, nh = units[ui + 1]
                load_head(C, S, A.slots[(ui + 1) % 2], D.qbt, D.kbt, D.vb, nh, n0, nL)
    pipe.flush()


def load_head_c(C, S, slot, D, h, s0, L, sl, JLf, cb):
    (q_t, k_t, v_t, hb, kxb, pat, patf) = slot
    for c in range(2):
        r0 = h * 64 + c * 32
        DMA(S, q_t[64 * c + 3:64 * c + 35, 0:L], D.qct[r0:r0 + 32, s0:s0 + L], [], [hb])
        DMA(S, k_t[64 * c + 3:64 * c + 35, 0:L], D.kct[r0:r0 + 32, s0:s0 + L], [], [hb])
    nb = L // 128
    for b0 in range(0, nb, 16):
        b1 = min(nb, b0 + 16)
        DMA(S, v_t[:, b0:b1, 0:64],
            D.vc[s0 + b0 * 128:s0 + b1 * 128, h * 64:(h + 1) * 64].rearrange("(b p) e -> p b e", p=128), [], [hb])
    pb = Buf()
    TS(S, "dve", patf[0:1, 0, :], JLf[0:1, :], sl, ALU.mult, [cb], [pb])
    CP(S, "dve", pat[0:1, 0, :], patf[0:1, 0, :], [pb], [pb])
    CP(S, "dve", patf[0:1, 1, :], pat[0:1, 0, :], [pb], [pb])
    TT(S, "dve", patf[0:1, 2, :], patf[0:1, 0, :], patf[0:1, 1, :], ALU.subtract, [pb], [pb])
    CP(S, "dve", pat[0:1, 1, :], patf[0:1, 2, :], [pb], [pb])
    CP(S, "dve", patf[0:1, 1, :], pat[0:1, 1, :], [pb], [pb])
    TT(S, "dve", patf[0:1, 0, :], patf[0:1, 2, :], patf[0:1, 1, :], ALU.subtract, [pb], [pb])
    CP(S, "dve", pat[0:1, 2, :], patf[0:1, 0, :], [pb], [pb])
    nt = L // 512
    for c in range(2):
        for r in range(3):
            DMA(S, q_t[64 * c + r:64 * c + r + 1, 0:L].rearrange("p (t j) -> p t j", j=512),
                pat[0:1, r, :].unsqueeze(1).to_broadcast([1, nt, 512]), [pb], [hb])
        S.emit("dve", lambda e, c=c: e.memset(k_t[64 * c:64 * c + 3, 0:512], 0.0), [], [kxb[0]])
        if L > 512:
            S.emit("dve", lambda e, c=c: e.memset(k_t[64 * c:64 * c + 3, 512:L], 1.0), [], list(kxb[1:L // 512]))


def phase3(C, S, sb, W, D, l, xsrc, ydst, seqs, NT):
    nc, P = C.nc, C.P
    build_consts(C, S, sb, lo=TC - 384, width=896)
    cb = C.constbuf
    maxL = max(L for _, L in seqs)
    NB = maxL // 128
    lambda_init = 0.8 - 0.6 * math.exp(-0.3 * l)
    A = Ctx()
    A.slots = []
    for i in range(2):
        q_t = sb(f"aq{i}", [128, maxL], BF16)
        k_t = sb(f"ak{i}", [128, maxL], BF16)
        v_t = sb(f"av{i}", [128, maxL // 128, VPAD], BF16)
        pat = sb(f"pat{i}", [1, 3, 512], BF16)
        patf = sb(f"patf{i}", [1, 3, 512], F32)
        hb = Buf()
        kxb = [Buf() for _ in range(maxL // 512)]
        S.emit("pool", lambda e, v_t=v_t: e.memset(v_t[:, :, 64:VPAD], 1.0), [], [hb])
        if KPAD > 35:
            S.emit("dve", lambda e, q_t=q_t: e.memset(q_t[:], 0.0), [], [hb] + kxb)
            S.emit("pool", lambda e, k_t=k_t: e.memset(k_t[:], 0.0), [], [hb] + kxb)
        A.slots.append((q_t, k_t, v_t, hb, kxb, pat, patf))
    A.pT = [(sb(f"pT{i}", [128, 512], BF16), Buf()) for i in range(10)]
    A.ex = [(sb(f"ex{i}", [128, 512], F32), Buf()) for i in range(4)]
    A.npT = 0
    A.nex = 0
    tli = sb("tli", [128, NB + 1], I32)
    tri = sb("tri", [128, NB + 1], I32)
    jli = sb("jli", [128, 512], I32)
    TLf = sb("TLf", [128, NB + 1], F32)
    TRf = sb("TRf", [128, NB + 1], F32)
    JLf = sb("JLf", [128, 512], F32)
    S.emit("pool", lambda e: e.iota(tli[:], [[128, NB + 1]], base=0, channel_multiplier=-1), [], [cb])
    S.emit("pool", lambda e: e.iota(tri[:], [[128, NB + 1]], base=0, channel_multiplier=1), [], [cb])
    S.emit("pool", lambda e: e.iota(jli[:], [[1, 512]], base=0, channel_multiplier=0), [], [cb])
    for a, b in ((TLf, tli), (TRf, tri), (JLf, jli)):
        CP(S, "dve", a[:], b[:], [cb], [cb])
    lamv = sb("lamv", [64, 4, 32], F32)
    lams = sb("lams", [64, 8], F32)
    lamj = sb("lamj", [64, 32], F32)
    for i, src in enumerate(W.lam):
        DMA(S, lamv[:, i, :], src[l:l + 1, :].to_broadcast([64, 32]), [], [cb])
    for i in range(2):
        TT(S, "dve", lamj[:], lamv[:, 2 * i, :], lamv[:, 2 * i + 1, :], ALU.mult, [cb], [cb])
        S.emit("dve", lambda e, i=i: e.tensor_reduce(out=lams[:, i:i + 1], in_=lamj[:], axis=AX.X, op=ALU.add), [cb], [cb])
    ACT(S, lams[:, 2:4], lams[:, 0:2], AF.Exp, [cb], [cb])
    TT(S, "dve", lams[:, 4:5], lams[:, 3:4], lams[:, 2:3], ALU.subtract, [cb], [cb])
    TS(S, "dve", lams[:, 5:6], lams[:, 4:5], -lambda_init, ALU.add, [cb], [cb])
    subg = sb("subg", [64, 2], F32)
    DMA(S, subg[:, 0:1], W.subln_g[l:l + 1, :].rearrange("o e -> e o"), [], [cb], slow=True)
    TS(S, "dve", subg[:, 1:2], subg[:, 0:1], 1.0 - lambda_init, ALU.mult, [cb], [cb])
    neglam = lams[:, 5:6]

    rtab = [(sb(f"rtab{i}", [128, 896], F32), Buf()) for i in range(2)]
    bl = [sb(f"bl{i}", [128, NB + 1], F32) for i in range(2)]
    br = [sb(f"br{i}", [128, NB + 1], F32) for i in range(2)]
    gate = [(sb(f"gate{i}", [64, 512], BF16), Buf()) for i in range(3)]
    ocs = [(sb(f"ocs{i}", [65, 2, 512], F32), Buf()) for i in range(2)]
    rc = [(sb(f"rc{i}", [128, 2, 512], F32), Buf()) for i in range(2)]
    a01 = [(sb(f"a01{i}", [64, 2, 512], F32), Buf()) for i in range(2)]
    at = [(sb(f"at{i}", [64, 512], F32), Buf()) for i in range(2)]
    sqb = [(sb(f"sqb{i}", [64, 512], F32), Buf()) for i in range(2)]
    rs = [(sb(f"rs{i}", [64, 512], F32), Buf()) for i in range(2)]
    ob = [(sb(f"ob{i}", [64, 512], BF16), Buf()) for i in range(2)]
    P.set_rot([0, 1, 2, 3, 6, 7])
    accs = [(P.banks[4], P.banks[5]), (P.banks[4], P.banks[5])]
    pipe = Pipe(2)
    units = [(s0, L, h) for (s0, L) in seqs for h in range(NHEAD)]
    load_head_c(C, S, A.slots[0], D, units[0][2], units[0][0], units[0][1], SLOPES[units[0][2]], JLf, cb)
    nq = 0
    for ui, (s0, L, h) in enumerate(units):
        q_t, k_t, v_t, hb, kxb, pat, patf = A.slots[ui % 2]
        sl = SLOPES[h]
        r_t, r_b = rtab[ui % 2]
        ACT(S, r_t[:], C.absd[:, 0:896], AF.Exp, [cb], [r_b], scale=-sl)
        bl_t, br_t = bl[ui % 2], br[ui % 2]
        TS(S, "dve", bl_t[:], TLf[:], -sl, ALU.mult, [cb], [r_b])
        TS(S, "dve", br_t[:], TRf[:], -sl, ALU.mult, [cb], [r_b])
        nblk = L // 128
        dmax = SKIP_T / sl
        if WARM_N:
            pe_warmup(S, P, C.identb[:], A.pT[0][0][:], WARM_N)
        for qi in range(L // 512):
            q0 = qi * 512
            def next_signs(qn=qi + 1, k_t=k_t, kxb=kxb):
                for c in range(2):
                    S.emit("dve", lambda e, c=c: e.memset(k_t[64 * c:64 * c + 3, (qn - 1) * 512:qn * 512], -1.0),
                           [], [kxb[qn - 1]])
                    S.emit("dve", lambda e, c=c: e.memset(k_t[64 * c:64 * c + 3, qn * 512:(qn + 1) * 512], 0.0),
                           [], [kxb[qn]])
            g_t, g_b = gate[nq % 3]
            DMA(S, g_t[:], D.gct[h * 64:(h + 1) * 64, s0 + q0:s0 + q0 + 512], [], [g_b])
            oc_t, oc_b = ocs[nq % 2]
            rc_t, rc_b = rc[nq % 2]
            a_t, a_b = a01[nq % 2]
            at_t, at_b = at[nq % 2]
            sq_t, sq_b = sqb[nq % 2]
            rs_t, rs_b = rs[nq % 2]
            o_t, o_b = ob[nq % 2]
            acc2 = accs[nq % 2]
            dstm = D.mixt[2 * BW + h * 64:2 * BW + (h + 1) * 64, s0 + q0:s0 + q0 + 512]
            lefts = [kb for kb in range(nblk) if kb * 128 - q0 <= -128 and (q0 - kb * 128 - 127) <= dmax]
            diags = [kb for kb in range(nblk) if 0 <= kb * 128 - q0 < 512]
            rights = [kb for kb in range(nblk) if kb * 128 - q0 >= 512 and (kb * 128 - q0 - 511) <= dmax]
            sbs = [("D", kb) for kb in diags] + [("R", kb) for kb in rights] + [("L", kb) for kb in lefts]
            sign_at = min(len(sbs) - 1, 9)

            def epi0(oc_t=oc_t, oc_b=oc_b, acc2=acc2):
                for c in range(2):
                    CP(S, "dve", oc_t[:, c, :], acc2[c][0][0:65, 0:512], [acc2[c][1]], [oc_b])

            def epiA(oc_t=oc_t, oc_b=oc_b, rc_t=rc_t, rc_b=rc_b):
                ACT(S, rc_t[64:65, :, :], oc_t[64:65, :, :], AF.Ln, [oc_b], [rc_b])
                ACT(S, rc_t[64:65, :, :], rc_t[64:65, :, :], AF.Exp, [rc_b], [rc_b], scale=-1.0)

            def epiB(oc_t=oc_t, oc_b=oc_b, rc_t=rc_t, rc_b=rc_b, a_t=a_t, a_b=a_b, at_t=at_t, at_b=at_b,
                     sq_t=sq_t, sq_b=sq_b):
                for c in range(2):
                    bk, bb = P.next()
                    MMG(S, [(bk[0:64, 0:512], C.ones_f[64:65, 0:64], rc_t[64:65, c, :], True, True)], [rc_b, cb], [bb])
                    TT(S, "dve", a_t[:, c, :], oc_t[0:64, c, :], bk[0:64, 0:512], ALU.mult, [oc_b, bb], [a_b])
                STT(S, at_t[:], a_t[:, 1, :], neglam, a_t[:, 0, :], ALU.mult, ALU.add, [a_b, cb], [at_b])
                TT(S, "dve", sq_t[:], at_t[:], at_t[:], ALU.mult, [at_b], [sq_b])

            def epiC(sq_t=sq_t, sq_b=sq_b, rs_t=rs_t, rs_b=rs_b):
                bk, bb = P.next()
                MMG(S, [(bk[0:64, 0:512], C.ones_f[0:64, 0:64], sq_t[:], True, True)], [sq_b, cb], [bb])
                ACT(S, rs_t[:], bk[0:64, 0:512], AF.Ln, [bb], [rs_b], scale=1.0 / 64, bias=EPS)
                ACT(S, rs_t[:], rs_t[:], AF.Exp, [rs_b], [rs_b], scale=-0.5)

            def epiD(at_t=at_t, at_b=at_b, rs_t=rs_t, rs_b=rs_b, o_t=o_t, o_b=o_b, g_t=g_t, g_b=g_b, dstm=dstm):
                TT(S, "dve", at_t[:], at_t[:], rs_t[:], ALU.mult, [at_b, rs_b], [at_b])
                STT(S, o_t[:], at_t[:], subg[:, 1:2], g_t[:], ALU.mult, ALU.mult, [at_b, g_b, cb], [o_b])
                DMA(S, dstm, o_t[:], [o_b], [])

            for si, (kind, kb) in enumerate(sbs):
                pts = []
                for _ in range(2):
                    pts.append(A.pT[A.npT % len(A.pT)])
                    A.npT += 1
                first, last = (si == 0), (si == len(sbs) - 1)

                def qk(kind=kind, kb=kb, pts=pts, q0=q0, k_t=k_t, q_t=q_t, hb=hb, r_t=r_t, r_b=r_b,
                       bl_t=bl_t, br_t=br_t, kxb=kxb):
                    bks = [P.next() for _ in range(2)]
                    MMG(S, [(bks[c][0][:, 0:512], k_t[64 * c:64 * c + KPAD, kb * 128:(kb + 1) * 128],
                             q_t[64 * c:64 * c + KPAD, q0:q0 + 512], True, True) for _ in range(QK_REP) for c in range(2)],
                        [hb, kxb[kb // 4]], [bks[0][1], bks[1][1]])
                    o = kb * 128 - q0
                    for c in range(2):
                        bk, bb = bks[c]
                        p_t, p_b = pts[c]
                        if kind == "L":
                            n = (-o) // 128
                            ACT(S, p_t[:], bk[:, 0:512], AF.Exp, [bb, r_b], [p_b], bias=bl_t[:, n:n + 1])
                        elif kind == "R":
                            n = o // 128
                            ACT(S, p_t[:], bk[:, 0:512], AF.Exp, [bb, r_b], [p_b], bias=br_t[:, n:n + 1])
                        else:
                            ex_t, ex_b = A.ex[A.nex % len(A.ex)]
                            A.nex += 1
                            ACT(S, ex_t[:], bk[:, 0:512], AF.Exp, [bb], [ex_b])
                            TT(S, "pool" if c == 0 else "dve", p_t[:], ex_t[:], r_t[:, 384 - o:384 - o + 512], ALU.mult,
                               [ex_b, r_b], [p_b])

                def pv(kb=kb, pts=pts, first=first, last=last, acc2=acc2, v_t=v_t, hb=hb,
                       epi0=epi0, epiA=epiA, epiB=epiB, epiC=epiC, epiD=epiD):
                    MMG(S, [(acc2[c][0][0:VPAD, 0:512], v_t[:, kb, :], pts[c][0][:], first, last) for c in range(2)],
                        [hb, pts[0][1], pts[1][1]], [acc2[0][1], acc2[1][1]])
                    if last:
                        epi0()
                        pipe.at(2, epiA)
                        pipe.at(5, epiB)
                        pipe.at(8, epiC)
                        pipe.at(11, epiD)
                pipe.block(qk, pv)
                if si == sign_at and qi + 1 < L // 512:
                    next_signs()
                if WARM_Q and si == 2:
                    pe_warmup(S, P, C.identb[:], A.pT[0][0][:], WARM_Q)
            nq += 1
            if qi == 0 and ui + 1 < len(units):
                n0, nL, nh = units[ui + 1]
                load_head_c(C, S, A.slots[(ui + 1) % 2], D, nh, n0, nL, SLOPES[nh], JLf, cb)
    pipe.flush()


def phase4(C, S, sb, W, D, l, xsrc, ydst, seqs, NT):
    nc, P = C.nc, C.P
    P.set_rot(range(8))
    wo = sb("wo", [128, 12, D_MODEL], BF16)
    wbuf = Buf()
    wst = [(sb(f"wost{i}", [128, D_MODEL], F32), Buf()) for i in range(2)]
    for kc in range(12):
        st, stb = wst[kc % 2]
        DMA(S, st[:], W.w_out[l, kc * 128:(kc + 1) * 128, :], [], [stb])
        CP(S, ("dve", "act")[kc % 2], wo[:, kc, :], st[:], [stb], [wbuf])
    cw = sb("cw", [128, 3, 3], F32)
    for ch in range(3):
        DMA(S, cw[:, ch, :], W.conv_w[l, :, ch * 128:(ch + 1) * 128].rearrange("j p -> p j"), [], [wbuf], slow=True)
    mx = [(sb(f"mx{i}", [128, 12, 512], BF16), Buf()) for i in range(2)]
    zt = [(sb(f"zt{i}", [128, 3, 514], F32), Buf()) for i in range(2)]
    gd = [(sb(f"gd{i}", [128, 3, 512], F32), Buf()) for i in range(2)]
    cv = [(sb(f"cv{i}", [128, 3, 512], F32), Buf()) for i in range(2)]
    xr = [(sb(f"xr{i}", [128, D_MODEL], F32), Buf()) for i in range(3)]
    yo = [(sb(f"yo{i}", [128, D_MODEL], F32), Buf()) for i in range(3)]
    starts = {s0 for s0, _ in seqs}
    ends = {s0 + L for s0, L in seqs}
    ngroups = NT // 512
    nt = 0

    def loads(g):
        tok0 = g * 512
        m_t, m_b = mx[g % 2]
        DMA(S, m_t[:, 0:9, :], D.mixt[0:3 * BW, tok0:tok0 + 512].rearrange("(c p) t -> p c t", p=128), [], [m_b])
        z_t, z_b = zt[g % 2]
        lo = 0 if tok0 in starts else 1
        hi = 0 if (tok0 + 512) in ends else 1
        if lo == 0:
            S.emit("pool", lambda e: e.memset(z_t[:, :, 0:1], 0.0), [], [z_b])
        if hi == 0:
            S.emit("pool", lambda e: e.memset(z_t[:, :, 513:514], 0.0), [], [z_b])
        DMA(S, z_t[:, :, 1 - lo:513 + hi],
            D.zt[:, tok0 - lo:tok0 + 512 + hi].rearrange("(c p) t -> p c t", p=128), [], [z_b])
        g_t, g_b = gd[g % 2]
        DMA(S, g_t[:], D.gdt[:, tok0:tok0 + 512].rearrange("(c p) t -> p c t", p=128), [], [g_b])

    loads(0)
    for g in range(ngroups):
        tok0 = g * 512
        if g + 1 < ngroups:
            loads(g + 1)
        m_t, m_b = mx[g % 2]
        z_t, z_b = zt[g % 2]
        g_t, g_b = gd[g % 2]
        c_t, c_b = cv[g % 2]
        for ch in range(3):
            TS(S, "dve", c_t[:, ch, :], z_t[:, ch, 0:512], cw[:, ch, 0:1], ALU.mult, [z_b, wbuf], [c_b])
            STT(S, c_t[:, ch, :], z_t[:, ch, 1:513], cw[:, ch, 1:2], c_t[:, ch, :], ALU.mult, ALU.add, [z_b, c_b, wbuf], [c_b])
            STT(S, c_t[:, ch, :], z_t[:, ch, 2:514], cw[:, ch, 2:3], c_t[:, ch, :], ALU.mult, ALU.add, [z_b, c_b, wbuf], [c_b])
            TT(S, "pool", m_t[:, 9 + ch, :], c_t[:, ch, :], g_t[:, ch, :], ALU.mult, [c_b, g_b], [m_b])
        for j in range(4):
            t0 = tok0 + j * 128
            x_t, x_b = xr[nt % 3]
            DMA(S, x_t[:], xsrc[t0:t0 + 128, :], [], [x_b])
            y_t, y_b = yo[nt % 3]
            for half in range(2):
                bk, bb = P.next()
                MMG(S, [(bk[:, 0:512], m_t[:, kc, j * 128:(j + 1) * 128], wo[:, kc, half * 512:(half + 1) * 512],
                         kc == 0, kc == 11) for kc in range(12)], [m_b, wbuf], [bb])
                TT(S, "dve", y_t[:, half * 512:(half + 1) * 512], bk[:, 0:512], x_t[:, half * 512:(half + 1) * 512],
                   ALU.add, [bb, x_b], [y_b])
            DMA(S, ydst[t0:t0 + 128, :], y_t[:], [y_b], [])
            nt += 1


_NC_CACHE = {}
SEQS = (8192, 4096, 4096)
WNAMES = ("norm_g", "w_in", "sgu_g", "w_s", "b_s", "qn_b", "kn_b", "qn_c", "kn_c",
          "lam_q1", "lam_k1", "lam_q2", "lam_k2", "subln_g", "conv_w", "w_out")


def kernel(x_prompt, x_sample, **w):
    x_prompt = np.asarray(x_prompt, dtype=np.float32)
    x_sample = np.asarray(x_sample, dtype=np.float32)
    if "nc" not in _NC_CACHE:
        _NC_CACHE["nc"] = build_program(SEQS)
    nc = _NC_CACHE["nc"]
    wmap = {k: np.ascontiguousarray(np.asarray(w[k], dtype=np.float32)) for k in WNAMES}
    in_maps = []
    for c in range(N_CORES):
        xc = np.concatenate([x_prompt[c], x_sample[2 * c], x_sample[2 * c + 1]], axis=0)
        m = {"x": np.ascontiguousarray(xc)}
        m.update(wmap)
        in_maps.append(m)
    res = run_bass_kernel_spmd(nc, in_maps, core_ids=list(range(N_CORES)))
    yp = np.empty_like(x_prompt)
    ys = np.empty_like(x_sample)
    for c in range(N_CORES):
        y = res.results[c]["y"]
        yp[c] = y[0:8192]
        ys[2 * c] = y[8192:12288]
        ys[2 * c + 1] = y[12288:16384]
    return (yp, ys)
```

```python
import math
from contextlib import ExitStack
import numpy as np
import concourse.bass as bass
import concourse.mybir as mybir
from concourse.bass_utils import run_bass_kernel_spmd

F32 = mybir.dt.float32
BF16 = mybir.dt.bfloat16
I32 = mybir.dt.int32
AF = mybir.ActivationFunctionType
ALU = mybir.AluOpType
AX = mybir.AxisListType

D_MODEL = 1024
DEPTH = 2
BW = 384
PROJ_W = 15 * BW
MIX_W = 4 * BW
EPS = 1e-6
NHEAD = 6
SLOPES = [2.0 ** (-8.0 * (i + 1) / NHEAD) for i in range(NHEAD)]
N_CORES = 8
(P_AU, P_AV, P_AG, P_BQ, P_BK, P_BV, P_BG, P_CQ, P_CK, P_CV, P_CG, P_DI, P_DB, P_DC, P_DG) = range(15)
TC = 1408
TW = 2944
SKIP_T = 60.0
VPAD = 128
KPAD = 64
WARM_N = 0
QK_REP = 1
WARM_Q = 0
B_POOL_EVERY = 0
SBK = 2


class Buf:
    __slots__ = ("w", "r")

    def __init__(self):
        self.w = None
        self.r = {}


class Sched:
    ENG = ("pe", "act", "dve", "pool", "sp")

    def __init__(self, n_dma=24):
        self.cnt = {e: 0 for e in ("pe", "act", "dve", "pool")}
        self.dma_val = [0] * n_dma
        self.dma_rr = 0
        self.seen = {e: {} for e in self.ENG}
        self.ops = {e: [] for e in self.ENG}

    def new_phase(self):
        self.ops = {e: [] for e in self.ENG}

    def emit(self, eng, fn, reads=(), writes=(), dma=False):
        waits = {}
        seen = self.seen[eng]

        def add(ev, raw):
            if ev is None:
                return
            k, val = ev
            if k[0] == 'e' and k[1] == eng:
                if not (raw and eng in ("act", "dve", "pool")):
                    return
            if seen.get(k, 0) >= val:
                return
            if waits.get(k, 0) < val:
                waits[k] = val

        for b in reads:
            add(b.w, True)
        for b in writes:
            add(b.w, False)
            for k, v in b.r.items():
                add((k, v), False)
        if dma:
            s = self.dma_rr
            self.dma_rr = (s + 1) % len(self.dma_val)
            prev = self.dma_val[s]
            if prev > 0:
                add((('d', s), prev), False)
            self.dma_val[s] = prev + 16
            ev = (('d', s), prev + 16)
        else:
            self.cnt[eng] += 1
            ev = (('e', eng), self.cnt[eng])
        for k, v in waits.items():
            seen[k] = v
        self.ops[eng].append((fn, list(waits.items()), ev))
        for b in reads:
            if b.r.get(ev[0], 0) < ev[1]:
                b.r[ev[0]] = ev[1]
        for b in writes:
            b.w = ev
            b.r = {}
        return ev

    def barrier(self):
        allev = [(('e', e), c) for e, c in self.cnt.items() if c > 0]
        allev += [(('d', s), v) for s, v in enumerate(self.dma_val) if v > 0]
        for eng in self.ENG:
            waits = {}
            for k, v in allev:
                if k[0] == 'e' and k[1] == eng:
                    continue
                if self.seen[eng].get(k, 0) >= v:
                    continue
                waits[k] = v
                self.seen[eng][k] = v
            self.ops[eng].append((None, list(waits.items()), None))

    def replay(self, nc, sems):
        with nc.Block() as block:
            def run(eng, e):
                for fn, waits, ev in self.ops[eng]:
                    for k, v in waits:
                        e.wait_ge(sems[k], v)
                    if fn is None:
                        continue
                    ins = fn(e)
                    if ev[0][0] == 'd':
                        ins.then_inc(sems[ev[0]], 16)
                    else:
                        ins.then_inc(sems[ev[0]], 1)

            @block.tensor
            def _(e):
                run("pe", e)

            @block.scalar
            def _(e):
                run("act", e)

            @block.vector
            def _(e):
                run("dve", e)

            @block.gpsimd
            def _(e):
                run("pool", e)

            @block.sync
            def _(e):
                run("sp", e)


def ACT(S, out, in_, func, R, W, bias=None, scale=None, accum_out=None):
    kw = {}
    if bias is not None:
        kw["bias"] = bias
    if scale is not None:
        kw["scale"] = scale
    if accum_out is not None:
        kw["accum_out"] = accum_out
    S.emit("act", lambda e: e.activation(out=out, in_=in_, func=func, **kw), R, W)


def TT(S, eng, out, in0, in1, op, R, W):
    S.emit(eng, lambda e: e.tensor_tensor(out=out, in0=in0, in1=in1, op=op), R, W)


def TS(S, eng, out, in0, s1, op0, R, W, s2=None, op1=None):
    if op1 is None:
        S.emit(eng, lambda e: e.tensor_scalar(out=out, in0=in0, scalar1=s1, scalar2=None, op0=op0), R, W)
    else:
        S.emit(eng, lambda e: e.tensor_scalar(out=out, in0=in0, scalar1=s1, scalar2=s2, op0=op0, op1=op1), R, W)


def STT(S, out, in0, scalar, in1, op0, op1, R, W):
    S.emit("dve", lambda e: e.scalar_tensor_tensor(out=out, in0=in0, scalar=scalar, in1=in1, op0=op0, op1=op1), R, W)


def CP(S, eng, out, in_, R, W):
    if eng == "act":
        S.emit(eng, lambda e: e.copy(out=out, in_=in_), R, W)
    else:
        S.emit(eng, lambda e: e.tensor_copy(out=out, in_=in_), R, W)


def MMG(S, mms, R, W):
    def fn(e):
        ins = None
        for (out, lhsT, rhs, st, sp) in mms:
            ins = e.matmul(out, lhsT=lhsT, rhs=rhs, start=st, stop=sp)
        return ins
    S.emit("pe", fn, R, W)


def TRG(S, trs, R, W):
    def fn(e):
        ins = None
        for (out, in_, ident) in trs:
            ins = e.transpose(out=out, in_=in_, identity=ident)
        return ins
    S.emit("pe", fn, R, W)


def DMA(S, out, in_, R, W, slow=False):
    if slow:
        S.emit("sp", lambda e: e.dma_start(out=out, in_=in_, allow_slow_non_contiguous=True), R, W, dma=True)
    else:
        S.emit("sp", lambda e: e.dma_start(out=out, in_=in_), R, W, dma=True)


class PS:
    def __init__(self, banks):
        self.banks = banks
        self.rot = list(range(8))
        self.i = 0

    def set_rot(self, idxs):
        self.rot = list(idxs)
        self.i = 0

    def next(self):
        b = self.banks[self.rot[self.i % len(self.rot)]]
        self.i += 1
        return b


class Ctx:
    pass


_uid = [0]


def mk_sb(nc, es):
    def sb(name, shape, dt):
        _uid[0] += 1
        return es.enter_context(nc.sbuf_tensor(f"{name}_{_uid[0]}", shape, dt))
    return sb


def build_consts(C, S, sb, full=True, lo=0, width=TW):
    nc = C.nc
    b = Buf()
    C.constbuf = b
    C.identb = sb("identb", [128, 128], BF16)
    C.identf = sb("identf", [128, 128], F32)
    C.ones_f = sb("ones_f", [128, 64], F32)
    S.emit("pool", lambda e: e.memset(C.ones_f[:], 1.0), [], [b])
    if not full:
        iot = sb("iot", [128, 128], I32)
        S.emit("pool", lambda e: e.iota(iot[:], [[-1, 128]], base=0, channel_multiplier=1), [], [b])
        TS(S, "dve", C.identb[:], iot[:], 0.0, ALU.is_equal, [b], [b])
        TS(S, "dve", C.identf[:], iot[:], 0.0, ALU.is_equal, [b], [b])
        return
    C.absd = sb("absd", [128, width], F32)
    iot = sb("iot", [128, width], I32)
    S.emit("pool", lambda e: e.iota(iot[:], [[-1, width]], base=TC - lo, channel_multiplier=1), [], [b])
    TS(S, "dve", C.identb[:], iot[:, TC - lo:TC - lo + 128], 0.0, ALU.is_equal, [b], [b])
    TS(S, "dve", C.identf[:], iot[:, TC - lo:TC - lo + 128], 0.0, ALU.is_equal, [b], [b])
    C.negd = sb("negd", [128, width], F32)
    CP(S, "dve", C.negd[:], iot[:], [b], [b])
    TS(S, "dve", C.absd[:], C.negd[:], -1.0, ALU.mult, [b], [b])
    TT(S, "dve", C.absd[:], C.absd[:], C.negd[:], ALU.max, [b], [b])
    C.iot = iot


def build_program(seq_lens, dump=False, depth=DEPTH, phases=4):
    nc = bass.Bass("TRN2", target_bir_lowering=False)
    NT = sum(seq_lens)
    seqs = []
    o = 0
    for L in seq_lens:
        seqs.append((o, L))
        o += L
    C = Ctx()
    C.nc = nc
    ext_in = lambda name, shape: nc.dram_tensor(name, shape, F32, kind="ExternalInput").ap()
    x_in = ext_in("x", [NT, D_MODEL])
    norm_g = ext_in("norm_g", [DEPTH, D_MODEL])
    w_in = ext_in("w_in", [DEPTH, D_MODEL, PROJ_W])
    sgu_g = ext_in("sgu_g", [DEPTH, BW])
    w_s = ext_in("w_s", [DEPTH, NHEAD, 128, 128])
    b_s = ext_in("b_s", [DEPTH, NHEAD, 128])
    qn_b = ext_in("qn_b", [DEPTH, 64])
    kn_b = ext_in("kn_b", [DEPTH, 64])
    qn_c = ext_in("qn_c", [DEPTH, 32])
    kn_c = ext_in("kn_c", [DEPTH, 32])
    lam_q1 = ext_in("lam_q1", [DEPTH, 32])
    lam_k1 = ext_in("lam_k1", [DEPTH, 32])
    lam_q2 = ext_in("lam_q2", [DEPTH, 32])
    lam_k2 = ext_in("lam_k2", [DEPTH, 32])
    subln_g = ext_in("subln_g", [DEPTH, 64])
    conv_w = ext_in("conv_w", [DEPTH, 3, BW])
    w_out = ext_in("w_out", [DEPTH, MIX_W, D_MODEL])
    y_out = nc.dram_tensor("y", [NT, D_MODEL], F32, kind="ExternalOutput").ap()

    skind = "ExternalOutput" if dump else "Internal"
    scr = lambda name, shape, dt: nc.dram_tensor(name, shape, dt, kind=skind).ap()
    D = Ctx()
    D.x1 = scr("x1", [NT, D_MODEL], F32)
    D.qbt = scr("qbt", [BW, NT], BF16)
    D.kbt = scr("kbt", [BW, NT], BF16)
    D.vb = scr("vb", [NT, BW], BF16)
    D.gbt = scr("gbt", [BW, NT], BF16)
    D.qct = scr("qct", [BW, NT], BF16)
    D.kct = scr("kct", [BW, NT], BF16)
    D.vc = scr("vc", [NT, BW], BF16)
    D.gct = scr("gct", [BW, NT], BF16)
    D.zt = scr("zt", [BW, NT], F32)
    D.gdt = scr("gdt", [BW, NT], F32)
    D.mixt = scr("mixt", [MIX_W, NT], BF16)

    S = Sched()
    with ExitStack() as top:
        sems = {}
        for e in ("pe", "act", "dve", "pool"):
            sems[('e', e)] = top.enter_context(nc.semaphore(f"sem_{e}"))
        for s in range(len(S.dma_val)):
            sems[('d', s)] = top.enter_context(nc.semaphore(f"sem_d{s}"))
        banks = []
        for i in range(8):
            t = top.enter_context(nc.psum_tensor(f"bank{i}", [128, 512], F32))
            banks.append((t, Buf()))
        P = PS(banks)
        C.P = P
        W = Ctx()
        W.norm_g, W.w_in, W.sgu_g, W.w_s, W.b_s = norm_g, w_in, sgu_g, w_s, b_s
        W.qn_b, W.kn_b, W.qn_c, W.kn_c = qn_b, kn_b, qn_c, kn_c
        W.lam = (lam_q1, lam_k1, lam_q2, lam_k2)
        W.subln_g, W.conv_w, W.w_out = subln_g, conv_w, w_out

        for l in range(depth):
            xsrc = x_in if l == 0 else D.x1
            ydst = y_out if l == depth - 1 else D.x1
            for ph in (phase1, phase2, phase3, phase4)[:phases]:
                S.new_phase()
                for bk in banks:
                    bk[1].w = None
                    bk[1].r = {}
                with ExitStack() as es:
                    sb = mk_sb(nc, es)
                    ph(C, S, sb, W, D, l, xsrc, ydst, seqs, NT)
                    S.barrier()
                    S.replay(nc, sems)
    return nc


def phase1(C, S, sb, W, D, l, xsrc, ydst, seqs, NT):
    nc, P = C.nc, C.P
    P.set_rot(range(8))
    build_consts(C, S, sb, full=False)
    cb = C.constbuf
    wb = sb("wb", [128, 8, PROJ_W], BF16)
    wbuf = Buf()
    g8 = sb("g8", [128, 8], F32)
    DMA(S, g8[:], W.norm_g[l:l + 1, :].rearrange("o (k p) -> p (o k)", p=128), [], [wbuf], slow=True)
    WST = 640
    wst = [(sb(f"wst{i}", [128, WST], F32), Buf()) for i in range(2)]
    n = 0
    for kc in range(8):
        for c0 in range(0, PROJ_W, WST):
            st, stb = wst[n % 2]
            DMA(S, st[:], W.w_in[l, kc * 128:(kc + 1) * 128, c0:c0 + WST], [], [stb])
            if n % 2 == 0:
                TS(S, "dve", wb[:, kc, c0:c0 + WST], st[:], g8[:, kc:kc + 1], ALU.mult, [stb, wbuf], [wbuf])
            else:
                ACT(S, wb[:, kc, c0:c0 + WST], st[:], AF.Copy, [stb, wbuf], [wbuf], scale=g8[:, kc:kc + 1])
            n += 1
    wsT = sb("wsT", [128, NHEAD, 128], BF16)
    for g in range(NHEAD):
        st, stb = wst[n % 2]
        n += 1
        DMA(S, st[:, 0:128], W.w_s[l, g], [], [stb])
        bk, bb = P.next()
        TRG(S, [(bk[:, 0:128], st[:, 0:128], C.identf[:])], [stb, cb], [bb])
        CP(S, "dve", wsT[:, g, :], bk[:, 0:128], [bb], [wbuf])
    bsT = sb("bsT", [128, NHEAD], F32)
    DMA(S, bsT[:], W.b_s[l].rearrange("g t -> t g"), [], [wbuf], slow=True)
    sgu_bc = sb("sgu_bc", [128, BW], F32)
    DMA(S, sgu_bc[:], W.sgu_g[l:l + 1, :].to_broadcast([128, BW]), [], [wbuf])
    g64 = sb("g64", [128, 4, 64], F32)
    DMA(S, g64[:, 0, :], W.qn_b[l:l + 1, :].to_broadcast([128, 64]), [], [wbuf])
    DMA(S, g64[:, 1, :], W.kn_b[l:l + 1, :].to_broadcast([128, 64]), [], [wbuf])
    DMA(S, g64[:, 2, 0:32], W.qn_c[l:l + 1, :].to_broadcast([128, 32]), [], [wbuf])
    DMA(S, g64[:, 3, 0:32], W.kn_c[l:l + 1, :].to_broadcast([128, 32]), [], [wbuf])
    gq_b = sb("gq_b", [128, BW], F32)
    gk_b = sb("gk_b", [128, BW], F32)
    gq_c = sb("gq_c", [128, BW], F32)
    gk_c = sb("gk_c", [128, BW], F32)
    TS(S, "dve", gq_b[:].rearrange("p (h e) -> p h e", e=64), g64[:, 0, :].unsqueeze(1).to_broadcast([128, 6, 64]),
       0.125, ALU.mult, [wbuf], [wbuf])
    TS(S, "dve", gk_b[:].rearrange("p (h e) -> p h e", e=64), g64[:, 1, :].unsqueeze(1).to_broadcast([128, 6, 64]),
       1.0, ALU.mult, [wbuf], [wbuf])
    TS(S, "dve", gq_c[:].rearrange("p (h e) -> p h e", e=32), g64[:, 2, 0:32].unsqueeze(1).to_broadcast([128, 12, 32]),
       1.0 / math.sqrt(32.0), ALU.mult, [wbuf], [wbuf])
    TS(S, "dve", gk_c[:].rearrange("p (h e) -> p h e", e=32), g64[:, 3, 0:32].unsqueeze(1).to_broadcast([128, 12, 32]),
       1.0, ALU.mult, [wbuf], [wbuf])

    def slots(name, shape, dt, k):
        return [(sb(f"{name}{i}", shape, dt), Buf()) for i in range(k)]
    xs = slots("xs", [128, D_MODEL], F32, 2)
    junk = sb("junk", [128, D_MODEL], BF16)
    junkb = Buf()
    hb = slots("hb", [128, D_MODEL], BF16, 4)
    hT = slots("hT", [128, 8, 512], BF16, 2)
    st4 = slots("st4", [128, 4], F32, 2)
    sqf = slots("sqf", [128, BW], F32, 2)
    s12 = slots("s12", [128, 3, 12], F32, 2)
    vn = slots("vn", [128, BW], BF16, 1)
    sg = slots("sg", [128, BW], F32, 1)
    t1 = slots("t1", [128, BW], F32, 1)
    t2 = slots("t2", [128, BW], F32, 1)
    mixA = slots("mixA", [128, BW], BF16, 1)
    qt = slots("qt", [128, BW], F32, 2)
    qn = slots("qn", [128, BW], BF16, 4)
    stg_T = {nm: slots(f"stg_{nm}", [128, 3, 512], BF16, 1)[0] for nm in ("mixa", "qbt", "kbt", "qct", "kct")}
    stg_v = {nm: slots(f"stg_{nm}", [128, 4, BW], BF16, 1)[0] for nm in ("vb", "vc")}
    fm_b = slots("fm_b", [128, 512], BF16, 3)
    fm_f = slots("fm_f", [128, 512], F32, 2)
    tmpf = slots("tmpf", [128, 512], F32, 2)
    cnt = {"x": 0, "h": 0, "st": 0, "sq": 0, "s12": 0, "vn": 0, "sg": 0, "t1": 0, "t2": 0, "mixA": 0, "qt": 0, "qn": 0,
           "fmb": 0, "fmf": 0, "tmpf": 0}

    def nxt(lst, key):
        v = lst[cnt[key] % len(lst)]
        cnt[key] += 1
        return v

    ngroups = NT // 512

    def prep(g):
        res = []
        for j in range(4):
            t0 = g * 512 + j * 128
            x_t, x_b = nxt(xs, "x")
            DMA(S, x_t[:], xsrc[t0:t0 + 128, :], [], [x_b])
            s_t, s_b = nxt(st4, "st")
            ACT(S, junk[:], x_t[:], AF.Square, [x_b], [junkb, s_b], accum_out=s_t[:, 0:1])
            ACT(S, s_t[:, 1:2], s_t[:, 0:1], AF.Sqrt, [s_b], [s_b], scale=1.0 / D_MODEL, bias=EPS)
            S.emit("dve", lambda e, s_t=s_t: e.reciprocal(out=s_t[:, 2:3], in_=s_t[:, 1:2]), [s_b], [s_b])
            h_t, h_b = nxt(hb, "h")
            ACT(S, h_t[:], x_t[:], AF.Copy, [x_b, s_b], [h_b], scale=s_t[:, 2:3])
            res.append((h_t, h_b))
        return res

    def transposes(g, hs):
        hT_t, hT_b = hT[g % 2]
        for j, (h_t, h_b) in enumerate(hs):
            bk, bb = P.next()
            bkb = bk[:].bitcast(BF16)
            TRG(S, [(bkb[:, kc * 128:(kc + 1) * 128], h_t[:, kc * 128:(kc + 1) * 128], C.identb[:]) for kc in range(8)],
                [h_b, cb], [bb])
            CP(S, "dve" if j % 2 == 0 else "act", hT_t[:, :, j * 128:(j + 1) * 128],
               bkb[:, 0:1024].rearrange("p (k t) -> p k t", t=128), [bb], [hT_b])
        return hT_t, hT_b

    def proj_tm(hT_t, hT_b, j, piece):
        bk, bb = P.next()
        MMG(S, [(bk[:, 0:BW], hT_t[:, kc, j * 128:(j + 1) * 128], wb[:, kc, piece * BW:(piece + 1) * BW], kc == 0, kc == 7)
                for kc in range(8)], [hT_b, wbuf], [bb])
        return bk, bb

    def headnorm(bk, bb, nh, gain, eng2):
        hd = BW // nh
        sq_t, sq_b = nxt(sqf, "sq")
        ACT(S, sq_t[:], bk[:, 0:BW], AF.Square, [bb], [sq_b])
        s_t, s_b = nxt(s12, "s12")
        S.emit("dve", lambda e: e.tensor_reduce(out=s_t[:, 0, 0:nh], in_=sq_t[:].rearrange("p (h e) -> p h e", e=hd),
                                                axis=AX.X, op=ALU.add), [sq_b], [s_b])
        ACT(S, s_t[:, 1, 0:nh], s_t[:, 0, 0:nh], AF.Sqrt, [s_b], [s_b], scale=1.0 / hd, bias=EPS)
        S.emit("dve", lambda e: e.reciprocal(out=s_t[:, 2, 0:nh], in_=s_t[:, 1, 0:nh]), [s_b], [s_b])
        q_t, q_b = nxt(qt, "qt")
        TT(S, "dve", q_t[:].rearrange("p (h e) -> p h e", e=hd), bk[:, 0:BW].rearrange("p (h e) -> p h e", e=hd),
           s_t[:, 2, 0:nh].unsqueeze(2).to_broadcast([128, nh, hd]), ALU.mult, [bb, s_b], [q_b])
        n_t, n_b = nxt(qn, "qn")
        TT(S, eng2, n_t[:], q_t[:], gain[:], ALU.mult, [q_b, wbuf], [n_b])
        return n_t, n_b

    def tr3(src_t, src_b, stg, j, eng):
        st_t, st_b = stg
        bk, bb = P.next()
        bkb = bk[:].bitcast(BF16)
        TRG(S, [(bkb[:, c * 128:(c + 1) * 128], src_t[:, c * 128:(c + 1) * 128], C.identb[:]) for c in range(3)],
            [src_b, cb], [bb])
        CP(S, eng, st_t[:, :, j * 128:(j + 1) * 128], bkb[:, 0:384].rearrange("p (c t) -> p c t", t=128), [bb], [st_b])

    hs = prep(0)
    for g in range(ngroups):
        tok0 = g * 512
        hT_t, hT_b = transposes(g, hs)
        if g + 1 < ngroups:
            hs = prep(g + 1)
        for j in range(4):
            bk_v, bb_v = proj_tm(hT_t, hT_b, j, P_AV)
            s_t, s_b = nxt(st4, "st")
            sq_t, sq_b = nxt(sqf, "sq")
            ACT(S, sq_t[:], bk_v[:, 0:BW], AF.Square, [bb_v], [sq_b, s_b], accum_out=s_t[:, 0:1])
            ACT(S, s_t[:, 1:2], s_t[:, 0:1], AF.Sqrt, [s_b], [s_b], scale=1.0 / BW, bias=EPS)
            S.emit("dve", lambda e, s_t=s_t: e.reciprocal(out=s_t[:, 2:3], in_=s_t[:, 1:2]), [s_b], [s_b])
            vn_t, vn_b = nxt(vn, "vn")
            STT(S, vn_t[:], bk_v[:, 0:BW], s_t[:, 2:3], sgu_bc[:], ALU.mult, ALU.mult, [bb_v, s_b, wbuf], [vn_b])
            bk, bb = proj_tm(hT_t, hT_b, j, P_BQ)
            qb_n = headnorm(bk, bb, 6, gq_b, "pool")
            bk, bb = proj_tm(hT_t, hT_b, j, P_BK)
            kb_n = headnorm(bk, bb, 6, gk_b, "pool")
            bk_u, bb_u = proj_tm(hT_t, hT_b, j, P_AU)
            bk_g, bb_g = proj_tm(hT_t, hT_b, j, P_AG)
            sg_t, sg_b = nxt(sg, "sg")
            ACT(S, sg_t[:], bk_g[:, 0:BW], AF.Silu, [bb_g], [sg_b])
            bk_m, bb_m = P.next()
            MMG(S, [(bk_m[:, gg * 64:(gg + 1) * 64], wsT[:, gg, :], vn_t[:, gg * 64:(gg + 1) * 64], True, True)
                    for gg in range(NHEAD)], [vn_b, wbuf], [bb_m])
            t1_t, t1_b = nxt(t1, "t1")
            TT(S, "dve", t1_t[:].rearrange("p (h e) -> p h e", e=64), bk_m[:, 0:BW].rearrange("p (h e) -> p h e", e=64),
               bsT[:].unsqueeze(2).to_broadcast([128, 6, 64]), ALU.add, [bb_m, wbuf], [t1_b])
            t2_t, t2_b = nxt(t2, "t2")
            TT(S, "dve", t2_t[:], bk_u[:, 0:BW], t1_t[:], ALU.mult, [bb_u, t1_b], [t2_b])
            ma_t, ma_b = nxt(mixA, "mixA")
            TT(S, "pool", ma_t[:], t2_t[:], sg_t[:], ALU.mult, [t2_b, sg_b], [ma_b])
            bk, bb = proj_tm(hT_t, hT_b, j, P_CQ)
            qc_n = headnorm(bk, bb, 12, gq_c, "pool")
            tr3(qb_n[0], qb_n[1], stg_T["qbt"], j, "dve")
            tr3(kb_n[0], kb_n[1], stg_T["kbt"], j, "act")
            bk, bb = proj_tm(hT_t, hT_b, j, P_CK)
            kc_n = headnorm(bk, bb, 12, gk_c, "pool")
            bk, bb = proj_tm(hT_t, hT_b, j, P_BV)
            CP(S, "act", stg_v["vb"][0][:, j, :], bk[:, 0:BW], [bb], [stg_v["vb"][1]])
            tr3(ma_t, ma_b, stg_T["mixa"], j, "dve")
            bk, bb = proj_tm(hT_t, hT_b, j, P_CV)
            CP(S, "act", stg_v["vc"][0][:, j, :], bk[:, 0:BW], [bb], [stg_v["vc"][1]])
            tr3(qc_n[0], qc_n[1], stg_T["qct"], j, "dve")
            tr3(kc_n[0], kc_n[1], stg_T["kct"], j, "act")
        for nm, dst, r0 in (("mixa", D.mixt, 0), ("qbt", D.qbt, 0), ("kbt", D.kbt, 0), ("qct", D.qct, 0), ("kct", D.kct, 0)):
            st_t, st_b = stg_T[nm]
            DMA(S, dst[r0:r0 + BW, tok0:tok0 + 512].rearrange("(c p) t -> p c t", p=128), st_t[:], [st_b], [])
        for nm, dst in (("vb", D.vb), ("vc", D.vc)):
            st_t, st_b = stg_v[nm]
            DMA(S, dst[tok0:tok0 + 512, :].rearrange("(j p) c -> p j c", p=128), st_t[:], [st_b], [])

        def proj_fm(piece, ch):
            bk, bb = P.next()
            c0 = piece * BW + ch * 128
            MMG(S, [(bk[:, 0:512], wb[:, kc, c0:c0 + 128], hT_t[:, kc, :], kc == 0, kc == 7) for kc in range(8)],
                [hT_b, wbuf], [bb])
            return bk, bb

        for piece, dst in ((P_BG, D.gbt), (P_CG, D.gct)):
            for ch in range(3):
                bk, bb = proj_fm(piece, ch)
                f_t, f_b = nxt(fm_b, "fmb")
                ACT(S, f_t[:], bk[:, 0:512], AF.Silu, [bb], [f_b])
                DMA(S, dst[ch * 128:(ch + 1) * 128, tok0:tok0 + 512], f_t[:], [f_b], [])
        for ch in range(3):
            bk_i, bb_i = proj_fm(P_DI, ch)
            bk_c, bb_c = proj_fm(P_DC, ch)
            tm_t, tm_b = nxt(tmpf, "tmpf")
            CP(S, "act", tm_t[:], bk_i[:, 0:512], [bb_i], [tm_b])
            f_t, f_b = nxt(fm_f, "fmf")
            TT(S, "dve", f_t[:], bk_c[:, 0:512], tm_t[:], ALU.mult, [bb_c, tm_b], [f_b])
            DMA(S, D.zt[ch * 128:(ch + 1) * 128, tok0:tok0 + 512], f_t[:], [f_b], [])
            bk_g, bb_g = proj_fm(P_DG, ch)
            bk_b, bb_b = proj_fm(P_DB, ch)
            tm_t, tm_b = nxt(tmpf, "tmpf")
            ACT(S, tm_t[:], bk_g[:, 0:512], AF.Silu, [bb_g], [tm_b])
            f_t, f_b = nxt(fm_f, "fmf")
            TT(S, "dve", f_t[:], bk_b[:, 0:512], tm_t[:], ALU.mult, [bb_b, tm_b], [f_b])
            DMA(S, D.gdt[ch * 128:(ch + 1) * 128, tok0:tok0 + 512], f_t[:], [f_b], [])


def load_head(C, S, slot, qsrc, ksrc, vsrc, h, s0, L):
    (q_t, k_t, v_t, hb) = slot
    for r in range(2):
        DMA(S, q_t[64 * r:64 * r + 64, 0:L], qsrc[h * 64:(h + 1) * 64, s0:s0 + L], [], [hb])
        DMA(S, k_t[64 * r:64 * r + 64, 0:L], ksrc[h * 64:(h + 1) * 64, s0:s0 + L], [], [hb])
    nb = L // 128
    for b0 in range(0, nb, 16):
        b1 = min(nb, b0 + 16)
        DMA(S, v_t[:, b0:b1, 0:64],
            vsrc[s0 + b0 * 128:s0 + b1 * 128, h * 64:(h + 1) * 64].rearrange("(b p) e -> p b e", p=128), [], [hb])


def alloc_attn(C, S, sb, maxL, ex_dt=F32):
    A = Ctx()
    A.slots = []
    for i in range(2):
        q_t = sb(f"aq{i}", [128, maxL], BF16)
        k_t = sb(f"ak{i}", [128, maxL], BF16)
        v_t = sb(f"av{i}", [128, maxL // 128, VPAD], BF16)
        hb = Buf()
        S.emit("pool", lambda e, v_t=v_t: e.memset(v_t[:, :, 64:VPAD], 1.0), [], [hb])
        A.slots.append((q_t, k_t, v_t, hb))
    A.pT = [(sb(f"pT{i}", [128, 512], BF16), Buf()) for i in range(8)]
    A.ex = [(sb(f"ex{i}", [128, 512], ex_dt), Buf()) for i in range(6 if ex_dt == BF16 else 4)]
    A.npT = 0
    A.nex = 0
    return A


def pe_warmup(S, P, lhsT, rhs, n):
    bk, bb = P.next()
    MMG(S, [(bk[:, 0:512], lhsT, rhs, True, True) for _ in range(n)], [], [bb])


class Pipe:
    def __init__(self, look):
        self.look = look
        self.t = 0
        self.n = 0
        self.pend = []

    def at(self, delay, fn):
        self.pend.append((self.t + delay, self.n, fn))
        self.n += 1

    def tick(self):
        self.t += 1
        while True:
            self.pend.sort(key=lambda p: (p[0], p[1]))
            if not self.pend or self.pend[0][0] > self.t:
                break
            self.pend.pop(0)[2]()

    def block(self, qk_fn, pv_fn):
        qk_fn()
        self.at(self.look + 1, pv_fn)
        self.tick()

    def flush(self):
        while self.pend:
            self.tick()


def phase2(C, S, sb, W, D, l, xsrc, ydst, seqs, NT):
    nc, P = C.nc, C.P
    build_consts(C, S, sb)
    cb = C.constbuf
    maxL = max(L for _, L in seqs)
    A = alloc_attn(C, S, sb, maxL, ex_dt=BF16)
    mtab = sb("mtab", [128, TW], F32)
    mt2 = sb("mt2", [128, TW], F32)
    mi = C.iot

    def le(out, lim):
        TS(S, "dve", out, C.absd[:], -1.0, ALU.mult, [cb], [cb], s2=lim + 1.0, op1=ALU.add)
        TS(S, "dve", out, out, 0.0, ALU.max, [cb], [cb], s2=1.0, op1=ALU.min)
    le(mtab[:], 64.0)
    mt3 = C.negd
    for dil, lim in ((4, 256.0), (16, 1024.0)):
        TS(S, "dve", mt2[:], C.absd[:], 1.0 / dil, ALU.mult, [cb], [cb])
        CP(S, "dve", mi[:], mt2[:], [cb], [cb])
        CP(S, "dve", mt3[:], mi[:], [cb], [cb])
        TT(S, "dve", mt2[:], mt2[:], mt3[:], ALU.is_equal, [cb], [cb])
        le(mt3[:], lim)
        TT(S, "dve", mt2[:], mt2[:], mt3[:], ALU.mult, [cb], [cb])
        TT(S, "dve", mtab[:], mtab[:], mt2[:], ALU.add, [cb], [cb])
    rtab = [(sb(f"rtab{i}", [128, TW], BF16), Buf()) for i in range(2)]
    tmpb = Buf()
    gate = [(sb(f"gate{i}", [64, 512], BF16), Buf()) for i in range(3)]
    rc = [(sb(f"rc{i}", [128, 512], F32), Buf()) for i in range(2)]
    tq = [(sb(f"tq{i}", [64, 512], F32), Buf()) for i in range(2)]
    ob = [(sb(f"ob{i}", [64, 512], BF16), Buf()) for i in range(2)]
    P.set_rot([0, 1, 2, 3, 6, 7])
    accs = [P.banks[4], P.banks[5]]
    pipe = Pipe(2)
    units = [(s0, L, h) for (s0, L) in seqs for h in range(NHEAD)]
    load_head(C, S, A.slots[0], D.qbt, D.kbt, D.vb, units[0][2], units[0][0], units[0][1])
    nq = 0
    nblkc = [0]
    for ui, (s0, L, h) in enumerate(units):
        slot = A.slots[ui % 2]
        q_t, k_t, v_t, hb = slot
        r_t, r_b = rtab[ui % 2]
        ACT(S, C.negd[:], C.absd[:], AF.Exp, [cb], [tmpb], scale=-SLOPES[h])
        TT(S, "pool", r_t[:], C.negd[:], mtab[:], ALU.mult, [tmpb, cb], [r_b])
        nblk = L // 128
        for qi in range(L // 512):
            q0 = qi * 512
            acc, accb = accs[nq % 2]
            g_t, g_b = gate[nq % 3]
            DMA(S, g_t[:], D.gbt[h * 64:(h + 1) * 64, s0 + q0:s0 + q0 + 512], [], [g_b])
            kbs = list(range(max(0, q0 // 128 - 8), min(nblk, q0 // 128 + 4 + 8)))
            dmax_b = SKIP_T / SLOPES[h]
            kbs = [kb for kb in kbs if max(q0 - kb * 128 - 127, kb * 128 - q0 - 511, 0) <= dmax_b]
            rc_t, rc_b = rc[nq % 2]
            t_t, t_b = tq[nq % 2]
            o_t, o_b = ob[nq % 2]
            dst = D.mixt[BW + h * 64:BW + (h + 1) * 64, s0 + q0:s0 + q0 + 512]

            def epiA(acc=acc, accb=accb, rc_t=rc_t, rc_b=rc_b):
                ACT(S, rc_t[64:65, :], acc[64:65, 0:512], AF.Ln, [accb], [rc_b])
                ACT(S, rc_t[64:65, :], rc_t[64:65, :], AF.Exp, [rc_b], [rc_b], scale=-1.0)

            def epiB(acc=acc, accb=accb, rc_t=rc_t, rc_b=rc_b, t_t=t_t, t_b=t_b, o_t=o_t, o_b=o_b,
                     g_t=g_t, g_b=g_b, dst=dst):
                bc, bcb = P.next()
                MMG(S, [(bc[0:64, 0:512], C.ones_f[64:65, 0:64], rc_t[64:65, :], True, True)], [rc_b, cb], [bcb])
                TT(S, "dve", t_t[:], acc[0:64, 0:512], g_t[:], ALU.mult, [accb, g_b], [t_b])
                TT(S, "dve", o_t[:], bc[0:64, 0:512], t_t[:], ALU.mult, [bcb, t_b], [o_b])
                DMA(S, dst, o_t[:], [o_b], [])

            sbl = [kbs[i:i + 2] for i in range(0, len(kbs), 2)]
            for si, grp in enumerate(sbl):
                pts = []
                for _ in grp:
                    pts.append(A.pT[A.npT % len(A.pT)])
                    A.npT += 1
                first, last = (si == 0), (si == len(sbl) - 1)

                def qk(grp=grp, pts=pts, q0=q0, k_t=k_t, q_t=q_t, hb=hb, r_t=r_t, r_b=r_b):
                    bks = [P.next() for _ in grp]
                    MMG(S, [(bks[j][0][:, 0:512], k_t[64 * j:64 * j + 64, kb * 128:(kb + 1) * 128],
                             q_t[64 * j:64 * j + 64, q0:q0 + 512], True, True) for j, kb in enumerate(grp)],
                        [hb], [bb for _, bb in bks])
                    for j, kb in enumerate(grp):
                        o = kb * 128 - q0
                        bk, bb = bks[j]
                        p_t, p_b = pts[j]
                        ex_t, ex_b = A.ex[A.nex % len(A.ex)]
                        A.nex += 1
                        ACT(S, ex_t[:], bk[:, 0:512], AF.Exp, [bb], [ex_b])
                        eng = "pool" if (B_POOL_EVERY and nblkc[0] % B_POOL_EVERY == B_POOL_EVERY - 1) else "dve"
                        nblkc[0] += 1
                        TT(S, eng, p_t[:], ex_t[:], r_t[:, TC - o:TC - o + 512], ALU.mult, [ex_b, r_b], [p_b])

                def pv(grp=grp, pts=pts, first=first, last=last, acc=acc, accb=accb, v_t=v_t, hb=hb,
                       epiA=epiA, epiB=epiB):
                    MMG(S, [(acc[0:VPAD, 0:512], v_t[:, kb, :], pts[j][0][:], first and j == 0, last and j == len(grp) - 1)
                            for j, kb in enumerate(grp)], [hb] + [p_b for _, p_b in pts], [accb])
                    if last:
                        pipe.at(2, epiA)
                        pipe.at(4, epiB)
                pipe.block(qk, pv)
            nq += 1
            if qi == 0 and ui + 1 < len(units):
                n0, nL, nh = units[ui + 1]
                load_head(C, S, A.slots[(ui + 1) % 2], D.qbt, D.kbt, D.vb, nh, n0, nL)
    pipe.flush()


def load_head_c(C, S, slot, D, h, s0, L, sl, JLf, cb):
    (q_t, k_t, v_t, hb, kxb, pat, patf) = slot
    for c in range(2):
        r0 = h * 64 + c * 32
        DMA(S, q_t[64 * c + 3:64 * c + 35, 0:L], D.qct[r0:r0 + 32, s0:s0 + L], [], [hb])
        DMA(S, k_t[64 * c + 3:64 * c + 35, 0:L], D.kct[r0:r0 + 32, s0:s0 + L], [], [hb])
    nb = L // 128
    for b0 in range(0, nb, 16):
        b1 = min(nb, b0 + 16)
        DMA(S, v_t[:, b0:b1, 0:64],
            D.vc[s0 + b0 * 128:s0 + b1 * 128, h * 64:(h + 1) * 64].rearrange("(b p) e -> p b e", p=128), [], [hb])
    pb = Buf()
    TS(S, "dve", patf[0:1, 0, :], JLf[0:1, :], sl, ALU.mult, [cb], [pb])
    CP(S, "dve", pat[0:1, 0, :], patf[0:1, 0, :], [pb], [pb])
    CP(S, "dve", patf[0:1, 1, :], pat[0:1, 0, :], [pb], [pb])
    TT(S, "dve", patf[0:1, 2, :], patf[0:1, 0, :], patf[0:1, 1, :], ALU.subtract, [pb], [pb])
    CP(S, "dve", pat[0:1, 1, :], patf[0:1, 2, :], [pb], [pb])
    CP(S, "dve", patf[0:1, 1, :], pat[0:1, 1, :], [pb], [pb])
    TT(S, "dve", patf[0:1, 0, :], patf[0:1, 2, :], patf[0:1, 1, :], ALU.subtract, [pb], [pb])
    CP(S, "dve", pat[0:1, 2, :], patf[0:1, 0, :], [pb], [pb])
    nt = L // 512
    for c in range(2):
        for r in range(3):
            DMA(S, q_t[64 * c + r:64 * c + r + 1, 0:L].rearrange("p (t j) -> p t j", j=512),
                pat[0:1, r, :].unsqueeze(1).to_broadcast([1, nt, 512]), [pb], [hb])
        S.emit("dve", lambda e, c=c: e.memset(k_t[64 * c:64 * c + 3, 0:512], 0.0), [], [kxb[0]])
        if L > 512:
            S.emit("dve", lambda e, c=c: e.memset(k_t[64 * c:64 * c + 3, 512:L], 1.0), [], list(kxb[1:L // 512]))


def phase3(C, S, sb, W, D, l, xsrc, ydst, seqs, NT):
    nc, P = C.nc, C.P
    build_consts(C, S, sb, lo=TC - 384, width=896)
    cb = C.constbuf
    maxL = max(L for _, L in seqs)
    NB = maxL // 128
    lambda_init = 0.8 - 0.6 * math.exp(-0.3 * l)
    A = Ctx()
    A.slots = []
    for i in range(2):
        q_t = sb(f"aq{i}", [128, maxL], BF16)
        k_t = sb(f"ak{i}", [128, maxL], BF16)
        v_t = sb(f"av{i}", [128, maxL // 128, VPAD], BF16)
        pat = sb(f"pat{i}", [1, 3, 512], BF16)
        patf = sb(f"patf{i}", [1, 3, 512], F32)
        hb = Buf()
        kxb = [Buf() for _ in range(maxL // 512)]
        S.emit("pool", lambda e, v_t=v_t: e.memset(v_t[:, :, 64:VPAD], 1.0), [], [hb])
        if KPAD > 35:
            S.emit("dve", lambda e, q_t=q_t: e.memset(q_t[:], 0.0), [], [hb] + kxb)
            S.emit("pool", lambda e, k_t=k_t: e.memset(k_t[:], 0.0), [], [hb] + kxb)
        A.slots.append((q_t, k_t, v_t, hb, kxb, pat, patf))
    A.pT = [(sb(f"pT{i}", [128, 512], BF16), Buf()) for i in range(10)]
    A.ex = [(sb(f"ex{i}", [128, 512], F32), Buf()) for i in range(4)]
    A.npT = 0
    A.nex = 0
    tli = sb("tli", [128, NB + 1], I32)
    tri = sb("tri", [128, NB + 1], I32)
    jli = sb("jli", [128, 512], I32)
    TLf = sb("TLf", [128, NB + 1], F32)
    TRf = sb("TRf", [128, NB + 1], F32)
    JLf = sb("JLf", [128, 512], F32)
    S.emit("pool", lambda e: e.iota(tli[:], [[128, NB + 1]], base=0, channel_multiplier=-1), [], [cb])
    S.emit("pool", lambda e: e.iota(tri[:], [[128, NB + 1]], base=0, channel_multiplier=1), [], [cb])
    S.emit("pool", lambda e: e.iota(jli[:], [[1, 512]], base=0, channel_multiplier=0), [], [cb])
    for a, b in ((TLf, tli), (TRf, tri), (JLf, jli)):
        CP(S, "dve", a[:], b[:], [cb], [cb])
    lamv = sb("lamv", [64, 4, 32], F32)
    lams = sb("lams", [64, 8], F32)
    lamj = sb("lamj", [64, 32], F32)
    for i, src in enumerate(W.lam):
        DMA(S, lamv[:, i, :], src[l:l + 1, :].to_broadcast([64, 32]), [], [cb])
    for i in range(2):
        TT(S, "dve", lamj[:], lamv[:, 2 * i, :], lamv[:, 2 * i + 1, :], ALU.mult, [cb], [cb])
        S.emit("dve", lambda e, i=i: e.tensor_reduce(out=lams[:, i:i + 1], in_=lamj[:], axis=AX.X, op=ALU.add), [cb], [cb])
    ACT(S, lams[:, 2:4], lams[:, 0:2], AF.Exp, [cb], [cb])
    TT(S, "dve", lams[:, 4:5], lams[:, 3:4], lams[:, 2:3], ALU.subtract, [cb], [cb])
    TS(S, "dve", lams[:, 5:6], lams[:, 4:5], -lambda_init, ALU.add, [cb], [cb])
    subg = sb("subg", [64, 2], F32)
    DMA(S, subg[:, 0:1], W.subln_g[l:l + 1, :].rearrange("o e -> e o"), [], [cb], slow=True)
    TS(S, "dve", subg[:, 1:2], subg[:, 0:1], 1.0 - lambda_init, ALU.mult, [cb], [cb])
    neglam = lams[:, 5:6]

    rtab = [(sb(f"rtab{i}", [128, 896], F32), Buf()) for i in range(2)]
    bl = [sb(f"bl{i}", [128, NB + 1], F32) for i in range(2)]
    br = [sb(f"br{i}", [128, NB + 1], F32) for i in range(2)]
    gate = [(sb(f"gate{i}", [64, 512], BF16), Buf()) for i in range(3)]
    ocs = [(sb(f"ocs{i}", [65, 2, 512], F32), Buf()) for i in range(2)]
    rc = [(sb(f"rc{i}", [128, 2, 512], F32), Buf()) for i in range(2)]
    a01 = [(sb(f"a01{i}", [64, 2, 512], F32), Buf()) for i in range(2)]
    at = [(sb(f"at{i}", [64, 512], F32), Buf()) for i in range(2)]
    sqb = [(sb(f"sqb{i}", [64, 512], F32), Buf()) for i in range(2)]
    rs = [(sb(f"rs{i}", [64, 512], F32), Buf()) for i in range(2)]
    ob = [(sb(f"ob{i}", [64, 512], BF16), Buf()) for i in range(2)]
    P.set_rot([0, 1, 2, 3, 6, 7])
    accs = [(P.banks[4], P.banks[5]), (P.banks[4], P.banks[5])]
    pipe = Pipe(2)
    units = [(s0, L, h) for (s0, L) in seqs for h in range(NHEAD)]
    load_head_c(C, S, A.slots[0], D, units[0][2], units[0][0], units[0][1], SLOPES[units[0][2]], JLf, cb)
    nq = 0
    for ui, (s0, L, h) in enumerate(units):
        q_t, k_t, v_t, hb, kxb, pat, patf = A.slots[ui % 2]
        sl = SLOPES[h]
        r_t, r_b = rtab[ui % 2]
        ACT(S, r_t[:], C.absd[:, 0:896], AF.Exp, [cb], [r_b], scale=-sl)
        bl_t, br_t = bl[ui % 2], br[ui % 2]
        TS(S, "dve", bl_t[:], TLf[:], -sl, ALU.mult, [cb], [r_b])
        TS(S, "dve", br_t[:], TRf[:], -sl, ALU.mult, [cb], [r_b])
        nblk = L // 128
        dmax = SKIP_T / sl
        if WARM_N:
            pe_warmup(S, P, C.identb[:], A.pT[0][0][:], WARM_N)
        for qi in range(L // 512):
            q0 = qi * 512
            def next_signs(qn=qi + 1, k_t=k_t, kxb=kxb):
                for c in range(2):
                    S.emit("dve", lambda e, c=c: e.memset(k_t[64 * c:64 * c + 3, (qn - 1) * 512:qn * 512], -1.0),
                           [], [kxb[qn - 1]])
                    S.emit("dve", lambda e, c=c: e.memset(k_t[64 * c:64 * c + 3, qn * 512:(qn + 1) * 512], 0.0),
                           [], [kxb[qn]])
            g_t, g_b = gate[nq % 3]
            DMA(S, g_t[:], D.gct[h * 64:(h + 1) * 64, s0 + q0:s0 + q0 + 512], [], [g_b])
            oc_t, oc_b = ocs[nq % 2]
            rc_t, rc_b = rc[nq % 2]
            a_t, a_b = a01[nq % 2]
            at_t, at_b = at[nq % 2]
            sq_t, sq_b = sqb[nq % 2]
            rs_t, rs_b = rs[nq % 2]
            o_t, o_b = ob[nq % 2]
            acc2 = accs[nq % 2]
            dstm = D.mixt[2 * BW + h * 64:2 * BW + (h + 1) * 64, s0 + q0:s0 + q0 + 512]
            lefts = [kb for kb in range(nblk) if kb * 128 - q0 <= -128 and (q0 - kb * 128 - 127) <= dmax]
            diags = [kb for kb in range(nblk) if 0 <= kb * 128 - q0 < 512]
            rights = [kb for kb in range(nblk) if kb * 128 - q0 >= 512 and (kb * 128 - q0 - 511) <= dmax]
            sbs = [("D", kb) for kb in diags] + [("R", kb) for kb in rights] + [("L", kb) for kb in lefts]
            sign_at = min(len(sbs) - 1, 9)

            def epi0(oc_t=oc_t, oc_b=oc_b, acc2=acc2):
                for c in range(2):
                    CP(S, "dve", oc_t[:, c, :], acc2[c][0][0:65, 0:512], [acc2[c][1]], [oc_b])

            def epiA(oc_t=oc_t, oc_b=oc_b, rc_t=rc_t, rc_b=rc_b):
                ACT(S, rc_t[64:65, :, :], oc_t[64:65, :, :], AF.Ln, [oc_b], [rc_b])
                ACT(S, rc_t[64:65, :, :], rc_t[64:65, :, :], AF.Exp, [rc_b], [rc_b], scale=-1.0)

            def epiB(oc_t=oc_t, oc_b=oc_b, rc_t=rc_t, rc_b=rc_b, a_t=a_t, a_b=a_b, at_t=at_t, at_b=at_b,
                     sq_t=sq_t, sq_b=sq_b):
                for c in range(2):
                    bk, bb = P.next()
                    MMG(S, [(bk[0:64, 0:512], C.ones_f[64:65, 0:64], rc_t[64:65, c, :], True, True)], [rc_b, cb], [bb])
                    TT(S, "dve", a_t[:, c, :], oc_t[0:64, c, :], bk[0:64, 0:512], ALU.mult, [oc_b, bb], [a_b])
                STT(S, at_t[:], a_t[:, 1, :], neglam, a_t[:, 0, :], ALU.mult, ALU.add, [a_b, cb], [at_b])
                TT(S, "dve", sq_t[:], at_t[:], at_t[:], ALU.mult, [at_b], [sq_b])

            def epiC(sq_t=sq_t, sq_b=sq_b, rs_t=rs_t, rs_b=rs_b):
                bk, bb = P.next()
                MMG(S, [(bk[0:64, 0:512], C.ones_f[0:64, 0:64], sq_t[:], True, True)], [sq_b, cb], [bb])
                ACT(S, rs_t[:], bk[0:64, 0:512], AF.Ln, [bb], [rs_b], scale=1.0 / 64, bias=EPS)
                ACT(S, rs_t[:], rs_t[:], AF.Exp, [rs_b], [rs_b], scale=-0.5)

            def epiD(at_t=at_t, at_b=at_b, rs_t=rs_t, rs_b=rs_b, o_t=o_t, o_b=o_b, g_t=g_t, g_b=g_b, dstm=dstm):
                TT(S, "dve", at_t[:], at_t[:], rs_t[:], ALU.mult, [at_b, rs_b], [at_b])
                STT(S, o_t[:], at_t[:], subg[:, 1:2], g_t[:], ALU.mult, ALU.mult, [at_b, g_b, cb], [o_b])
                DMA(S, dstm, o_t[:], [o_b], [])

            for si, (kind, kb) in enumerate(sbs):
                pts = []
                for _ in range(2):
                    pts.append(A.pT[A.npT % len(A.pT)])
                    A.npT += 1
                first, last = (si == 0), (si == len(sbs) - 1)

                def qk(kind=kind, kb=kb, pts=pts, q0=q0, k_t=k_t, q_t=q_t, hb=hb, r_t=r_t, r_b=r_b,
                       bl_t=bl_t, br_t=br_t, kxb=kxb):
                    bks = [P.next() for _ in range(2)]
                    MMG(S, [(bks[c][0][:, 0:512], k_t[64 * c:64 * c + KPAD, kb * 128:(kb + 1) * 128],
                             q_t[64 * c:64 * c + KPAD, q0:q0 + 512], True, True) for _ in range(QK_REP) for c in range(2)],
                        [hb, kxb[kb // 4]], [bks[0][1], bks[1][1]])
                    o = kb * 128 - q0
                    for c in range(2):
                        bk, bb = bks[c]
                        p_t, p_b = pts[c]
                        if kind == "L":
                            n = (-o) // 128
                            ACT(S, p_t[:], bk[:, 0:512], AF.Exp, [bb, r_b], [p_b], bias=bl_t[:, n:n + 1])
                        elif kind == "R":
                            n = o // 128
                            ACT(S, p_t[:], bk[:, 0:512], AF.Exp, [bb, r_b], [p_b], bias=br_t[:, n:n + 1])
                        else:
                            ex_t, ex_b = A.ex[A.nex % len(A.ex)]
                            A.nex += 1
                            ACT(S, ex_t[:], bk[:, 0:512], AF.Exp, [bb], [ex_b])
                            TT(S, "pool" if c == 0 else "dve", p_t[:], ex_t[:], r_t[:, 384 - o:384 - o + 512], ALU.mult,
                               [ex_b, r_b], [p_b])

                def pv(kb=kb, pts=pts, first=first, last=last, acc2=acc2, v_t=v_t, hb=hb,
                       epi0=epi0, epiA=epiA, epiB=epiB, epiC=epiC, epiD=epiD):
                    MMG(S, [(acc2[c][0][0:VPAD, 0:512], v_t[:, kb, :], pts[c][0][:], first, last) for c in range(2)],
                        [hb, pts[0][1], pts[1][1]], [acc2[0][1], acc2[1][1]])
                    if last:
                        epi0()
                        pipe.at(2, epiA)
                        pipe.at(5, epiB)
                        pipe.at(8, epiC)
                        pipe.at(11, epiD)
                pipe.block(qk, pv)
                if si == sign_at and qi + 1 < L // 512:
                    next_signs()
                if WARM_Q and si == 2:
                    pe_warmup(S, P, C.identb[:], A.pT[0][0][:], WARM_Q)
            nq += 1
            if qi == 0 and ui + 1 < len(units):
                n0, nL, nh = units[ui + 1]
                load_head_c(C, S, A.slots[(ui + 1) % 2], D, nh, n0, nL, SLOPES[nh], JLf, cb)
    pipe.flush()


def phase4(C, S, sb, W, D, l, xsrc, ydst, seqs, NT):
    nc, P = C.nc, C.P
    P.set_rot(range(8))
    wo = sb("wo", [128, 12, D_MODEL], BF16)
    wbuf = Buf()
    wst = [(sb(f"wost{i}", [128, D_MODEL], F32), Buf()) for i in range(2)]
    for kc in range(12):
        st, stb = wst[kc % 2]
        DMA(S, st[:], W.w_out[l, kc * 128:(kc + 1) * 128, :], [], [stb])
        CP(S, ("dve", "act")[kc % 2], wo[:, kc, :], st[:], [stb], [wbuf])
    cw = sb("cw", [128, 3, 3], F32)
    for ch in range(3):
        DMA(S, cw[:, ch, :], W.conv_w[l, :, ch * 128:(ch + 1) * 128].rearrange("j p -> p j"), [], [wbuf], slow=True)
    mx = [(sb(f"mx{i}", [128, 12, 512], BF16), Buf()) for i in range(2)]
    zt = [(sb(f"zt{i}", [128, 3, 514], F32), Buf()) for i in range(2)]
    gd = [(sb(f"gd{i}", [128, 3, 512], F32), Buf()) for i in range(2)]
    cv = [(sb(f"cv{i}", [128, 3, 512], F32), Buf()) for i in range(2)]
    xr = [(sb(f"xr{i}", [128, D_MODEL], F32), Buf()) for i in range(3)]
    yo = [(sb(f"yo{i}", [128, D_MODEL], F32), Buf()) for i in range(3)]
    starts = {s0 for s0, _ in seqs}
    ends = {s0 + L for s0, L in seqs}
    ngroups = NT // 512
    nt = 0

    def loads(g):
        tok0 = g * 512
        m_t, m_b = mx[g % 2]
        DMA(S, m_t[:, 0:9, :], D.mixt[0:3 * BW, tok0:tok0 + 512].rearrange("(c p) t -> p c t", p=128), [], [m_b])
        z_t, z_b = zt[g % 2]
        lo = 0 if tok0 in starts else 1
        hi = 0 if (tok0 + 512) in ends else 1
        if lo == 0:
            S.emit("pool", lambda e: e.memset(z_t[:, :, 0:1], 0.0), [], [z_b])
        if hi == 0:
            S.emit("pool", lambda e: e.memset(z_t[:, :, 513:514], 0.0), [], [z_b])
        DMA(S, z_t[:, :, 1 - lo:513 + hi],
            D.zt[:, tok0 - lo:tok0 + 512 + hi].rearrange("(c p) t -> p c t", p=128), [], [z_b])
        g_t, g_b = gd[g % 2]
        DMA(S, g_t[:], D.gdt[:, tok0:tok0 + 512].rearrange("(c p) t -> p c t", p=128), [], [g_b])

    loads(0)
    for g in range(ngroups):
        tok0 = g * 512
        if g + 1 < ngroups:
            loads(g + 1)
        m_t, m_b = mx[g % 2]
        z_t, z_b = zt[g % 2]
        g_t, g_b = gd[g % 2]
        c_t, c_b = cv[g % 2]
        for ch in range(3):
            TS(S, "dve", c_t[:, ch, :], z_t[:, ch, 0:512], cw[:, ch, 0:1], ALU.mult, [z_b, wbuf], [c_b])
            STT(S, c_t[:, ch, :], z_t[:, ch, 1:513], cw[:, ch, 1:2], c_t[:, ch, :], ALU.mult, ALU.add, [z_b, c_b, wbuf], [c_b])
            STT(S, c_t[:, ch, :], z_t[:, ch, 2:514], cw[:, ch, 2:3], c_t[:, ch, :], ALU.mult, ALU.add, [z_b, c_b, wbuf], [c_b])
            TT(S, "pool", m_t[:, 9 + ch, :], c_t[:, ch, :], g_t[:, ch, :], ALU.mult, [c_b, g_b], [m_b])
        for j in range(4):
            t0 = tok0 + j * 128
            x_t, x_b = xr[nt % 3]
            DMA(S, x_t[:], xsrc[t0:t0 + 128, :], [], [x_b])
            y_t, y_b = yo[nt % 3]
            for half in range(2):
                bk, bb = P.next()
                MMG(S, [(bk[:, 0:512], m_t[:, kc, j * 128:(j + 1) * 128], wo[:, kc, half * 512:(half + 1) * 512],
                         kc == 0, kc == 11) for kc in range(12)], [m_b, wbuf], [bb])
                TT(S, "dve", y_t[:, half * 512:(half + 1) * 512], bk[:, 0:512], x_t[:, half * 512:(half + 1) * 512],
                   ALU.add, [bb, x_b], [y_b])
            DMA(S, ydst[t0:t0 + 128, :], y_t[:], [y_b], [])
            nt += 1


_NC_CACHE = {}
SEQS = (8192, 4096, 4096)
WNAMES = ("norm_g", "w_in", "sgu_g", "w_s", "b_s", "qn_b", "kn_b", "qn_c", "kn_c",
          "lam_q1", "lam_k1", "lam_q2", "lam_k2", "subln_g", "conv_w", "w_out")


def kernel(x_prompt, x_sample, **w):
    x_prompt = np.asarray(x_prompt, dtype=np.float32)
    x_sample = np.asarray(x_sample, dtype=np.float32)
    if "nc" not in _NC_CACHE:
        _NC_CACHE["nc"] = build_program(SEQS)
    nc = _NC_CACHE["nc"]
    wmap = {k: np.ascontiguousarray(np.asarray(w[k], dtype=np.float32)) for k in WNAMES}
    in_maps = []
    for c in range(N_CORES):
        xc = np.concatenate([x_prompt[c], x_sample[2 * c], x_sample[2 * c + 1]], axis=0)
        m = {"x": np.ascontiguousarray(xc)}
        m.update(wmap)
        in_maps.append(m)
    res = run_bass_kernel_spmd(nc, in_maps, core_ids=list(range(N_CORES)))
    yp = np.empty_like(x_prompt)
    ys = np.empty_like(x_sample)
    for c in range(N_CORES):
        y = res.results[c]["y"]
        yp[c] = y[0:8192]
        ys[2 * c] = y[8192:12288]
        ys[2 * c + 1] = y[12288:16384]
    return (yp, ys)
```

```python
import math
from contextlib import ExitStack
import numpy as np
import concourse.bass as bass
import concourse.mybir as mybir
from concourse.bass_utils import run_bass_kernel_spmd

F32 = mybir.dt.float32
BF16 = mybir.dt.bfloat16
I32 = mybir.dt.int32
AF = mybir.ActivationFunctionType
ALU = mybir.AluOpType
AX = mybir.AxisListType

D_MODEL = 1024
DEPTH = 2
BW = 384
PROJ_W = 15 * BW
MIX_W = 4 * BW
EPS = 1e-6
NHEAD = 6
SLOPES = [2.0 ** (-8.0 * (i + 1) / NHEAD) for i in range(NHEAD)]
N_CORES = 8
(P_AU, P_AV, P_AG, P_BQ, P_BK, P_BV, P_BG, P_CQ, P_CK, P_CV, P_CG, P_DI, P_DB, P_DC, P_DG) = range(15)
TC = 1408
TW = 2944
SKIP_T = 60.0
VPAD = 128
KPAD = 64
WARM_N = 0
QK_REP = 2
WARM_Q = 0
B_POOL_EVERY = 0
SBK = 2


class Buf:
    __slots__ = ("w", "r")

    def __init__(self):
        self.w = None
        self.r = {}


class Sched:
    ENG = ("pe", "act", "dve", "pool", "sp")

    def __init__(self, n_dma=24):
        self.cnt = {e: 0 for e in ("pe", "act", "dve", "pool")}
        self.dma_val = [0] * n_dma
        self.dma_rr = 0
        self.seen = {e: {} for e in self.ENG}
        self.ops = {e: [] for e in self.ENG}

    def new_phase(self):
        self.ops = {e: [] for e in self.ENG}

    def emit(self, eng, fn, reads=(), writes=(), dma=False):
        waits = {}
        seen = self.seen[eng]

        def add(ev, raw):
            if ev is None:
                return
            k, val = ev
            if k[0] == 'e' and k[1] == eng:
                if not (raw and eng in ("act", "dve", "pool")):
                    return
            if seen.get(k, 0) >= val:
                return
            if waits.get(k, 0) < val:
                waits[k] = val

        for b in reads:
            add(b.w, True)
        for b in writes:
            add(b.w, False)
            for k, v in b.r.items():
                add((k, v), False)
        if dma:
            s = self.dma_rr
            self.dma_rr = (s + 1) % len(self.dma_val)
            prev = self.dma_val[s]
            if prev > 0:
                add((('d', s), prev), False)
            self.dma_val[s] = prev + 16
            ev = (('d', s), prev + 16)
        else:
            self.cnt[eng] += 1
            ev = (('e', eng), self.cnt[eng])
        for k, v in waits.items():
            seen[k] = v
        self.ops[eng].append((fn, list(waits.items()), ev))
        for b in reads:
            if b.r.get(ev[0], 0) < ev[1]:
                b.r[ev[0]] = ev[1]
        for b in writes:
            b.w = ev
            b.r = {}
        return ev

    def barrier(self):
        allev = [(('e', e), c) for e, c in self.cnt.items() if c > 0]
        allev += [(('d', s), v) for s, v in enumerate(self.dma_val) if v > 0]
        for eng in self.ENG:
            waits = {}
            for k, v in allev:
                if k[0] == 'e' and k[1] == eng:
                    continue
                if self.seen[eng].get(k, 0) >= v:
                    continue
                waits[k] = v
                self.seen[eng][k] = v
            self.ops[eng].append((None, list(waits.items()), None))

    def replay(self, nc, sems):
        with nc.Block() as block:
            def run(eng, e):
                for fn, waits, ev in self.ops[eng]:
                    for k, v in waits:
                        e.wait_ge(sems[k], v)
                    if fn is None:
                        continue
                    ins = fn(e)
                    if ev[0][0] == 'd':
                        ins.then_inc(sems[ev[0]], 16)
                    else:
                        ins.then_inc(sems[ev[0]], 1)

            @block.tensor
            def _(e):
                run("pe", e)

            @block.scalar
            def _(e):
                run("act", e)

            @block.vector
            def _(e):
                run("dve", e)

            @block.gpsimd
            def _(e):
                run("pool", e)

            @block.sync
            def _(e):
                run("sp", e)


def ACT(S, out, in_, func, R, W, bias=None, scale=None, accum_out=None):
    kw = {}
    if bias is not None:
        kw["bias"] = bias
    if scale is not None:
        kw["scale"] = scale
    if accum_out is not None:
        kw["accum_out"] = accum_out
    S.emit("act", lambda e: e.activation(out=out, in_=in_, func=func, **kw), R, W)


def TT(S, eng, out, in0, in1, op, R, W):
    S.emit(eng, lambda e: e.tensor_tensor(out=out, in0=in0, in1=in1, op=op), R, W)


def TS(S, eng, out, in0, s1, op0, R, W, s2=None, op1=None):
    if op1 is None:
        S.emit(eng, lambda e: e.tensor_scalar(out=out, in0=in0, scalar1=s1, scalar2=None, op0=op0), R, W)
    else:
        S.emit(eng, lambda e: e.tensor_scalar(out=out, in0=in0, scalar1=s1, scalar2=s2, op0=op0, op1=op1), R, W)


def STT(S, out, in0, scalar, in1, op0, op1, R, W):
    S.emit("dve", lambda e: e.scalar_tensor_tensor(out=out, in0=in0, scalar=scalar, in1=in1, op0=op0, op1=op1), R, W)


def CP(S, eng, out, in_, R, W):
    if eng == "act":
        S.emit(eng, lambda e: e.copy(out=out, in_=in_), R, W)
    else:
        S.emit(eng, lambda e: e.tensor_copy(out=out, in_=in_), R, W)


def MMG(S, mms, R, W):
    def fn(e):
        ins = None
        for (out, lhsT, rhs, st, sp) in mms:
            ins = e.matmul(out, lhsT=lhsT, rhs=rhs, start=st, stop=sp)
        return ins
    S.emit("pe", fn, R, W)


def TRG(S, trs, R, W):
    def fn(e):
        ins = None
        for (out, in_, ident) in trs:
            ins = e.transpose(out=out, in_=in_, identity=ident)
        return ins
    S.emit("pe", fn, R, W)


def DMA(S, out, in_, R, W, slow=False):
    if slow:
        S.emit("sp", lambda e: e.dma_start(out=out, in_=in_, allow_slow_non_contiguous=True), R, W, dma=True)
    else:
        S.emit("sp", lambda e: e.dma_start(out=out, in_=in_), R, W, dma=True)


class PS:
    def __init__(self, banks):
        self.banks = banks
        self.rot = list(range(8))
        self.i = 0

    def set_rot(self, idxs):
        self.rot = list(idxs)
        self.i = 0

    def next(self):
        b = self.banks[self.rot[self.i % len(self.rot)]]
        self.i += 1
        return b


class Ctx:
    pass


_uid = [0]


def mk_sb(nc, es):
    def sb(name, shape, dt):
        _uid[0] += 1
        return es.enter_context(nc.sbuf_tensor(f"{name}_{_uid[0]}", shape, dt))
    return sb


def build_consts(C, S, sb, full=True, lo=0, width=TW):
    nc = C.nc
    b = Buf()
    C.constbuf = b
    C.identb = sb("identb", [128, 128], BF16)
    C.identf = sb("identf", [128, 128], F32)
    C.ones_f = sb("ones_f", [128, 64], F32)
    S.emit("pool", lambda e: e.memset(C.ones_f[:], 1.0), [], [b])
    if not full:
        iot = sb("iot", [128, 128], I32)
        S.emit("pool", lambda e: e.iota(iot[:], [[-1, 128]], base=0, channel_multiplier=1), [], [b])
        TS(S, "dve", C.identb[:], iot[:], 0.0, ALU.is_equal, [b], [b])
        TS(S, "dve", C.identf[:], iot[:], 0.0, ALU.is_equal, [b], [b])
        return
    C.absd = sb("absd", [128, width], F32)
    iot = sb("iot", [128, width], I32)
    S.emit("pool", lambda e: e.iota(iot[:], [[-1, width]], base=TC - lo, channel_multiplier=1), [], [b])
    TS(S, "dve", C.identb[:], iot[:, TC - lo:TC - lo + 128], 0.0, ALU.is_equal, [b], [b])
    TS(S, "dve", C.identf[:], iot[:, TC - lo:TC - lo + 128], 0.0, ALU.is_equal, [b], [b])
    C.negd = sb("negd", [128, width], F32)
    CP(S, "dve", C.negd[:], iot[:], [b], [b])
    TS(S, "dve", C.absd[:], C.negd[:], -1.0, ALU.mult, [b], [b])
    TT(S, "dve", C.absd[:], C.absd[:], C.negd[:], ALU.max, [b], [b])
    C.iot = iot


def build_program(seq_lens, dump=False, depth=DEPTH, phases=4):
    nc = bass.Bass("TRN2", target_bir_lowering=False)
    NT = sum(seq_lens)
    seqs = []
    o = 0
    for L in seq_lens:
        seqs.append((o, L))
        o += L
    C = Ctx()
    C.nc = nc
    ext_in = lambda name, shape: nc.dram_tensor(name, shape, F32, kind="ExternalInput").ap()
    x_in = ext_in("x", [NT, D_MODEL])
    norm_g = ext_in("norm_g", [DEPTH, D_MODEL])
    w_in = ext_in("w_in", [DEPTH, D_MODEL, PROJ_W])
    sgu_g = ext_in("sgu_g", [DEPTH, BW])
    w_s = ext_in("w_s", [DEPTH, NHEAD, 128, 128])
    b_s = ext_in("b_s", [DEPTH, NHEAD, 128])
    qn_b = ext_in("qn_b", [DEPTH, 64])
    kn_b = ext_in("kn_b", [DEPTH, 64])
    qn_c = ext_in("qn_c", [DEPTH, 32])
    kn_c = ext_in("kn_c", [DEPTH, 32])
    lam_q1 = ext_in("lam_q1", [DEPTH, 32])
    lam_k1 = ext_in("lam_k1", [DEPTH, 32])
    lam_q2 = ext_in("lam_q2", [DEPTH, 32])
    lam_k2 = ext_in("lam_k2", [DEPTH, 32])
    subln_g = ext_in("subln_g", [DEPTH, 64])
    conv_w = ext_in("conv_w", [DEPTH, 3, BW])
    w_out = ext_in("w_out", [DEPTH, MIX_W, D_MODEL])
    y_out = nc.dram_tensor("y", [NT, D_MODEL], F32, kind="ExternalOutput").ap()

    skind = "ExternalOutput" if dump else "Internal"
    scr = lambda name, shape, dt: nc.dram_tensor(name, shape, dt, kind=skind).ap()
    D = Ctx()
    D.x1 = scr("x1", [NT, D_MODEL], F32)
    D.qbt = scr("qbt", [BW, NT], BF16)
    D.kbt = scr("kbt", [BW, NT], BF16)
    D.vb = scr("vb", [NT, BW], BF16)
    D.gbt = scr("gbt", [BW, NT], BF16)
    D.qct = scr("qct", [BW, NT], BF16)
    D.kct = scr("kct", [BW, NT], BF16)
    D.vc = scr("vc", [NT, BW], BF16)
    D.gct = scr("gct", [BW, NT], BF16)
    D.zt = scr("zt", [BW, NT], F32)
    D.gdt = scr("gdt", [BW, NT], F32)
    D.mixt = scr("mixt", [MIX_W, NT], BF16)

    S = Sched()
    with ExitStack() as top:
        sems = {}
        for e in ("pe", "act", "dve", "pool"):
            sems[('e', e)] = top.enter_context(nc.semaphore(f"sem_{e}"))
        for s in range(len(S.dma_val)):
            sems[('d', s)] = top.enter_context(nc.semaphore(f"sem_d{s}"))
        banks = []
        for i in range(8):
            t = top.enter_context(nc.psum_tensor(f"bank{i}", [128, 512], F32))
            banks.append((t, Buf()))
        P = PS(banks)
        C.P = P
        W = Ctx()
        W.norm_g, W.w_in, W.sgu_g, W.w_s, W.b_s = norm_g, w_in, sgu_g, w_s, b_s
        W.qn_b, W.kn_b, W.qn_c, W.kn_c = qn_b, kn_b, qn_c, kn_c
        W.lam = (lam_q1, lam_k1, lam_q2, lam_k2)
        W.subln_g, W.conv_w, W.w_out = subln_g, conv_w, w_out

        for l in range(depth):
            xsrc = x_in if l == 0 else D.x1
            ydst = y_out if l == depth - 1 else D.x1
            for ph in (phase1, phase2, phase3, phase4)[:phases]:
                S.new_phase()
                for bk in banks:
                    bk[1].w = None
                    bk[1].r = {}
                with ExitStack() as es:
                    sb = mk_sb(nc, es)
                    ph(C, S, sb, W, D, l, xsrc, ydst, seqs, NT)
                    S.barrier()
                    S.replay(nc, sems)
    return nc


def phase1(C, S, sb, W, D, l, xsrc, ydst, seqs, NT):
    nc, P = C.nc, C.P
    P.set_rot(range(8))
    build_consts(C, S, sb, full=False)
    cb = C.constbuf
    wb = sb("wb", [128, 8, PROJ_W], BF16)
    wbuf = Buf()
    g8 = sb("g8", [128, 8], F32)
    DMA(S, g8[:], W.norm_g[l:l + 1, :].rearrange("o (k p) -> p (o k)", p=128), [], [wbuf], slow=True)
    WST = 640
    wst = [(sb(f"wst{i}", [128, WST], F32), Buf()) for i in range(2)]
    n = 0
    for kc in range(8):
        for c0 in range(0, PROJ_W, WST):
            st, stb = wst[n % 2]
            DMA(S, st[:], W.w_in[l, kc * 128:(kc + 1) * 128, c0:c0 + WST], [], [stb])
            if n % 2 == 0:
                TS(S, "dve", wb[:, kc, c0:c0 + WST], st[:], g8[:, kc:kc + 1], ALU.mult, [stb, wbuf], [wbuf])
            else:
                ACT(S, wb[:, kc, c0:c0 + WST], st[:], AF.Copy, [stb, wbuf], [wbuf], scale=g8[:, kc:kc + 1])
            n += 1
    wsT = sb("wsT", [128, NHEAD, 128], BF16)
    for g in range(NHEAD):
        st, stb = wst[n % 2]
        n += 1
        DMA(S, st[:, 0:128], W.w_s[l, g], [], [stb])
        bk, bb = P.next()
        TRG(S, [(bk[:, 0:128], st[:, 0:128], C.identf[:])], [stb, cb], [bb])
        CP(S, "dve", wsT[:, g, :], bk[:, 0:128], [bb], [wbuf])
    bsT = sb("bsT", [128, NHEAD], F32)
    DMA(S, bsT[:], W.b_s[l].rearrange("g t -> t g"), [], [wbuf], slow=True)
    sgu_bc = sb("sgu_bc", [128, BW], F32)
    DMA(S, sgu_bc[:], W.sgu_g[l:l + 1, :].to_broadcast([128, BW]), [], [wbuf])
    g64 = sb("g64", [128, 4, 64], F32)
    DMA(S, g64[:, 0, :], W.qn_b[l:l + 1, :].to_broadcast([128, 64]), [], [wbuf])
    DMA(S, g64[:, 1, :], W.kn_b[l:l + 1, :].to_broadcast([128, 64]), [], [wbuf])
    DMA(S, g64[:, 2, 0:32], W.qn_c[l:l + 1, :].to_broadcast([128, 32]), [], [wbuf])
    DMA(S, g64[:, 3, 0:32], W.kn_c[l:l + 1, :].to_broadcast([128, 32]), [], [wbuf])
    gq_b = sb("gq_b", [128, BW], F32)
    gk_b = sb("gk_b", [128, BW], F32)
    gq_c = sb("gq_c", [128, BW], F32)
    gk_c = sb("gk_c", [128, BW], F32)
    TS(S, "dve", gq_b[:].rearrange("p (h e) -> p h e", e=64), g64[:, 0, :].unsqueeze(1).to_broadcast([128, 6, 64]),
       0.125, ALU.mult, [wbuf], [wbuf])
    TS(S, "dve", gk_b[:].rearrange("p (h e) -> p h e", e=64), g64[:, 1, :].unsqueeze(1).to_broadcast([128, 6, 64]),
       1.0, ALU.mult, [wbuf], [wbuf])
    TS(S, "dve", gq_c[:].rearrange("p (h e) -> p h e", e=32), g64[:, 2, 0:32].unsqueeze(1).to_broadcast([128, 12, 32]),
       1.0 / math.sqrt(32.0), ALU.mult, [wbuf], [wbuf])
    TS(S, "dve", gk_c[:].rearrange("p (h e) -> p h e", e=32), g64[:, 3, 0:32].unsqueeze(1).to_broadcast([128, 12, 32]),
       1.0, ALU.mult, [wbuf], [wbuf])

    def slots(name, shape, dt, k):
        return [(sb(f"{name}{i}", shape, dt), Buf()) for i in range(k)]
    xs = slots("xs", [128, D_MODEL], F32, 2)
    junk = sb("junk", [128, D_MODEL], BF16)
    junkb = Buf()
    hb = slots("hb", [128, D_MODEL], BF16, 4)
    hT = slots("hT", [128, 8, 512], BF16, 2)
    st4 = slots("st4", [128, 4], F32, 2)
    sqf = slots("sqf", [128, BW], F32, 2)
    s12 = slots("s12", [128, 3, 12], F32, 2)
    vn = slots("vn", [128, BW], BF16, 1)
    sg = slots("sg", [128, BW], F32, 1)
    t1 = slots("t1", [128, BW], F32, 1)
    t2 = slots("t2", [128, BW], F32, 1)
    mixA = slots("mixA", [128, BW], BF16, 1)
    qt = slots("qt", [128, BW], F32, 2)
    qn = slots("qn", [128, BW], BF16, 4)
    stg_T = {nm: slots(f"stg_{nm}", [128, 3, 512], BF16, 1)[0] for nm in ("mixa", "qbt", "kbt", "qct", "kct")}
    stg_v = {nm: slots(f"stg_{nm}", [128, 4, BW], BF16, 1)[0] for nm in ("vb", "vc")}
    fm_b = slots("fm_b", [128, 512], BF16, 3)
    fm_f = slots("fm_f", [128, 512], F32, 2)
    tmpf = slots("tmpf", [128, 512], F32, 2)
    cnt = {"x": 0, "h": 0, "st": 0, "sq": 0, "s12": 0, "vn": 0, "sg": 0, "t1": 0, "t2": 0, "mixA": 0, "qt": 0, "qn": 0,
           "fmb": 0, "fmf": 0, "tmpf": 0}

    def nxt(lst, key):
        v = lst[cnt[key] % len(lst)]
        cnt[key] += 1
        return v

    ngroups = NT // 512

    def prep(g):
        res = []
        for j in range(4):
            t0 = g * 512 + j * 128
            x_t, x_b = nxt(xs, "x")
            DMA(S, x_t[:], xsrc[t0:t0 + 128, :], [], [x_b])
            s_t, s_b = nxt(st4, "st")
            ACT(S, junk[:], x_t[:], AF.Square, [x_b], [junkb, s_b], accum_out=s_t[:, 0:1])
            ACT(S, s_t[:, 1:2], s_t[:, 0:1], AF.Sqrt, [s_b], [s_b], scale=1.0 / D_MODEL, bias=EPS)
            S.emit("dve", lambda e, s_t=s_t: e.reciprocal(out=s_t[:, 2:3], in_=s_t[:, 1:2]), [s_b], [s_b])
            h_t, h_b = nxt(hb, "h")
            ACT(S, h_t[:], x_t[:], AF.Copy, [x_b, s_b], [h_b], scale=s_t[:, 2:3])
            res.append((h_t, h_b))
        return res

    def transposes(g, hs):
        hT_t, hT_b = hT[g % 2]
        for j, (h_t, h_b) in enumerate(hs):
            bk, bb = P.next()
            bkb = bk[:].bitcast(BF16)
            TRG(S, [(bkb[:, kc * 128:(kc + 1) * 128], h_t[:, kc * 128:(kc + 1) * 128], C.identb[:]) for kc in range(8)],
                [h_b, cb], [bb])
            CP(S, "dve" if j % 2 == 0 else "act", hT_t[:, :, j * 128:(j + 1) * 128],
               bkb[:, 0:1024].rearrange("p (k t) -> p k t", t=128), [bb], [hT_b])
        return hT_t, hT_b

    def proj_tm(hT_t, hT_b, j, piece):
        bk, bb = P.next()
        MMG(S, [(bk[:, 0:BW], hT_t[:, kc, j * 128:(j + 1) * 128], wb[:, kc, piece * BW:(piece + 1) * BW], kc == 0, kc == 7)
                for kc in range(8)], [hT_b, wbuf], [bb])
        return bk, bb

    def headnorm(bk, bb, nh, gain, eng2):
        hd = BW // nh
        sq_t, sq_b = nxt(sqf, "sq")
        ACT(S, sq_t[:], bk[:, 0:BW], AF.Square, [bb], [sq_b])
        s_t, s_b = nxt(s12, "s12")
        S.emit("dve", lambda e: e.tensor_reduce(out=s_t[:, 0, 0:nh], in_=sq_t[:].rearrange("p (h e) -> p h e", e=hd),
                                                axis=AX.X, op=ALU.add), [sq_b], [s_b])
        ACT(S, s_t[:, 1, 0:nh], s_t[:, 0, 0:nh], AF.Sqrt, [s_b], [s_b], scale=1.0 / hd, bias=EPS)
        S.emit("dve", lambda e: e.reciprocal(out=s_t[:, 2, 0:nh], in_=s_t[:, 1, 0:nh]), [s_b], [s_b])
        q_t, q_b = nxt(qt, "qt")
        TT(S, "dve", q_t[:].rearrange("p (h e) -> p h e", e=hd), bk[:, 0:BW].rearrange("p (h e) -> p h e", e=hd),
           s_t[:, 2, 0:nh].unsqueeze(2).to_broadcast([128, nh, hd]), ALU.mult, [bb, s_b], [q_b])
        n_t, n_b = nxt(qn, "qn")
        TT(S, eng2, n_t[:], q_t[:], gain[:], ALU.mult, [q_b, wbuf], [n_b])
        return n_t, n_b

    def tr3(src_t, src_b, stg, j, eng):
        st_t, st_b = stg
        bk, bb = P.next()
        bkb = bk[:].bitcast(BF16)
        TRG(S, [(bkb[:, c * 128:(c + 1) * 128], src_t[:, c * 128:(c + 1) * 128], C.identb[:]) for c in range(3)],
            [src_b, cb], [bb])
        CP(S, eng, st_t[:, :, j * 128:(j + 1) * 128], bkb[:, 0:384].rearrange("p (c t) -> p c t", t=128), [bb], [st_b])

    hs = prep(0)
    for g in range(ngroups):
        tok0 = g * 512
        hT_t, hT_b = transposes(g, hs)
        if g + 1 < ngroups:
            hs = prep(g + 1)
        for j in range(4):
            bk_v, bb_v = proj_tm(hT_t, hT_b, j, P_AV)
            s_t, s_b = nxt(st4, "st")
            sq_t, sq_b = nxt(sqf, "sq")
            ACT(S, sq_t[:], bk_v[:, 0:BW], AF.Square, [bb_v], [sq_b, s_b], accum_out=s_t[:, 0:1])
            ACT(S, s_t[:, 1:2], s_t[:, 0:1], AF.Sqrt, [s_b], [s_b], scale=1.0 / BW, bias=EPS)
            S.emit("dve", lambda e, s_t=s_t: e.reciprocal(out=s_t[:, 2:3], in_=s_t[:, 1:2]), [s_b], [s_b])
            vn_t, vn_b = nxt(vn, "vn")
            STT(S, vn_t[:], bk_v[:, 0:BW], s_t[:, 2:3], sgu_bc[:], ALU.mult, ALU.mult, [bb_v, s_b, wbuf], [vn_b])
            bk, bb = proj_tm(hT_t, hT_b, j, P_BQ)
            qb_n = headnorm(bk, bb, 6, gq_b, "pool")
            bk, bb = proj_tm(hT_t, hT_b, j, P_BK)
            kb_n = headnorm(bk, bb, 6, gk_b, "pool")
            bk_u, bb_u = proj_tm(hT_t, hT_b, j, P_AU)
            bk_g, bb_g = proj_tm(hT_t, hT_b, j, P_AG)
            sg_t, sg_b = nxt(sg, "sg")
            ACT(S, sg_t[:], bk_g[:, 0:BW], AF.Silu, [bb_g], [sg_b])
            bk_m, bb_m = P.next()
            MMG(S, [(bk_m[:, gg * 64:(gg + 1) * 64], wsT[:, gg, :], vn_t[:, gg * 64:(gg + 1) * 64], True, True)
                    for gg in range(NHEAD)], [vn_b, wbuf], [bb_m])
            t1_t, t1_b = nxt(t1, "t1")
            TT(S, "dve", t1_t[:].rearrange("p (h e) -> p h e", e=64), bk_m[:, 0:BW].rearrange("p (h e) -> p h e", e=64),
               bsT[:].unsqueeze(2).to_broadcast([128, 6, 64]), ALU.add, [bb_m, wbuf], [t1_b])
            t2_t, t2_b = nxt(t2, "t2")
            TT(S, "dve", t2_t[:], bk_u[:, 0:BW], t1_t[:], ALU.mult, [bb_u, t1_b], [t2_b])
            ma_t, ma_b = nxt(mixA, "mixA")
            TT(S, "pool", ma_t[:], t2_t[:], sg_t[:], ALU.mult, [t2_b, sg_b], [ma_b])
            bk, bb = proj_tm(hT_t, hT_b, j, P_CQ)
            qc_n = headnorm(bk, bb, 12, gq_c, "pool")
            tr3(qb_n[0], qb_n[1], stg_T["qbt"], j, "dve")
            tr3(kb_n[0], kb_n[1], stg_T["kbt"], j, "act")
            bk, bb = proj_tm(hT_t, hT_b, j, P_CK)
            kc_n = headnorm(bk, bb, 12, gk_c, "pool")
            bk, bb = proj_tm(hT_t, hT_b, j, P_BV)
            CP(S, "act", stg_v["vb"][0][:, j, :], bk[:, 0:BW], [bb], [stg_v["vb"][1]])
            tr3(ma_t, ma_b, stg_T["mixa"], j, "dve")
            bk, bb = proj_tm(hT_t, hT_b, j, P_CV)
            CP(S, "act", stg_v["vc"][0][:, j, :], bk[:, 0:BW], [bb], [stg_v["vc"][1]])
            tr3(qc_n[0], qc_n[1], stg_T["qct"], j, "dve")
            tr3(kc_n[0], kc_n[1], stg_T["kct"], j, "act")
        for nm, dst, r0 in (("mixa", D.mixt, 0), ("qbt", D.qbt, 0), ("kbt", D.kbt, 0), ("qct", D.qct, 0), ("kct", D.kct, 0)):
            st_t, st_b = stg_T[nm]
            DMA(S, dst[r0:r0 + BW, tok0:tok0 + 512].rearrange("(c p) t -> p c t", p=128), st_t[:], [st_b], [])
        for nm, dst in (("vb", D.vb), ("vc", D.vc)):
            st_t, st_b = stg_v[nm]
            DMA(S, dst[tok0:tok0 + 512, :].rearrange("(j p) c -> p j c", p=128), st_t[:], [st_b], [])

        def proj_fm(piece, ch):
            bk, bb = P.next()
            c0 = piece * BW + ch * 128
            MMG(S, [(bk[:, 0:512], wb[:, kc, c0:c0 + 128], hT_t[:, kc, :], kc == 0, kc == 7) for kc in range(8)],
                [hT_b, wbuf], [bb])
            return bk, bb

        for piece, dst in ((P_BG, D.gbt), (P_CG, D.gct)):
            for ch in range(3):
                bk, bb = proj_fm(piece, ch)
                f_t, f_b = nxt(fm_b, "fmb")
                ACT(S, f_t[:], bk[:, 0:512], AF.Silu, [bb], [f_b])
                DMA(S, dst[ch * 128:(ch + 1) * 128, tok0:tok0 + 512], f_t[:], [f_b], [])
        for ch in range(3):
            bk_i, bb_i = proj_fm(P_DI, ch)
            bk_c, bb_c = proj_fm(P_DC, ch)
            tm_t, tm_b = nxt(tmpf, "tmpf")
            CP(S, "act", tm_t[:], bk_i[:, 0:512], [bb_i], [tm_b])
            f_t, f_b = nxt(fm_f, "fmf")
            TT(S, "dve", f_t[:], bk_c[:, 0:512], tm_t[:], ALU.mult, [bb_c, tm_b], [f_b])
            DMA(S, D.zt[ch * 128:(ch + 1) * 128, tok0:tok0 + 512], f_t[:], [f_b], [])
            bk_g, bb_g = proj_fm(P_DG, ch)
            bk_b, bb_b = proj_fm(P_DB, ch)
            tm_t, tm_b = nxt(tmpf, "tmpf")
            ACT(S, tm_t[:], bk_g[:, 0:512], AF.Silu, [bb_g], [tm_b])
            f_t, f_b = nxt(fm_f, "fmf")
            TT(S, "dve", f_t[:], bk_b[:, 0:512], tm_t[:], ALU.mult, [bb_b, tm_b], [f_b])
            DMA(S, D.gdt[ch * 128:(ch + 1) * 128, tok0:tok0 + 512], f_t[:], [f_b], [])


def load_head(C, S, slot, qsrc, ksrc, vsrc, h, s0, L):
    (q_t, k_t, v_t, hb) = slot
    for r in range(2):
        DMA(S, q_t[64 * r:64 * r + 64, 0:L], qsrc[h * 64:(h + 1) * 64, s0:s0 + L], [], [hb])
        DMA(S, k_t[64 * r:64 * r + 64, 0:L], ksrc[h * 64:(h + 1) * 64, s0:s0 + L], [], [hb])
    nb = L // 128
    for b0 in range(0, nb, 16):
        b1 = min(nb, b0 + 16)
        DMA(S, v_t[:, b0:b1, 0:64],
            vsrc[s0 + b0 * 128:s0 + b1 * 128, h * 64:(h + 1) * 64].rearrange("(b p) e -> p b e", p=128), [], [hb])


def alloc_attn(C, S, sb, maxL, ex_dt=F32):
    A = Ctx()
    A.slots = []
    for i in range(2):
        q_t = sb(f"aq{i}", [128, maxL], BF16)
        k_t = sb(f"ak{i}", [128, maxL], BF16)
        v_t = sb(f"av{i}", [128, maxL // 128, VPAD], BF16)
        hb = Buf()
        S.emit("pool", lambda e, v_t=v_t: e.memset(v_t[:, :, 64:VPAD], 1.0), [], [hb])
        A.slots.append((q_t, k_t, v_t, hb))
    A.pT = [(sb(f"pT{i}", [128, 512], BF16), Buf()) for i in range(8)]
    A.ex = [(sb(f"ex{i}", [128, 512], ex_dt), Buf()) for i in range(6 if ex_dt == BF16 else 4)]
    A.npT = 0
    A.nex = 0
    return A


def pe_warmup(S, P, lhsT, rhs, n):
    bk, bb = P.next()
    MMG(S, [(bk[:, 0:512], lhsT, rhs, True, True) for _ in range(n)], [], [bb])


class Pipe:
    def __init__(self, look):
        self.look = look
        self.t = 0
        self.n = 0
        self.pend = []

    def at(self, delay, fn):
        self.pend.append((self.t + delay, self.n, fn))
        self.n += 1

    def tick(self):
        self.t += 1
        while True:
            self.pend.sort(key=lambda p: (p[0], p[1]))
            if not self.pend or self.pend[0][0] > self.t:
                break
            self.pend.pop(0)[2]()

    def block(self, qk_fn, pv_fn):
        qk_fn()
        self.at(self.look + 1, pv_fn)
        self.tick()

    def flush(self):
        while self.pend:
            self.tick()


def phase2(C, S, sb, W, D, l, xsrc, ydst, seqs, NT):
    nc, P = C.nc, C.P
    build_consts(C, S, sb)
    cb = C.constbuf
    maxL = max(L for _, L in seqs)
    A = alloc_attn(C, S, sb, maxL, ex_dt=BF16)
    mtab = sb("mtab", [128, TW], F32)
    mt2 = sb("mt2", [128, TW], F32)
    mi = C.iot

    def le(out, lim):
        TS(S, "dve", out, C.absd[:], -1.0, ALU.mult, [cb], [cb], s2=lim + 1.0, op1=ALU.add)
        TS(S, "dve", out, out, 0.0, ALU.max, [cb], [cb], s2=1.0, op1=ALU.min)
    le(mtab[:], 64.0)
    mt3 = C.negd
    for dil, lim in ((4, 256.0), (16, 1024.0)):
        TS(S, "dve", mt2[:], C.absd[:], 1.0 / dil, ALU.mult, [cb], [cb])
        CP(S, "dve", mi[:], mt2[:], [cb], [cb])
        CP(S, "dve", mt3[:], mi[:], [cb], [cb])
        TT(S, "dve", mt2[:], mt2[:], mt3[:], ALU.is_equal, [cb], [cb])
        le(mt3[:], lim)
        TT(S, "dve", mt2[:], mt2[:], mt3[:], ALU.mult, [cb], [cb])
        TT(S, "dve", mtab[:], mtab[:], mt2[:], ALU.add, [cb], [cb])
    rtab = [(sb(f"rtab{i}", [128, TW], BF16), Buf()) for i in range(2)]
    tmpb = Buf()
    gate = [(sb(f"gate{i}", [64, 512], BF16), Buf()) for i in range(3)]
    rc = [(sb(f"rc{i}", [128, 512], F32), Buf()) for i in range(2)]
    tq = [(sb(f"tq{i}", [64, 512], F32), Buf()) for i in range(2)]
    ob = [(sb(f"ob{i}", [64, 512], BF16), Buf()) for i in range(2)]
    P.set_rot([0, 1, 2, 3, 6, 7])
    accs = [P.banks[4], P.banks[5]]
    pipe = Pipe(2)
    units = [(s0, L, h) for (s0, L) in seqs for h in range(NHEAD)]
    load_head(C, S, A.slots[0], D.qbt, D.kbt, D.vb, units[0][2], units[0][0], units[0][1])
    nq = 0
    nblkc = [0]
    for ui, (s0, L, h) in enumerate(units):
        slot = A.slots[ui % 2]
        q_t, k_t, v_t, hb = slot
        r_t, r_b = rtab[ui % 2]
        ACT(S, C.negd[:], C.absd[:], AF.Exp, [cb], [tmpb], scale=-SLOPES[h])
        TT(S, "pool", r_t[:], C.negd[:], mtab[:], ALU.mult, [tmpb, cb], [r_b])
        nblk = L // 128
        for qi in range(L // 512):
            q0 = qi * 512
            acc, accb = accs[nq % 2]
            g_t, g_b = gate[nq % 3]
            DMA(S, g_t[:], D.gbt[h * 64:(h + 1) * 64, s0 + q0:s0 + q0 + 512], [], [g_b])
            kbs = list(range(max(0, q0 // 128 - 8), min(nblk, q0 // 128 + 4 + 8)))
            dmax_b = SKIP_T / SLOPES[h]
            kbs = [kb for kb in kbs if max(q0 - kb * 128 - 127, kb * 128 - q0 - 511, 0) <= dmax_b]
            rc_t, rc_b = rc[nq % 2]
            t_t, t_b = tq[nq % 2]
            o_t, o_b = ob[nq % 2]
            dst = D.mixt[BW + h * 64:BW + (h + 1) * 64, s0 + q0:s0 + q0 + 512]

            def epiA(acc=acc, accb=accb, rc_t=rc_t, rc_b=rc_b):
                ACT(S, rc_t[64:65, :], acc[64:65, 0:512], AF.Ln, [accb], [rc_b])
                ACT(S, rc_t[64:65, :], rc_t[64:65, :], AF.Exp, [rc_b], [rc_b], scale=-1.0)

            def epiB(acc=acc, accb=accb, rc_t=rc_t, rc_b=rc_b, t_t=t_t, t_b=t_b, o_t=o_t, o_b=o_b,
                     g_t=g_t, g_b=g_b, dst=dst):
                bc, bcb = P.next()
                MMG(S, [(bc[0:64, 0:512], C.ones_f[64:65, 0:64], rc_t[64:65, :], True, True)], [rc_b, cb], [bcb])
                TT(S, "dve", t_t[:], acc[0:64, 0:512], g_t[:], ALU.mult, [accb, g_b], [t_b])
                TT(S, "dve", o_t[:], bc[0:64, 0:512], t_t[:], ALU.mult, [bcb, t_b], [o_b])
                DMA(S, dst, o_t[:], [o_b], [])

            sbl = [kbs[i:i + 2] for i in range(0, len(kbs), 2)]
            for si, grp in enumerate(sbl):
                pts = []
                for _ in grp:
                    pts.append(A.pT[A.npT % len(A.pT)])
                    A.npT += 1
                first, last = (si == 0), (si == len(sbl) - 1)

                def qk(grp=grp, pts=pts, q0=q0, k_t=k_t, q_t=q_t, hb=hb, r_t=r_t, r_b=r_b):
                    bks = [P.next() for _ in grp]
                    MMG(S, [(bks[j][0][:, 0:512], k_t[64 * j:64 * j + 64, kb * 128:(kb + 1) * 128],
                             q_t[64 * j:64 * j + 64, q0:q0 + 512], True, True) for j, kb in enumerate(grp)],
                        [hb], [bb for _, bb in bks])
                    for j, kb in enumerate(grp):
                        o = kb * 128 - q0
                        bk, bb = bks[j]
                        p_t, p_b = pts[j]
                        ex_t, ex_b = A.ex[A.nex % len(A.ex)]
                        A.nex += 1
                        ACT(S, ex_t[:], bk[:, 0:512], AF.Exp, [bb], [ex_b])
                        eng = "pool" if (B_POOL_EVERY and nblkc[0] % B_POOL_EVERY == B_POOL_EVERY - 1) else "dve"
                        nblkc[0] += 1
                        TT(S, eng, p_t[:], ex_t[:], r_t[:, TC - o:TC - o + 512], ALU.mult, [ex_b, r_b], [p_b])

                def pv(grp=grp, pts=pts, first=first, last=last, acc=acc, accb=accb, v_t=v_t, hb=hb,
                       epiA=epiA, epiB=epiB):
                    MMG(S, [(acc[0:VPAD, 0:512], v_t[:, kb, :], pts[j][0][:], first and j == 0, last and j == len(grp) - 1)
                            for j, kb in enumerate(grp)], [hb] + [p_b for _, p_b in pts], [accb])
                    if last:
                        pipe.at(2, epiA)
                        pipe.at(4, epiB)
                pipe.block(qk, pv)
            nq += 1
            if qi == 0 and ui + 1 < len(units):
                n0, nL, nh = units[ui + 1]
                load_head(C, S, A.slots[(ui + 1) % 2], D.qbt, D.kbt, D.vb, nh, n0, nL)
    pipe.flush()


def load_head_c(C, S, slot, D, h, s0, L, sl, JLf, cb):
    (q_t, k_t, v_t, hb, kxb, pat, patf) = slot
    for c in range(2):
        r0 = h * 64 + c * 32
        DMA(S, q_t[64 * c + 3:64 * c + 35, 0:L], D.qct[r0:r0 + 32, s0:s0 + L], [], [hb])
        DMA(S, k_t[64 * c + 3:64 * c + 35, 0:L], D.kct[r0:r0 + 32, s0:s0 + L], [], [hb])
    nb = L // 128
    for b0 in range(0, nb, 16):
        b1 = min(nb, b0 + 16)
        DMA(S, v_t[:, b0:b1, 0:64],
            D.vc[s0 + b0 * 128:s0 + b1 * 128, h * 64:(h + 1) * 64].rearrange("(b p) e -> p b e", p=128), [], [hb])
    pb = Buf()
    TS(S, "dve", patf[0:1, 0, :], JLf[0:1, :], sl, ALU.mult, [cb], [pb])
    CP(S, "dve", pat[0:1, 0, :], patf[0:1, 0, :], [pb], [pb])
    CP(S, "dve", patf[0:1, 1, :], pat[0:1, 0, :], [pb], [pb])
    TT(S, "dve", patf[0:1, 2, :], patf[0:1, 0, :], patf[0:1, 1, :], ALU.subtract, [pb], [pb])
    CP(S, "dve", pat[0:1, 1, :], patf[0:1, 2, :], [pb], [pb])
    CP(S, "dve", patf[0:1, 1, :], pat[0:1, 1, :], [pb], [pb])
    TT(S, "dve", patf[0:1, 0, :], patf[0:1, 2, :], patf[0:1, 1, :], ALU.subtract, [pb], [pb])
    CP(S, "dve", pat[0:1, 2, :], patf[0:1, 0, :], [pb], [pb])
    nt = L // 512
    for c in range(2):
        for r in range(3):
            DMA(S, q_t[64 * c + r:64 * c + r + 1, 0:L].rearrange("p (t j) -> p t j", j=512),
                pat[0:1, r, :].unsqueeze(1).to_broadcast([1, nt, 512]), [pb], [hb])
        S.emit("dve", lambda e, c=c: e.memset(k_t[64 * c:64 * c + 3, 0:512], 0.0), [], [kxb[0]])
        if L > 512:
            S.emit("dve", lambda e, c=c: e.memset(k_t[64 * c:64 * c + 3, 512:L], 1.0), [], list(kxb[1:L // 512]))


def phase3(C, S, sb, W, D, l, xsrc, ydst, seqs, NT):
    nc, P = C.nc, C.P
    build_consts(C, S, sb, lo=TC - 384, width=896)
    cb = C.constbuf
    maxL = max(L for _, L in seqs)
    NB = maxL // 128
    lambda_init = 0.8 - 0.6 * math.exp(-0.3 * l)
    A = Ctx()
    A.slots = []
    for i in range(2):
        q_t = sb(f"aq{i}", [128, maxL], BF16)
        k_t = sb(f"ak{i}", [128, maxL], BF16)
        v_t = sb(f"av{i}", [128, maxL // 128, VPAD], BF16)
        pat = sb(f"pat{i}", [1, 3, 512], BF16)
        patf = sb(f"patf{i}", [1, 3, 512], F32)
        hb = Buf()
        kxb = [Buf() for _ in range(maxL // 512)]
        S.emit("pool", lambda e, v_t=v_t: e.memset(v_t[:, :, 64:VPAD], 1.0), [], [hb])
        if KPAD > 35:
            S.emit("dve", lambda e, q_t=q_t: e.memset(q_t[:], 0.0), [], [hb] + kxb)
            S.emit("pool", lambda e, k_t=k_t: e.memset(k_t[:], 0.0), [], [hb] + kxb)
        A.slots.append((q_t, k_t, v_t, hb, kxb, pat, patf))
    A.pT = [(sb(f"pT{i}", [128, 512], BF16), Buf()) for i in range(10)]
    A.ex = [(sb(f"ex{i}", [128, 512], BF16), Buf()) for i in range(6)]
    A.npT = 0
    A.nex = 0
    tli = sb("tli", [128, NB + 1], I32)
    tri = sb("tri", [128, NB + 1], I32)
    jli = sb("jli", [128, 512], I32)
    TLf = sb("TLf", [128, NB + 1], F32)
    TRf = sb("TRf", [128, NB + 1], F32)
    JLf = sb("JLf", [128, 512], F32)
    S.emit("pool", lambda e: e.iota(tli[:], [[128, NB + 1]], base=0, channel_multiplier=-1), [], [cb])
    S.emit("pool", lambda e: e.iota(tri[:], [[128, NB + 1]], base=0, channel_multiplier=1), [], [cb])
    S.emit("pool", lambda e: e.iota(jli[:], [[1, 512]], base=0, channel_multiplier=0), [], [cb])
    for a, b in ((TLf, tli), (TRf, tri), (JLf, jli)):
        CP(S, "dve", a[:], b[:], [cb], [cb])
    lamv = sb("lamv", [64, 4, 32], F32)
    lams = sb("lams", [64, 8], F32)
    lamj = sb("lamj", [64, 32], F32)
    for i, src in enumerate(W.lam):
        DMA(S, lamv[:, i, :], src[l:l + 1, :].to_broadcast([64, 32]), [], [cb])
    for i in range(2):
        TT(S, "dve", lamj[:], lamv[:, 2 * i, :], lamv[:, 2 * i + 1, :], ALU.mult, [cb], [cb])
        S.emit("dve", lambda e, i=i: e.tensor_reduce(out=lams[:, i:i + 1], in_=lamj[:], axis=AX.X, op=ALU.add), [cb], [cb])
    ACT(S, lams[:, 2:4], lams[:, 0:2], AF.Exp, [cb], [cb])
    TT(S, "dve", lams[:, 4:5], lams[:, 3:4], lams[:, 2:3], ALU.subtract, [cb], [cb])
    TS(S, "dve", lams[:, 5:6], lams[:, 4:5], -lambda_init, ALU.add, [cb], [cb])
    subg = sb("subg", [64, 2], F32)
    DMA(S, subg[:, 0:1], W.subln_g[l:l + 1, :].rearrange("o e -> e o"), [], [cb], slow=True)
    TS(S, "dve", subg[:, 1:2], subg[:, 0:1], 1.0 - lambda_init, ALU.mult, [cb], [cb])
    neglam = lams[:, 5:6]

    rtab = [(sb(f"rtab{i}", [128, 896], BF16), Buf()) for i in range(2)]
    bl = [sb(f"bl{i}", [128, NB + 1], F32) for i in range(2)]
    br = [sb(f"br{i}", [128, NB + 1], F32) for i in range(2)]
    gate = [(sb(f"gate{i}", [64, 512], BF16), Buf()) for i in range(3)]
    ocs = [(sb(f"ocs{i}", [65, 2, 512], F32), Buf()) for i in range(2)]
    rc = [(sb(f"rc{i}", [128, 2, 512], F32), Buf()) for i in range(2)]
    a01 = [(sb(f"a01{i}", [64, 2, 512], F32), Buf()) for i in range(2)]
    at = [(sb(f"at{i}", [64, 512], F32), Buf()) for i in range(2)]
    sqb = [(sb(f"sqb{i}", [64, 512], F32), Buf()) for i in range(2)]
    rs = [(sb(f"rs{i}", [64, 512], F32), Buf()) for i in range(2)]
    ob = [(sb(f"ob{i}", [64, 512], BF16), Buf()) for i in range(2)]
    P.set_rot([0, 1, 2, 3, 6, 7])
    accs = [(P.banks[4], P.banks[5]), (P.banks[4], P.banks[5])]
    pipe = Pipe(2)
    units = [(s0, L, h) for (s0, L) in seqs for h in range(NHEAD)]
    load_head_c(C, S, A.slots[0], D, units[0][2], units[0][0], units[0][1], SLOPES[units[0][2]], JLf, cb)
    nq = 0
    for ui, (s0, L, h) in enumerate(units):
        q_t, k_t, v_t, hb, kxb, pat, patf = A.slots[ui % 2]
        sl = SLOPES[h]
        r_t, r_b = rtab[ui % 2]
        ACT(S, r_t[:], C.absd[:, 0:896], AF.Exp, [cb], [r_b], scale=-sl)
        bl_t, br_t = bl[ui % 2], br[ui % 2]
        TS(S, "dve", bl_t[:], TLf[:], -sl, ALU.mult, [cb], [r_b])
        TS(S, "dve", br_t[:], TRf[:], -sl, ALU.mult, [cb], [r_b])
        nblk = L // 128
        dmax = SKIP_T / sl
        if WARM_N:
            pe_warmup(S, P, C.identb[:], A.pT[0][0][:], WARM_N)
        for qi in range(L // 512):
            q0 = qi * 512
            def next_signs(qn=qi + 1, k_t=k_t, kxb=kxb):
                for c in range(2):
                    S.emit("dve", lambda e, c=c: e.memset(k_t[64 * c:64 * c + 3, (qn - 1) * 512:qn * 512], -1.0),
                           [], [kxb[qn - 1]])
                    S.emit("dve", lambda e, c=c: e.memset(k_t[64 * c:64 * c + 3, qn * 512:(qn + 1) * 512], 0.0),
                           [], [kxb[qn]])
            g_t, g_b = gate[nq % 3]
            DMA(S, g_t[:], D.gct[h * 64:(h + 1) * 64, s0 + q0:s0 + q0 + 512], [], [g_b])
            oc_t, oc_b = ocs[nq % 2]
            rc_t, rc_b = rc[nq % 2]
            a_t, a_b = a01[nq % 2]
            at_t, at_b = at[nq % 2]
            sq_t, sq_b = sqb[nq % 2]
            rs_t, rs_b = rs[nq % 2]
            o_t, o_b = ob[nq % 2]
            acc2 = accs[nq % 2]
            dstm = D.mixt[2 * BW + h * 64:2 * BW + (h + 1) * 64, s0 + q0:s0 + q0 + 512]
            lefts = [kb for kb in range(nblk) if kb * 128 - q0 <= -128 and (q0 - kb * 128 - 127) <= dmax]
            diags = [kb for kb in range(nblk) if 0 <= kb * 128 - q0 < 512]
            rights = [kb for kb in range(nblk) if kb * 128 - q0 >= 512 and (kb * 128 - q0 - 511) <= dmax]
            sbs = [("D", kb) for kb in diags] + [("R", kb) for kb in rights] + [("L", kb) for kb in lefts]
            sign_at = min(len(sbs) - 1, 9)

            def epi0(oc_t=oc_t, oc_b=oc_b, acc2=acc2):
                for c in range(2):
                    CP(S, "dve", oc_t[:, c, :], acc2[c][0][0:65, 0:512], [acc2[c][1]], [oc_b])

            def epiA(oc_t=oc_t, oc_b=oc_b, rc_t=rc_t, rc_b=rc_b):
                ACT(S, rc_t[64:65, :, :], oc_t[64:65, :, :], AF.Ln, [oc_b], [rc_b])
                ACT(S, rc_t[64:65, :, :], rc_t[64:65, :, :], AF.Exp, [rc_b], [rc_b], scale=-1.0)

            def epiB(oc_t=oc_t, oc_b=oc_b, rc_t=rc_t, rc_b=rc_b, a_t=a_t, a_b=a_b, at_t=at_t, at_b=at_b,
                     sq_t=sq_t, sq_b=sq_b):
                for c in range(2):
                    bk, bb = P.next()
                    MMG(S, [(bk[0:64, 0:512], C.ones_f[64:65, 0:64], rc_t[64:65, c, :], True, True)], [rc_b, cb], [bb])
                    TT(S, "dve", a_t[:, c, :], oc_t[0:64, c, :], bk[0:64, 0:512], ALU.mult, [oc_b, bb], [a_b])
                STT(S, at_t[:], a_t[:, 1, :], neglam, a_t[:, 0, :], ALU.mult, ALU.add, [a_b, cb], [at_b])
                TT(S, "dve", sq_t[:], at_t[:], at_t[:], ALU.mult, [at_b], [sq_b])

            def epiC(sq_t=sq_t, sq_b=sq_b, rs_t=rs_t, rs_b=rs_b):
                bk, bb = P.next()
                MMG(S, [(bk[0:64, 0:512], C.ones_f[0:64, 0:64], sq_t[:], True, True)], [sq_b, cb], [bb])
                ACT(S, rs_t[:], bk[0:64, 0:512], AF.Ln, [bb], [rs_b], scale=1.0 / 64, bias=EPS)
                ACT(S, rs_t[:], rs_t[:], AF.Exp, [rs_b], [rs_b], scale=-0.5)

            def epiD(at_t=at_t, at_b=at_b, rs_t=rs_t, rs_b=rs_b, o_t=o_t, o_b=o_b, g_t=g_t, g_b=g_b, dstm=dstm):
                TT(S, "dve", at_t[:], at_t[:], rs_t[:], ALU.mult, [at_b, rs_b], [at_b])
                STT(S, o_t[:], at_t[:], subg[:, 1:2], g_t[:], ALU.mult, ALU.mult, [at_b, g_b, cb], [o_b])
                DMA(S, dstm, o_t[:], [o_b], [])

            for si, (kind, kb) in enumerate(sbs):
                pts = []
                for _ in range(2):
                    pts.append(A.pT[A.npT % len(A.pT)])
                    A.npT += 1
                first, last = (si == 0), (si == len(sbs) - 1)

                def qk(kind=kind, kb=kb, pts=pts, q0=q0, k_t=k_t, q_t=q_t, hb=hb, r_t=r_t, r_b=r_b,
                       bl_t=bl_t, br_t=br_t, kxb=kxb):
                    bks = [P.next() for _ in range(2)]
                    MMG(S, [(bks[c][0][:, 0:512], k_t[64 * c:64 * c + KPAD, kb * 128:(kb + 1) * 128],
                             q_t[64 * c:64 * c + KPAD, q0:q0 + 512], True, True) for _ in range(QK_REP) for c in range(2)],
                        [hb, kxb[kb // 4]], [bks[0][1], bks[1][1]])
                    o = kb * 128 - q0
                    for c in range(2):
                        bk, bb = bks[c]
                        p_t, p_b = pts[c]
                        if kind == "L":
                            n = (-o) // 128
                            ACT(S, p_t[:], bk[:, 0:512], AF.Exp, [bb, r_b], [p_b], bias=bl_t[:, n:n + 1])
                        elif kind == "R":
                            n = o // 128
                            ACT(S, p_t[:], bk[:, 0:512], AF.Exp, [bb, r_b], [p_b], bias=br_t[:, n:n + 1])
                        else:
                            ex_t, ex_b = A.ex[A.nex % len(A.ex)]
                            A.nex += 1
                            ACT(S, ex_t[:], bk[:, 0:512], AF.Exp, [bb], [ex_b])
                            TT(S, "dve", p_t[:], ex_t[:], r_t[:, 384 - o:384 - o + 512], ALU.mult,
                               [ex_b, r_b], [p_b])

                def pv(kb=kb, pts=pts, first=first, last=last, acc2=acc2, v_t=v_t, hb=hb,
                       epi0=epi0, epiA=epiA, epiB=epiB, epiC=epiC, epiD=epiD):
                    MMG(S, [(acc2[c][0][0:VPAD, 0:512], v_t[:, kb, :], pts[c][0][:], first, last) for c in range(2)],
                        [hb, pts[0][1], pts[1][1]], [acc2[0][1], acc2[1][1]])
                    if last:
                        epi0()
                        pipe.at(2, epiA)
                        pipe.at(5, epiB)
                        pipe.at(8, epiC)
                        pipe.at(11, epiD)
                pipe.block(qk, pv)
                if si == sign_at and qi + 1 < L // 512:
                    next_signs()
                if WARM_Q and si == 2:
                    pe_warmup(S, P, C.identb[:], A.pT[0][0][:], WARM_Q)
            nq += 1
            if qi == 0 and ui + 1 < len(units):
                n0, nL, nh = units[ui + 1]
                load_head_c(C, S, A.slots[(ui + 1) % 2], D, nh, n0, nL, SLOPES[nh], JLf, cb)
    pipe.flush()


def phase4(C, S, sb, W, D, l, xsrc, ydst, seqs, NT):
    nc, P = C.nc, C.P
    P.set_rot(range(8))
    wo = sb("wo", [128, 12, D_MODEL], BF16)
    wbuf = Buf()
    wst = [(sb(f"wost{i}", [128, D_MODEL], F32), Buf()) for i in range(2)]
    for kc in range(12):
        st, stb = wst[kc % 2]
        DMA(S, st[:], W.w_out[l, kc * 128:(kc + 1) * 128, :], [], [stb])
        CP(S, ("dve", "act")[kc % 2], wo[:, kc, :], st[:], [stb], [wbuf])
    cw = sb("cw", [128, 3, 3], F32)
    for ch in range(3):
        DMA(S, cw[:, ch, :], W.conv_w[l, :, ch * 128:(ch + 1) * 128].rearrange("j p -> p j"), [], [wbuf], slow=True)
    mx = [(sb(f"mx{i}", [128, 12, 512], BF16), Buf()) for i in range(2)]
    zt = [(sb(f"zt{i}", [128, 3, 514], F32), Buf()) for i in range(2)]
    gd = [(sb(f"gd{i}", [128, 3, 512], F32), Buf()) for i in range(2)]
    cv = [(sb(f"cv{i}", [128, 3, 512], F32), Buf()) for i in range(2)]
    xr = [(sb(f"xr{i}", [128, D_MODEL], F32), Buf()) for i in range(3)]
    yo = [(sb(f"yo{i}", [128, D_MODEL], F32), Buf()) for i in range(3)]
    starts = {s0 for s0, _ in seqs}
    ends = {s0 + L for s0, L in seqs}
    ngroups = NT // 512
    nt = 0

    def loads(g):
        tok0 = g * 512
        m_t, m_b = mx[g % 2]
        DMA(S, m_t[:, 0:9, :], D.mixt[0:3 * BW, tok0:tok0 + 512].rearrange("(c p) t -> p c t", p=128), [], [m_b])
        z_t, z_b = zt[g % 2]
        lo = 0 if tok0 in starts else 1
        hi = 0 if (tok0 + 512) in ends else 1
        if lo == 0:
            S.emit("pool", lambda e: e.memset(z_t[:, :, 0:1], 0.0), [], [z_b])
        if hi == 0:
            S.emit("pool", lambda e: e.memset(z_t[:, :, 513:514], 0.0), [], [z_b])
        DMA(S, z_t[:, :, 1 - lo:513 + hi],
            D.zt[:, tok0 - lo:tok0 + 512 + hi].rearrange("(c p) t -> p c t", p=128), [], [z_b])
        g_t, g_b = gd[g % 2]
        DMA(S, g_t[:], D.gdt[:, tok0:tok0 + 512].rearrange("(c p) t -> p c t", p=128), [], [g_b])

    loads(0)
    for g in range(ngroups):
        tok0 = g * 512
        if g + 1 < ngroups:
            loads(g + 1)
        m_t, m_b = mx[g % 2]
        z_t, z_b = zt[g % 2]
        g_t, g_b = gd[g % 2]
        c_t, c_b = cv[g % 2]
        for ch in range(3):
            TS(S, "dve", c_t[:, ch, :], z_t[:, ch, 0:512], cw[:, ch, 0:1], ALU.mult, [z_b, wbuf], [c_b])
            STT(S, c_t[:, ch, :], z_t[:, ch, 1:513], cw[:, ch, 1:2], c_t[:, ch, :], ALU.mult, ALU.add, [z_b, c_b, wbuf], [c_b])
            STT(S, c_t[:, ch, :], z_t[:, ch, 2:514], cw[:, ch, 2:3], c_t[:, ch, :], ALU.mult, ALU.add, [z_b, c_b, wbuf], [c_b])
            TT(S, "pool", m_t[:, 9 + ch, :], c_t[:, ch, :], g_t[:, ch, :], ALU.mult, [c_b, g_b], [m_b])
        for j in range(4):
            t0 = tok0 + j * 128
            x_t, x_b = xr[nt % 3]
            DMA(S, x_t[:], xsrc[t0:t0 + 128, :], [], [x_b])
            y_t, y_b = yo[nt % 3]
            for half in range(2):
                bk, bb = P.next()
                MMG(S, [(bk[:, 0:512], m_t[:, kc, j * 128:(j + 1) * 128], wo[:, kc, half * 512:(half + 1) * 512],
                         kc == 0, kc == 11) for kc in range(12)], [m_b, wbuf], [bb])
                TT(S, "dve", y_t[:, half * 512:(half + 1) * 512], bk[:, 0:512], x_t[:, half * 512:(half + 1) * 512],
                   ALU.add, [bb, x_b], [y_b])
            DMA(S, ydst[t0:t0 + 128, :], y_t[:], [y_b], [])
            nt += 1


_NC_CACHE = {}
SEQS = (8192, 4096, 4096)
WNAMES = ("norm_g", "w_in", "sgu_g", "w_s", "b_s", "qn_b", "kn_b", "qn_c", "kn_c",
          "lam_q1", "lam_k1", "lam_q2", "lam_k2", "subln_g", "conv_w", "w_out")


def kernel(x_prompt, x_sample, **w):
    x_prompt = np.asarray(x_prompt, dtype=np.float32)
    x_sample = np.asarray(x_sample, dtype=np.float32)
    if "nc" not in _NC_CACHE:
        _NC_CACHE["nc"] = build_program(SEQS)
    nc = _NC_CACHE["nc"]
    wmap = {k: np.ascontiguousarray(np.asarray(w[k], dtype=np.float32)) for k in WNAMES}
    in_maps = []
    for c in range(N_CORES):
        xc = np.concatenate([x_prompt[c], x_sample[2 * c], x_sample[2 * c + 1]], axis=0)
        m = {"x": np.ascontiguousarray(xc)}
        m.update(wmap)
        in_maps.append(m)
    res = run_bass_kernel_spmd(nc, in_maps, core_ids=list(range(N_CORES)))
    yp = np.empty_like(x_prompt)
    ys = np.empty_like(x_sample)
    for c in range(N_CORES):
        y = res.results[c]["y"]
        yp[c] = y[0:8192]
        ys[2 * c] = y[8192:12288]
        ys[2 * c + 1] = y[12288:16384]
    return (yp, ys)
```
